# Optimizing a Trainium2 kernel written in Bass

```python
import jax, jax.numpy as jnp
from jax import lax
import numpy as np

D_MODEL = 2048
BATCH = 2
SEQ = 8192
DEPTH = 4

CHUNK = 64
RET_HEADS = 8
RET_DK = 128
RET_DV = 256
RET_QK = RET_HEADS * RET_DK
RET_V = RET_HEADS * RET_DV
SGU_GROUPS = 8
SGU_LEN = 128
SGU_WIDTH = D_MODEL
SGU_CH = SGU_WIDTH // SGU_GROUPS
D_FF = -(-(8 * D_MODEL) // (3 * 256)) * 256
ROPE_BASE = 10000.0
EPS = 1e-6
SPLITS = [RET_QK, 2 * RET_QK, 2 * RET_QK + RET_V, 2 * RET_QK + 2 * RET_V,
          2 * RET_QK + 2 * RET_V + SGU_WIDTH, 2 * RET_QK + 2 * RET_V + 2 * SGU_WIDTH,
          2 * RET_QK + 2 * RET_V + 2 * SGU_WIDTH + D_MODEL]
IN_COLS = 2 * RET_QK + 2 * RET_V + 2 * SGU_WIDTH + 2 * D_MODEL

kernel_name = "hybrid_retention_sgu_griffin_merge"


def rms_norm(x, w):
    xf = x.astype(jnp.float32)
    y = xf * lax.rsqrt(jnp.mean(xf * xf, axis=-1, keepdims=True) + EPS)
    return (y * w.astype(jnp.float32)).astype(x.dtype)


def layer_norm(x, w, b):
    xf = x.astype(jnp.float32)
    mu = jnp.mean(xf, axis=-1, keepdims=True)
    var = jnp.mean(jnp.square(xf - mu), axis=-1, keepdims=True)
    y = (xf - mu) * lax.rsqrt(var + EPS) * w.astype(jnp.float32) + b.astype(jnp.float32)
    return y.astype(x.dtype)


def head_group_norm(o, w):
    B, S, H, dv = o.shape
    of = o.astype(jnp.float32)
    mu = jnp.mean(of, axis=-1, keepdims=True)
    var = jnp.mean(jnp.square(of - mu), axis=-1, keepdims=True)
    y = ((of - mu) * lax.rsqrt(var + EPS)).reshape(B, S, H * dv) * w.astype(jnp.float32)
    return y.astype(o.dtype)


def rotary(x, pos):
    half = x.shape[-1] // 2
    inv = ROPE_BASE ** (-jnp.arange(half, dtype=jnp.float32) / half)
    ang = pos.astype(jnp.float32)[:, None] * inv[None, :]
    cos = jnp.cos(ang)[None, :, None, :]
    sin = jnp.sin(ang)[None, :, None, :]
    xf = x.astype(jnp.float32)
    x1, x2 = xf[..., :half], xf[..., half:]
    return jnp.concatenate([x1 * cos - x2 * sin, x2 * cos + x1 * sin], axis=-1).astype(x.dtype)


def retention(q, k, v):
    B, S, H, dk = q.shape
    dv = v.shape[-1]
    nc = S // CHUNK
    dt = q.dtype
    log_g = jnp.log1p(-(2.0 ** (-5.0 - jnp.arange(H, dtype=jnp.float32))))
    idx = jnp.arange(CHUNK, dtype=jnp.float32)
    intra_decay = jnp.exp(log_g[:, None, None] * jnp.abs(idx[:, None] - idx[None, :]))
    q_decay = jnp.exp(log_g[None, :] * (idx[:, None] + 1.0))
    k_decay = jnp.exp(log_g[None, :] * (CHUNK - 1.0 - idx[:, None]))
    chunk_decay = jnp.exp(log_g * CHUNK)

    q = q.reshape(B, nc, CHUNK, H, dk) * (dk ** -0.5)
    k = k.reshape(B, nc, CHUNK, H, dk)
    v = v.reshape(B, nc, CHUNK, H, dv)

    scores = jnp.einsum('bnihd,bnjhd->bnhij', q, k) * intra_decay.astype(dt)
    o_intra = jnp.einsum('bnhij,bnjhe->bnihe', scores, v)

    qs = q * q_decay.astype(dt)[:, :, None]
    ks = k * k_decay.astype(dt)[:, :, None]
    cdec = chunk_decay.astype(dt)[None, :, None, None]

    def step(state, xs):
        qc, kc, vc = xs
        o = jnp.einsum('bihd,bhde->bihe', qc, state)
        state = state * cdec + jnp.einsum('bjhd,bjhe->bhde', kc, vc)
        return state, o

    s0 = jnp.zeros((B, H, dk, dv), dtype=v.dtype)
    _, o_inter = lax.scan(step, s0, (jnp.swapaxes(qs, 0, 1), jnp.swapaxes(ks, 0, 1),
                                     jnp.swapaxes(v, 0, 1)))
    o = o_intra + jnp.swapaxes(o_inter, 0, 1)
    return o.reshape(B, S, H, dv)


def spatial_gating(u, v, ln_w, ln_b, w_s, b_s):
    B, S, W = v.shape
    ng = S // SGU_LEN
    vn = layer_norm(v, ln_w, ln_b).reshape(B, ng, SGU_LEN, SGU_GROUPS, SGU_CH)
    pos = jnp.arange(SGU_LEN)
    mask = (pos[None, :] // CHUNK) <= (pos[:, None] // CHUNK)
    w = jnp.where(mask[None], w_s, jnp.zeros_like(w_s))
    mixed = jnp.einsum('gij,bnjgc->bnigc', w, vn) + b_s.T[None, None, :, :, None]
    return u * mixed.reshape(B, S, W)


def hybrid_layer(x, pos, norm_mix_w, w_in, ret_gn_w, ret_proj, sgu_ln_w, sgu_ln_b,
                 sgu_w_s, sgu_b_s, sgu_proj, w_out, norm_ffn_w, w_ffn_in, w_ffn_out):
    B, S, _ = x.shape
    h = rms_norm(x, norm_mix_w)
    z = h @ w_in
    q, k, v, g, su, sv, gate_a, gate_b = jnp.split(z, SPLITS, axis=-1)

    q = rotary(q.reshape(B, S, RET_HEADS, RET_DK), pos)
    k = rotary(k.reshape(B, S, RET_HEADS, RET_DK), pos)
    v = v.reshape(B, S, RET_HEADS, RET_DV)
    ret = head_group_norm(retention(q, k, v), ret_gn_w)
    branch_a = (jax.nn.silu(g) * ret) @ ret_proj

    zu = jax.nn.gelu(su, approximate=False)
    zv = jax.nn.gelu(sv, approximate=False)
    branch_b = spatial_gating(zu, zv, sgu_ln_w, sgu_ln_b, sgu_w_s, sgu_b_s) @ sgu_proj

    merged = jax.nn.sigmoid(gate_a) * branch_a + jax.nn.sigmoid(gate_b) * branch_b
    x = x + merged @ w_out

    h = rms_norm(x, norm_ffn_w)
    a, c = jnp.split(h @ w_ffn_in, 2, axis=-1)
    x = x + (jax.nn.silu(a) * c) @ w_ffn_out
    return x


def setup_inputs(seed: int = 0) -> dict:
    key = jax.random.key(seed)
    ks = jax.random.split(key, 16)
    f32 = jnp.float32
    nrm = lambda k, shape, scale: jax.random.normal(k, shape, f32) * scale
    return {
        "x": jax.random.normal(ks[0], (BATCH, SEQ, D_MODEL), f32),
        "norm_mix_w": 1.0 + nrm(ks[1], (DEPTH, D_MODEL), 0.02),
        "w_in": nrm(ks[2], (DEPTH, D_MODEL, IN_COLS), D_MODEL ** -0.5),
        "ret_gn_w": 1.0 + nrm(ks[3], (DEPTH, RET_V), 0.02),
        "ret_proj": nrm(ks[4], (DEPTH, RET_V, D_MODEL), RET_V ** -0.5),
        "sgu_ln_w": 1.0 + nrm(ks[5], (DEPTH, SGU_WIDTH), 0.02),
        "sgu_ln_b": nrm(ks[6], (DEPTH, SGU_WIDTH), 0.02),
        "sgu_w_s": nrm(ks[7], (DEPTH, SGU_GROUPS, SGU_LEN, SGU_LEN), SGU_LEN ** -0.5),
        "sgu_b_s": 1.0 + nrm(ks[8], (DEPTH, SGU_GROUPS, SGU_LEN), 0.02),
        "sgu_proj": nrm(ks[9], (DEPTH, SGU_WIDTH, D_MODEL), SGU_WIDTH ** -0.5),
        "w_out": nrm(ks[10], (DEPTH, D_MODEL, D_MODEL), D_MODEL ** -0.5),
        "norm_ffn_w": 1.0 + nrm(ks[11], (DEPTH, D_MODEL), 0.02),
        "w_ffn_in": nrm(ks[12], (DEPTH, D_MODEL, 2 * D_FF), D_MODEL ** -0.5),
        "w_ffn_out": nrm(ks[13], (DEPTH, D_FF, D_MODEL), D_FF ** -0.5),
        "final_norm_w": 1.0 + nrm(ks[14], (D_MODEL,), 0.02),
    }


def reference(x, norm_mix_w, w_in, ret_gn_w, ret_proj, sgu_ln_w, sgu_ln_b, sgu_w_s, sgu_b_s,
              sgu_proj, w_out, norm_ffn_w, w_ffn_in, w_ffn_out, final_norm_w):
    pos = jnp.arange(x.shape[1], dtype=jnp.int32)
    for l in range(DEPTH):
        x = hybrid_layer(x, pos, norm_mix_w[l], w_in[l], ret_gn_w[l], ret_proj[l], sgu_ln_w[l],
                         sgu_ln_b[l], sgu_w_s[l], sgu_b_s[l], sgu_proj[l], w_out[l],
                         norm_ffn_w[l], w_ffn_in[l], w_ffn_out[l])
    return rms_norm(x, final_norm_w)
```

```python
import math
from contextlib import ExitStack

import numpy as np
import concourse.bass as bass
import concourse.mybir as mybir
from concourse.bass_utils import run_bass_kernel_spmd

F32 = mybir.dt.float32
BF16 = mybir.dt.bfloat16
AF = mybir.ActivationFunctionType
ALU = mybir.AluOpType
EPS = 1e-6
ENGS = ("pe", "act", "dve", "pool", "sp")


class Fw:
    def __init__(self, nc, stack, n_dma_slots=8):
        self.nc = nc
        self.q = {e: [] for e in ENGS}
        self.sem = {e: stack.enter_context(nc.semaphore("s_" + e)) for e in ENGS}
        self.cnt = {e: 0 for e in ENGS}
        self.waited = {e: {} for e in ENGS}
        self.last_w = {}
        self.readers = {}
        self.slots = {}
        self.slot_rr = {}
        for qn in ("sp", "pool"):
            self.slots[qn] = [[stack.enter_context(nc.semaphore("d_%s%d" % (qn, i))), 0]
                              for i in range(n_dma_slots)]
            self.slot_rr[qn] = 0
        self.cc_sem = stack.enter_context(nc.semaphore("cc_sem"))
        self.coll_cnt = 0

    def _wait(self, eng, ev):
        if ev is None:
            return
        sem, val, src = ev
        if src == eng and eng == "pe":
            return
        key = id(sem)
        if self.waited[eng].get(key, 0) >= val:
            return
        self.waited[eng][key] = val
        self.q[eng].append(lambda E, sem=sem, val=val: E.wait_ge(sem, val))

    def _deps(self, eng, reads, writes):
        for r in reads:
            self._wait(eng, self.last_w.get(r))
        for w in writes:
            self._wait(eng, self.last_w.get(w))
            for ev in self.readers.get(w, ()):
                self._wait(eng, ev)

    def _record(self, ev, reads, writes):
        for w in writes:
            self.last_w[w] = ev
            self.readers[w] = []
        for r in reads:
            lst = self.readers.setdefault(r, [])
            lst[:] = [e for e in lst if e[0] is not ev[0]]
            lst.append(ev)

    def op(self, eng, fn, reads=(), writes=()):
        self._deps(eng, reads, writes)
        self.cnt[eng] += 1
        sem = self.sem[eng]
        ev = (sem, self.cnt[eng], eng)
        self.q[eng].append(lambda E, fn=fn, sem=sem: fn(E).then_inc(sem, 1))
        self._record(ev, reads, writes)
        return ev

    def pe_group(self, fns, reads=(), writes=()):
        self._deps("pe", reads, writes)
        for fn in fns[:-1]:
            self.q["pe"].append(fn)
        self.cnt["pe"] += 1
        sem = self.sem["pe"]
        ev = (sem, self.cnt["pe"], "pe")
        last = fns[-1]
        self.q["pe"].append(lambda E, fn=last, sem=sem: fn(E).then_inc(sem, 1))
        self._record(ev, reads, writes)
        return ev

    def coll(self, fn, reads=(), writes=()):
        self._deps("pool", reads, writes)
        self.coll_cnt += 1
        n = self.coll_cnt
        sem = self.cc_sem
        self.q["pool"].append(lambda E, fn=fn, sem=sem: fn(E).then_inc(sem, 1))
        self.q["pool"].append(lambda E, sem=sem, n=n: E.wait_ge(sem, n))
        self.cnt["pool"] += 1
        psem = self.sem["pool"]
        ev = (psem, self.cnt["pool"], "pool")
        self.q["pool"].append(lambda E, psem=psem: E.sem_inc(psem, 1))
        self._record(ev, reads, writes)
        return ev

    def dma(self, qn, fn, reads=(), writes=()):
        self._deps(qn, reads, writes)
        i = self.slot_rr[qn]
        self.slot_rr[qn] = (i + 1) % len(self.slots[qn])
        slot = self.slots[qn][i]
        sem = slot[0]
        if slot[1] > 0:
            self._wait(qn, (sem, 16 * slot[1], None))
        slot[1] += 1
        ev = (sem, 16 * slot[1], None)
        self.q[qn].append(lambda E, fn=fn, sem=sem: fn(E).then_inc(sem, 16))
        self._record(ev, reads, writes)
        return ev

    def _sp_wait_all(self):
        for q2 in self.slots:
            for sem, c in self.slots[q2]:
                if c > 0:
                    self._wait("sp", (sem, 16 * c, None))
        for e in ENGS:
            if self.cnt[e] > 0 and e != "sp":
                self._wait("sp", (self.sem[e], self.cnt[e], e))

    def barrier(self):
        self._sp_wait_all()
        self.cnt["sp"] += 1
        sem = self.sem["sp"]
        ev = (sem, self.cnt["sp"], "sp")
        self.q["sp"].append(lambda E, sem=sem: E.sem_inc(sem, 1))
        for e in ENGS:
            if e != "sp":
                self._wait(e, ev)

    def finish(self):
        self._sp_wait_all()

    def run(self, block):
        q = self.q

        @block.tensor
        def _(E):
            for f in q["pe"]:
                f(E)

        @block.scalar
        def _(E):
            for f in q["act"]:
                f(E)

        @block.vector
        def _(E):
            for f in q["dve"]:
                f(E)

        @block.gpsimd
        def _(E):
            for f in q["pool"]:
                f(E)

        @block.sync
        def _(E):
            for f in q["sp"]:
                f(E)


class Cfg:
    def __init__(self, D=2048, H=8, DFF=5632, T=2048, NL=4, NCORES=8, SEQ=8192, BATCH=2):
        self.D, self.H, self.DFF, self.T, self.NL = D, H, DFF, T, NL
        self.NCORES, self.SEQ, self.BATCH = NCORES, SEQ, BATCH
        self.G = SEQ // T
        assert self.G * BATCH == NCORES
        self.KC = D // 128
        self.QK = H * 128
        self.RV = H * 256
        self.RC = self.RV // 128
        self.SG = D // 256
        self.NT = T // 128
        self.NB = T // 512
        self.FC = DFF // 128
        self.INC = 2 * self.QK + 2 * self.RV + 4 * D
        self.oq, self.ok = 0, self.QK
        self.ov = 2 * self.QK
        self.og = self.ov + self.RV
        self.osu = self.og + self.RV
        self.osv = self.osu + D
        self.oga = self.osv + D
        self.ogb = self.oga + D


def _tables(cfg, core):
    H, T, NT, G = cfg.H, cfg.T, cfg.NT, cfg.G
    rank = core % G
    pos0 = rank * T
    half = 64
    inv = (10000.0 ** (-np.arange(half, dtype=np.float32) / np.float32(half))).astype(np.float32)
    pos = (pos0 + np.arange(T)).astype(np.float32)
    ang = (pos[None, :] * inv[:, None]).astype(np.float32)
    cos = np.cos(ang).astype(np.float32)
    sin = np.sin(ang).astype(np.float32)
    tb_cos = np.concatenate([cos, cos], 0)
    tb_sin = np.concatenate([-sin, sin], 0)
    logg = np.log1p(-(2.0 ** (-5.0 - np.arange(H, dtype=np.float64))))
    p = np.arange(128, dtype=np.float64)
    n = np.arange(NT, dtype=np.float64)
    loc = n[None, :, None] * 128 + p[:, None, None]
    kdec = np.exp(-logg[None, None, :] * loc)
    odec = np.exp(logg[None, None, :] * loc) * (128.0 ** -0.5)
    coef = np.zeros((128, G, H), np.float64)
    for s in range(G):
        if s < rank:
            coef[:, s, :] = np.exp(logg * T * (rank - s - 1))[None, :]
    oneh = np.zeros((128, G, H), np.float64)
    oneh[:, rank, :] = np.exp(logg * T)[None, :]
    gT = np.broadcast_to(np.exp(logg * T)[None, :], (128, H))
    j = np.arange(128)[:, None]
    i = np.arange(128)[None, :]
    cj, ci = j // 64, i // 64
    mask = np.zeros((128, H, 128), np.float64)
    for h in range(H):
        m = np.where(i >= j, 1.0, np.exp(logg[h] * 2.0 * (j - i)))
        m = np.where(cj > ci, 0.0, m)
        m = np.where(cj < ci, 1.0, m)
        mask[:, h, :] = m
    cmask = (cj <= ci).astype(np.float32)
    ident = np.eye(128, dtype=np.float32)
    pm = np.zeros((128, 128), np.float32)
    for d in range(128):
        pm[(d + 64) % 128, d] = 1.0
    f = lambda a: np.ascontiguousarray(np.asarray(a, dtype=np.float32))
    return {
        "tb_cos": f(tb_cos), "tb_sin": f(tb_sin),
        "tb_kdec": f(kdec.reshape(128, NT * H)), "tb_odec": f(odec.reshape(128, NT * H)),
        "tb_coef": f(coef.reshape(128, G * H)), "tb_oneh": f(oneh.reshape(128, G * H)), "tb_gT": f(gT),
        "tb_mask": f(mask.reshape(128, H * 128)), "tb_cmask": f(cmask),
        "tb_ident": f(ident), "tb_pm": f(pm), "tb_ones": np.ones((128, 128), np.float32),
    }


def _build(cfg):
    c = cfg
    D, H, T, NL, KC, NT, NB, FC, RC, SG, G, DFF = c.D, c.H, c.T, c.NL, c.KC, c.NT, c.NB, c.FC, c.RC, c.SG, c.G, c.DFF
    nc = bass.Bass("TRN2", target_bir_lowering=False)

    def din(name, shape):
        return nc.dram_tensor(name, list(shape), F32, kind="ExternalInput").ap()

    def dscr(name, shape, dt):
        return nc.dram_tensor(name, list(shape), dt, kind="Internal").ap()

    x_in = din("x", [T, D])
    w_in = din("w_in", [NL, D, c.INC])
    ret_proj = din("ret_proj", [NL, c.RV, D])
    sgu_proj = din("sgu_proj", [NL, D, D])
    w_out = din("w_out", [NL, D, D])
    w_ffn_in = din("w_ffn_in", [NL, D, 2 * DFF])
    w_ffn_out = din("w_ffn_out", [NL, DFF, D])
    nw1_d = din("norm_mix_w", [NL, 128, KC])
    nw2_d = din("norm_ffn_w", [NL, 128, KC])
    nwf_d = din("final_norm_w", [128, KC])
    gnw_d = din("ret_gn_w", [NL, 128, RC])
    lnw_d = din("sgu_ln_w", [NL, D])
    lnb_d = din("sgu_ln_b", [NL, D])
    ws_d = din("sgu_w_s", [NL, SG, 128, 128])
    bs_d = din("sgu_b_s", [NL, SG * 128])
    tbs = {}
    for nm, shp in (("tb_cos", [128, T]), ("tb_sin", [128, T]), ("tb_kdec", [128, NT * H]),
                    ("tb_odec", [128, NT * H]), ("tb_coef", [128, G * H]), ("tb_oneh", [128, G * H]),
                    ("tb_gT", [128, H]), ("tb_mask", [128, H * 128]), ("tb_cmask", [128, 128]),
                    ("tb_ident", [128, 128]), ("tb_pm", [128, 128]), ("tb_ones", [128, 128])):
        tbs[nm] = din(nm, shp)
    y_out = nc.dram_tensor("y", [T, D], F32, kind="ExternalOutput").ap()

    XT = dscr("XT", [KC, 128, T], F32)
    ZQ = dscr("ZQ", [H, 128, T], BF16)
    ZK = dscr("ZK", [H, 128, T], BF16)
    ZVv = dscr("ZVv", [128, NT, c.RV], BF16)
    ZG = dscr("ZG", [128, NT, c.RV], BF16)
    ZU = dscr("ZU", [KC, 128, T], BF16)
    ZS = dscr("ZS", [128, NT, D], BF16)
    GA = dscr("GA", [KC, 128, T], BF16)
    GB = dscr("GB", [KC, 128, T], BF16)
    GT = dscr("GT", [RC, 128, T], BF16)
    ST = dscr("ST", [KC, 128, T], BF16)
    AT = dscr("AT", [FC, 128, T], BF16)
    EXI = dscr("EXI", [G * H * 128, 256], F32)
    EXO = dscr("EXO", [G * H * 128, 256], F32)

    WBC = 256
    ARENA = 65536
    with ExitStack() as st:
        fw = Fw(nc, st)
        sb = lambda name, shape, dt: st.enter_context(nc.sbuf_tensor(name, list(shape), dt))
        arena = sb("arena", [128, ARENA], BF16)
        KH = FC // 2
        assert FC % 2 == 0
        wb = [sb("wb%d" % i, [128, max(KC, KH) * WBC], BF16) for i in range(3)]
        WCTX = dict(slots=[w[:] for w in wb], wbc=WBC, key="wb")
        kdec_t = sb("kdec_t", [128, NT * H], F32)
        odec_t = sb("odec_t", [128, NT * H], F32)
        coef_t = sb("coef_t", [128, G * H], F32)
        oneh_t = sb("oneh_t", [128, G * H], F32)
        gT_t = sb("gT_t", [128, H], F32)
        mask_t = sb("mask_t", [128, H * 128], F32)
        cmask_t = sb("cmask_t", [128, 128], F32)
        ident_b = sb("ident_b", [128, 128], BF16)
        ident_f = sb("ident_f", [128, 128], F32)
        pm_b = sb("pm_b", [128, 128], BF16)
        ones_b = sb("ones_b", [128, 128], BF16)
        nw1_t = sb("nw1_t", [128, NL * KC], F32)
        nw2_t = sb("nw2_t", [128, NL * KC], F32)
        nwf_t = sb("nwf_t", [128, KC], F32)
        gnw_t = sb("gnw_t", [128, NL * RC], F32)
        wsb_t = sb("wsb_t", [128, SG * 128], BF16)
        wT_t = sb("wT_t", [128, SG * 128], BF16)
        bsf_t = sb("bsf_t", [1, SG * 128], F32)
        bsh_t = sb("bsh_t", [1, SG * 128], BF16)
        bsr_t = sb("bsr_t", [1, SG * 128], F32)
        bsl_t = sb("bsl_t", [1, SG * 128], BF16)
        NSB = 4
        stg_b = [sb("stgb%d" % i, [128, 512], BF16) for i in range(NSB)]
        stg_f = [sb("stgf%d" % i, [128, 512], F32) for i in range(3)]
        ld_b = [sb("ldb%d" % i, [128, 512], BF16) for i in range(2)]
        ld_f = [sb("ldf%d" % i, [128, 512], F32) for i in range(2)]
        rstd_t = sb("rstd_t", [128, 512], F32)
        small = sb("small", [128, 256], F32)
        s0_t = sb("s0_t", [128, 256], F32)
        psG = [st.enter_context(nc.psum_tensor("psG%d" % i, [128, 512], F32)) for i in range(4)]
        psT = [st.enter_context(nc.psum_tensor("psT%d" % i, [128, 1024], BF16)) for i in range(2)]
        psX = st.enter_context(nc.psum_tensor("psX", [128, 512], F32))
        psS = st.enter_context(nc.psum_tensor("psS", [128, 512], F32))
        block = st.enter_context(nc.Block())

        rr = {"g": 0, "g3": 0, "sb": 0, "sf": 0, "lb": 0, "lf": 0, "t": 0}

        def nxt(kind, n):
            i = rr[kind]
            rr[kind] = (i + 1) % n
            return i

        def av(off, n):
            return arena[:, off:off + n]

        def avf(off, n):
            return arena[:, off:off + n].bitcast(F32)

        def mm(ps, lhsT, rhs, start, stop):
            return lambda E: E.matmul(ps, lhsT=lhsT, rhs=rhs, start=start, stop=stop)

        def trp(out, in_, ident):
            return lambda E: E.transpose(out=out, in_=in_, identity=ident)

        def actf(out, in_, func, scale=1.0, bias=0.0):
            return lambda E: E.activation(out=out, in_=in_, func=func, bias=bias, scale=scale)

        def tt(out, in0, in1, op):
            return lambda E: E.tensor_tensor(out=out, in0=in0, in1=in1, op=op)

        def stt(out, in0, scalar, in1, op0, op1):
            return lambda E: E.scalar_tensor_tensor(out=out, in0=in0, scalar=scalar, in1=in1, op0=op0, op1=op1)

        def tsc(out, in0, s1, s2, op0, op1):
            return lambda E: E.tensor_scalar(out=out, in0=in0, scalar1=s1, scalar2=s2, op0=op0, op1=op1)

        def ld(qn, out, in_, reads=(), writes=()):
            return fw.dma(qn, lambda E: E.dma_start(out=out, in_=in_), reads=reads, writes=writes)

        ld("pool", ident_b[:], tbs["tb_ident"], writes=["identb"])
        ld("pool", pm_b[:], tbs["tb_pm"], writes=["pmb"])
        ld("pool", ones_b[:], tbs["tb_ones"], writes=["onesb"])
        ld("sp", ident_f[:], tbs["tb_ident"], writes=["identf"])
        ld("sp", kdec_t[:], tbs["tb_kdec"], writes=["kdec"])
        ld("sp", odec_t[:], tbs["tb_odec"], writes=["odec"])
        ld("sp", coef_t[:], tbs["tb_coef"], writes=["coef"])
        ld("sp", oneh_t[:], tbs["tb_oneh"], writes=["oneh"])
        ld("sp", gT_t[:], tbs["tb_gT"], writes=["gT"])
        ld("sp", mask_t[:], tbs["tb_mask"], writes=["mask"])
        ld("sp", cmask_t[:], tbs["tb_cmask"], writes=["cmask"])
        ld("sp", nw1_t[:].rearrange("p (l k) -> p l k", l=NL), nw1_d.rearrange("l p k -> p l k"), writes=["nw1"])
        ld("sp", nw2_t[:].rearrange("p (l k) -> p l k", l=NL), nw2_d.rearrange("l p k -> p l k"), writes=["nw2"])
        ld("sp", gnw_t[:].rearrange("p (l k) -> p l k", l=NL), gnw_d.rearrange("l p k -> p l k"), writes=["gnw"])
        ld("sp", nwf_t[:], nwf_d, writes=["nwf"])
        CONST_KEYS = ["cos", "sin", "identb", "pmb", "onesb", "identf", "kdec", "odec", "coef", "oneh", "gT",
                      "mask", "cmask", "nw1", "nw2", "gnw", "nwf"]

        def ingest():
            XS = 0
            for n in range(NT):
                s = n % 2
                xt = avf(XS + s * 2 * D, 2 * D)
                ld("sp", xt, x_in[n * 128:(n + 1) * 128, :], writes=[("xs", s)])
                for k0 in range(0, KC, 4):
                    gi = nxt("g", 4)
                    ps = psG[gi]
                    fw.pe_group([trp(ps[:, j * 128:(j + 1) * 128], xt[:, (k0 + j) * 128:(k0 + j + 1) * 128], ident_f[:])
                                 for j in range(4)], reads=[("xs", s), "identf"], writes=[("psG", gi)])
                    si = nxt("sf", 3)
                    fw.op("act", actf(stg_f[si][:], ps[:], AF.Copy), reads=[("psG", gi)], writes=[("stgf", si)])
                    ld("pool", XT[k0:k0 + 4, :, n * 128:(n + 1) * 128].rearrange("k p t -> p k t"),
                       stg_f[si][:].rearrange("p (k t) -> p k t", k=4), reads=[("stgf", si)], writes=["XT"])

        BUFA = 0
        BUFB = KC * T

        def hT(k, t0, n):
            return arena[:, BUFA + k * T + t0: BUFA + k * T + t0 + n]

        def mT(k, t0, n):
            return arena[:, BUFB + k * T + t0: BUFB + k * T + t0 + n]

        def norm_pass(nw_ap_fn, final=False):
            XB = BUFB
            assert XB + 2 * KC * 1024 <= ARENA
            for tb in range(NB):
                xbk = ("xb", tb % 2)
                xb = avf(XB + (tb % 2) * KC * 1024, KC * 1024).rearrange("p (k t) -> p k t", k=KC)
                ld("sp", xb, XT[:, :, tb * 512:(tb + 1) * 512].rearrange("k p t -> p k t"),
                   reads=["XT"], writes=[xbk])
                fns = []
                for k in range(KC):
                    si = nxt("sb", NSB)
                    fw.op("act", actf(stg_b[si][:], xb[:, k, :], AF.Square), reads=[xbk], writes=[("stgb", si)])
                    fw.pe_group([mm(psX[:], ones_b[:], stg_b[si][:], k == 0, k == KC - 1)],
                                reads=[("stgb", si), "onesb"], writes=["psX"])
                fw.op("act", actf(rstd_t[:], psX[:], AF.Sqrt, scale=1.0 / D, bias=EPS), reads=["psX"], writes=["rstd"])
                fw.op("dve", lambda E: E.reciprocal(out=rstd_t[:], in_=rstd_t[:]), reads=["rstd"], writes=["rstd"])
                if not final:
                    for k in range(KC):
                        fw.op("dve", stt(hT(k, tb * 512, 512), xb[:, k, :], nw_ap_fn(k), rstd_t[:], ALU.mult, ALU.mult),
                              reads=[xbk, "rstd", "nw1", "nw2"], writes=["bufA"])
                else:
                    for k in range(KC):
                        fw.op("dve", stt(xb[:, k, :], xb[:, k, :], nw_ap_fn(k), rstd_t[:], ALU.mult, ALU.mult),
                              reads=[xbk, "rstd", "nwf"], writes=[xbk])
                    for tl in range(4):
                        n = tb * 4 + tl
                        s = n % 2
                        yt = avf(BUFA + s * 2 * D, 2 * D)
                        for k0 in range(0, KC, 4):
                            gi = nxt("g", 4)
                            ps = psG[gi]
                            fw.pe_group([trp(ps[:, j * 128:(j + 1) * 128], xb[:, k0 + j, tl * 128:(tl + 1) * 128], ident_f[:])
                                         for j in range(4)], reads=[xbk, "identf"], writes=[("psG", gi)])
                            fw.op("act", actf(yt[:, k0 * 128:(k0 + 4) * 128], ps[:], AF.Copy),
                                  reads=[("psG", gi)], writes=[("yt", s)])
                        ld("pool", y_out[n * 128:(n + 1) * 128, :], yt, reads=[("yt", s)], writes=["y"])

        pre = {"done": None}

        def prefetch_w(wload, tag):
            wload(0, 0)
            pre["done"] = tag

        def gemm(Xfn, kc, nblk, wload, units_of_block, wctx=None, tag=None):
            wctx = wctx or WCTX
            deferred = []
            ns = len(wctx["slots"])
            if tag is not None and pre["done"] == tag:
                pre["done"] = None
            else:
                wload(0, 0)
            for b in range(1, min(ns - 1, nblk)):
                wload(b, b % ns)
            for blk in range(nblk):
                slot = blk % ns
                if blk + ns - 1 < nblk:
                    wload(blk + ns - 1, (blk + ns - 1) % ns)
                for u in units_of_block(blk, slot):
                    gi = nxt("g", 4)
                    ps = psG[gi][:, 0:u["n"]]
                    fw.pe_group([mm(ps, u["lhs"](k), u["rhs"](k), k == 0, k == kc - 1) for k in range(kc)],
                                reads=[(wctx["key"], slot), "bufA", "bufB"], writes=[("psG", gi)])
                    for dfn in deferred:
                        dfn()
                    deferred = []
                    d = u["epi"](ps, gi)
                    if d is not None:
                        deferred.append(d)
            for dfn in deferred:
                dfn()

        def wslice(slot, k, c0, n, wctx=None):
            wctx = wctx or WCTX
            return wctx["slots"][slot][:, k * wctx["wbc"] + c0: k * wctx["wbc"] + c0 + n]

        def wload_cols(Wl, kc_total, col_of_blk, ncols=None, dst_off=0, wctx=None):
            wctx = wctx or WCTX
            wbc = wctx["wbc"]
            ncols = ncols or wbc

            def f(blk, slot):
                col = col_of_blk(blk)
                src = Wl.rearrange("(k p) n -> p k n", p=128)
                dst = wctx["slots"][slot].rearrange("p (k n) -> p k n", n=wbc)
                step = 4
                for k0 in range(0, kc_total, step):
                    k1 = min(kc_total, k0 + step)
                    ld("pool", dst[:, k0:k1, dst_off:dst_off + ncols], src[:, k0:k1, col:col + ncols],
                       writes=[(wctx["key"], slot)])
            return f

        def phase1(l):
            Wl = w_in[l]
            W1 = 512 if c.QK % 512 == 0 else 256
            W1S = KC * W1
            w1ctx = dict(slots=[arena[:, BUFB + i * W1S: BUFB + (i + 1) * W1S] for i in range(2)], wbc=W1, key="wb1")
            CS = BUFB + 2 * W1S
            assert CS + 2 * T <= ARENA
            cos_v = arena[:, CS:CS + T]
            sin_v = arena[:, CS + T:CS + 2 * T]
            ld("pool", cos_v, tbs["tb_cos"], writes=["cos"])
            ld("pool", sin_v, tbs["tb_sin"], writes=["sin"])
            nblk = c.INC // W1

            def kind_of(col):
                if col < c.ok:
                    return "q"
                if col < c.ov:
                    return "k"
                if col < c.og:
                    return "v"
                if col < c.osu:
                    return "g"
                if col < c.osv:
                    return "su"
                if col < c.oga:
                    return "sv"
                if col < c.ogb:
                    return "ga"
                return "gb"

            def units(blk, slot):
                col = blk * W1
                kind = kind_of(col)
                us = []
                if kind in ("v", "g", "sv"):
                    base = {"v": c.ov, "g": c.og, "sv": c.osv}[kind]
                    dst = {"v": ZVv, "g": ZG, "sv": ZS}[kind]
                    func = {"v": AF.Copy, "g": AF.Silu, "sv": AF.Gelu}[kind]
                    for n in range(NT):
                        def epi(ps, gi, n=n, dst=dst, func=func, c0=col - base):
                            si = nxt("sb", NSB)
                            fw.op("act", actf(stg_b[si][:, 0:W1], ps, func), reads=[("psG", gi)], writes=[("stgb", si)])
                            ld("pool", dst[:, n, c0:c0 + W1], stg_b[si][:, 0:W1], reads=[("stgb", si)], writes=["Z"])
                            return None
                        us.append(dict(lhs=lambda k, n=n: hT(k, n * 128, 128),
                                       rhs=lambda k, slot=slot: wslice(slot, k, 0, W1, w1ctx), n=W1, epi=epi))
                else:
                    base = {"q": c.oq, "k": c.ok, "su": c.osu, "ga": c.oga, "gb": c.ogb}[kind]
                    for m in range(W1 // 128):
                        ch = (col - base) // 128 + m
                        for tb in range(NB):
                            if kind in ("q", "k"):
                                dst = ZQ if kind == "q" else ZK

                                def epi(ps, gi, ch=ch, tb=tb, dst=dst):
                                    si = nxt("sb", NSB)
                                    zb = stg_b[si]
                                    fw.op("act", actf(zb[:], ps, AF.Copy), reads=[("psG", gi)], writes=[("stgb", si)])

                                    def later():
                                        fw.pe_group([mm(psX[:], pm_b[:], zb[:], True, True)],
                                                    reads=[("stgb", si), "pmb"], writes=["psX"])
                                        f1 = nxt("sf", 3)
                                        fw.op("dve", tt(stg_f[f1][:], zb[:], cos_v[:, tb * 512:(tb + 1) * 512], ALU.mult),
                                              reads=[("stgb", si), "cos"], writes=[("stgf", f1)])
                                        f2 = nxt("sf", 3)
                                        fw.op("dve", tt(stg_f[f2][:], psX[:], sin_v[:, tb * 512:(tb + 1) * 512], ALU.mult),
                                              reads=["psX", "sin"], writes=[("stgf", f2)])
                                        so = nxt("sb", NSB)
                                        fw.op("dve", tt(stg_b[so][:], stg_f[f1][:], stg_f[f2][:], ALU.add),
                                              reads=[("stgf", f1), ("stgf", f2)], writes=[("stgb", so)])
                                        ld("pool", dst[ch, :, tb * 512:(tb + 1) * 512], stg_b[so][:],
                                           reads=[("stgb", so)], writes=["Z"])
                                    return later
                            else:
                                dst = {"su": ZU, "ga": GA, "gb": GB}[kind]
                                func = AF.Gelu if kind == "su" else AF.Sigmoid

                                def epi(ps, gi, ch=ch, tb=tb, dst=dst, func=func):
                                    si = nxt("sb", NSB)
                                    fw.op("act", actf(stg_b[si][:], ps, func), reads=[("psG", gi)], writes=[("stgb", si)])
                                    ld("pool", dst[ch, :, tb * 512:(tb + 1) * 512], stg_b[si][:],
                                       reads=[("stgb", si)], writes=["Z"])
                                    return None
                            us.append(dict(lhs=lambda k, slot=slot, m=m: wslice(slot, k, m * 128, 128, w1ctx),
                                           rhs=lambda k, tb=tb: hT(k, tb * 512, 512), n=512, epi=epi))
                return us

            gemm(hT, KC, nblk, wload_cols(Wl, KC, lambda blk: blk * W1, wctx=w1ctx), units, wctx=w1ctx)

        R_SLOT = 2 * T + 2 * NT * 256
        R0 = 0
        RTMP = R0 + 2 * R_SLOT
        RTMP_SZ = 14336
        PA_END = 512 + 4 * G * 256
        O_OFF = RTMP + RTMP_SZ
        GTH = O_OFF + 2 * 2 * NT * 256
        assert GTH + 4 * T <= ARENA, "arena overflow (retention)"
        S_OFF = RTMP + PA_END
        assert S_OFF + 4 * D + 2 * D + KC * 512 * 2 <= ARENA, "arena overflow (sgu)"

        def load_layer_small(l):
            ld("pool", wsb_t[:].rearrange("p (g j) -> p g j", g=SG), ws_d[l].rearrange("g i j -> i g j"), writes=["wsb"])
            ld("sp", bsf_t[:], bs_d[l:l + 1, :], writes=["bsf"])
            for g in range(SG):
                ti = nxt("t", 2)
                fw.pe_group([trp(psT[ti][:, 0:128], wsb_t[:, g * 128:(g + 1) * 128], ident_b[:])],
                            reads=["wsb", "identb"], writes=[("psT", ti)])
                fw.op("dve", tt(wT_t[:, g * 128:(g + 1) * 128], psT[ti][:, 0:128], cmask_t[:], ALU.mult),
                      reads=[("psT", ti), "cmask"], writes=["wT"])
            fw.op("dve", lambda E: E.tensor_copy(out=bsh_t[:], in_=bsf_t[:]), reads=["bsf"], writes=["bsh"])
            fw.op("dve", tt(bsr_t[:], bsf_t[:], bsh_t[:], ALU.subtract), reads=["bsf", "bsh"], writes=["bsr"])
            fw.op("dve", lambda E: E.tensor_copy(out=bsl_t[:], in_=bsr_t[:]), reads=["bsr"], writes=["bsl"])

        def head_views(s):
            b = R0 + s * R_SLOT
            qT = av(b, T)
            kT = av(b + T, T)
            v = av(b + 2 * T, NT * 256)
            sg = av(b + 2 * T + NT * 256, NT * 256)
            return qT, kT, v, sg

        def kt_make(h, n, kT, s, pi=0):
            ti = nxt("t", 2)
            fw.pe_group([trp(psT[ti][:, 0:128], kT[:, n * 128:(n + 1) * 128], ident_b[:])],
                        reads=[("hk", s), "identb"], writes=[("psT", ti)])
            kk = pi * 2 + n % 2
            kt = av(RTMP + kk * 128, 128)
            fw.op("act", actf(kt, psT[ti][:, 0:128], AF.Copy, scale=kdec_t[:, n * H + h:n * H + h + 1]),
                  reads=[("psT", ti), "kdec"], writes=[("kt", kk)])
            return kt, ("kt", kk)

        def pass_a():
            for h in range(H):
                s = h % 2
                qT, kT, v, sg = head_views(s)
                ld("sp", kT, ZK[h], writes=[("hk", s)])
                ld("sp", v.rearrange("p (n e) -> p n e", n=NT), ZVv[:, :, h * 256:(h + 1) * 256], writes=[("hv", s)])
                acc = psG[h % 2][:, 0:256]
                import os as _os
                PA = int(_os.environ.get("K_PA", "9"))
                for n in range(NT):
                    kt, ktk = kt_make(h, n, kT, s)
                    if PA >= 2:
                        fw.pe_group([mm(acc, kt, v[:, n * 256:(n + 1) * 256], n == 0, n == NT - 1)],
                                    reads=[ktk, ("hv", s)], writes=[("psG", h % 2)])
                if PA < 3:
                    continue
                ex = avf(RTMP + 512 + (h % 2) * 2 * G * 256, 2 * G * 256)
                for sl in range(G):
                    fw.op("dve", lambda E, ex=ex, sl=sl, acc=acc, h=h: E.tensor_scalar_mul(
                        out=ex[:, sl * 256:(sl + 1) * 256], in0=acc, scalar1=oneh_t[:, sl * H + h: sl * H + h + 1]),
                        reads=[("psG", h % 2), "oneh"], writes=[("ex", h % 2)])
                if PA < 4:
                    continue
                ld("pool", EXI.rearrange("(g h p) e -> p g h e", g=G, h=H)[:, :, h, :],
                   ex.rearrange("p (g e) -> p g e", g=G), reads=[("ex", h % 2)], writes=["EXI"])
            import os as _os
            if _os.environ.get("K_NOCOLL"):
                return
            if _os.environ.get("K_COLL8"):
                rgs = [list(range(c.NCORES))]
            else:
                rgs = [list(range(b * G, (b + 1) * G)) for b in range(c.BATCH)]
            fw.coll(lambda E: E.collective_compute(
                "AllReduce", ALU.add, replica_groups=rgs,
                ins=[EXI.opt()], outs=[EXO.opt()]), reads=["EXI"], writes=["EXO"])

        def sgu(l):
            ZV_O = S_OFF
            TMP_O = ZV_O + 4 * D
            ZU_O = TMP_O + 2 * D
            OUT_O = ZU_O + KC * 512
            LN_O = OUT_O + KC * 512
            assert LN_O + 2 * D <= ARENA
            lnw_v = av(LN_O, D)
            lnb_v = av(LN_O + D, D)
            ld("pool", lnw_v, lnw_d[l:l + 1, :].partition_broadcast(128), writes=["lnw"])
            ld("pool", lnb_v, lnb_d[l:l + 1, :].partition_broadcast(128), writes=["lnb"])
            for tb in range(NB):
                zv = av(ZV_O, 4 * D)
                vn = zv
                zu = av(ZU_O, KC * 512).rearrange("p (k t) -> p k t", k=KC)
                out = av(OUT_O, KC * 512).rearrange("p (k t) -> p k t", k=KC)
                ld("sp", zv.rearrange("p (n d) -> p n d", n=4), ZS[:, tb * 4:(tb + 1) * 4, :], writes=[("zv", 0), ("zv", 1), ("zv", 2), ("zv", 3)])
                ld("sp", zu, ZU[:, :, tb * 512:(tb + 1) * 512].rearrange("k p t -> p k t"), writes=["zu"])
                mv = small[:, 0:8]
                nchk = max(1, D // 512)
                for tl in range(4):
                    stats = small[:, 16 + tl * 6 * nchk: 16 + (tl + 1) * 6 * nchk]
                    for cc in range(nchk):
                        w = min(512, D)
                        fw.op("dve", lambda E, tl=tl, cc=cc, stats=stats, w=w: E.bn_stats(
                            out=stats[:, cc * 6:(cc + 1) * 6], in_=zv[:, tl * D + cc * w: tl * D + (cc + 1) * w]),
                            reads=[("zv", tl)], writes=[("sgst", tl)])
                    fw.op("dve", lambda E, tl=tl, stats=stats: E.bn_aggr(out=mv[:, tl * 2:tl * 2 + 2], in_=stats.rearrange("p (c s) -> p c s", s=6)),
                          reads=[("sgst", tl)], writes=["sgmv"])
                rs = small[:, 8:12]
                mv3 = mv.rearrange("p (t two) -> p t two", two=2)
                fw.op("act", actf(rs, mv3[:, :, 1], AF.Sqrt, bias=EPS), reads=["sgmv"], writes=["sgrs"])
                fw.op("dve", lambda E: E.reciprocal(out=rs, in_=rs), reads=["sgrs"], writes=["sgrs"])
                for tl in range(4):
                    tmp = avf(TMP_O, 2 * D)
                    fw.op("dve", stt(tmp, zv[:, tl * D:(tl + 1) * D], mv[:, tl * 2:tl * 2 + 1], lnw_v,
                                     ALU.subtract, ALU.mult), reads=[("zv", tl), "sgmv", "lnw"], writes=["sgtmp"])
                    fw.op("dve", stt(vn[:, tl * D:(tl + 1) * D], tmp, rs[:, tl:tl + 1], lnb_v, ALU.mult, ALU.add),
                          reads=["sgtmp", "sgrs", "lnb"], writes=[("zv", tl)])
                for tl in range(4):
                    for k0 in range(0, KC, 4):
                        gi = nxt("g", 4)
                        ps = psG[gi]
                        fns = []
                        for j in range(4):
                            k = k0 + j
                            g = k // 2
                            o = ps[:, j * 128:(j + 1) * 128]
                            fns.append(mm(o, vn[:, tl * D + k * 128: tl * D + (k + 1) * 128], wT_t[:, g * 128:(g + 1) * 128], True, False))
                            fns.append(mm(o, ones_b[0:1, :], bsh_t[0:1, g * 128:(g + 1) * 128], False, False))
                            fns.append(mm(o, ones_b[0:1, :], bsl_t[0:1, g * 128:(g + 1) * 128], False, True))
                        fw.pe_group(fns, reads=[("zv", tl), "wT", "bsh", "bsl", "onesb"], writes=[("psG", gi)])
                        fw.op("dve", tt(out[:, k0:k0 + 4, tl * 128:(tl + 1) * 128],
                                        ps[:].rearrange("p (k t) -> p k t", k=4),
                                        zu[:, k0:k0 + 4, tl * 128:(tl + 1) * 128], ALU.mult),
                              reads=[("psG", gi), "zu"], writes=["sgout"])
                ld("pool", ST[:, :, tb * 512:(tb + 1) * 512].rearrange("k p t -> p k t"), out, reads=["sgout"], writes=["ST"])

        def pass_b(l):
            SIN_HI = RTMP + 512
            SIN_LO = SIN_HI + H * 256
            EXL = SIN_LO + H * 256
            for h in range(H):
                e = h % 2
                exl = avf(EXL + e * 2 * G * 256, 2 * G * 256)
                ld("sp", exl.rearrange("p (g e) -> p g e", g=G),
                   EXO.rearrange("(g h p) e -> p g h e", g=G, h=H)[:, :, h, :], reads=["EXO"], writes=[("exl", e)])
                s0 = s0_t[:]
                fw.op("dve", lambda E, exl=exl, h=h, s0=s0: E.tensor_scalar_mul(out=s0, in0=exl[:, 0:256], scalar1=coef_t[:, h:h + 1]),
                      reads=[("exl", e), "coef"], writes=["s0"])
                for sl in range(1, G):
                    fw.op("dve", stt(s0, exl[:, sl * 256:(sl + 1) * 256], coef_t[:, sl * H + h: sl * H + h + 1], s0,
                                     ALU.mult, ALU.add), reads=[("exl", e), "coef", "s0"], writes=["s0"])
                hi = av(SIN_HI + h * 256, 256)
                lo = av(SIN_LO + h * 256, 256)
                fw.op("dve", lambda E, hi=hi, s0=s0: E.tensor_copy(out=hi, in_=s0), reads=["s0"], writes=["sinhi"])
                fw.op("dve", tt(s0, s0, hi, ALU.subtract), reads=["s0", "sinhi"], writes=["s0"])
                fw.op("dve", lambda E, lo=lo, s0=s0: E.tensor_copy(out=lo, in_=s0), reads=["s0"], writes=["sinlo"])
            SB_O = EXL + 4 * G * 256
            SC_O = SB_O + 1024
            TF_O = SC_O + 512
            U_O = TF_O + 2048
            assert U_O + 1024 <= RTMP + RTMP_SZ, "RTMP overflow"

            def head_gen(h, pi):
                s = pi
                qT, kT, v, sg = head_views(s)
                ld("sp", qT, ZQ[h], writes=[("hq", s)])
                ld("sp", kT, ZK[h], writes=[("hk", s)])
                ld("sp", v.rearrange("p (n e) -> p n e", n=NT), ZVv[:, :, h * 256:(h + 1) * 256], writes=[("hv", s)])
                ld("sp", sg.rearrange("p (n e) -> p n e", n=NT), ZG[:, :, h * 256:(h + 1) * 256], writes=[("hg", s)])
                if pi == 0:
                    acc, acck = psS[:, 0:256], ("psS", 0)
                else:
                    acc, acck = psG[3][:, 0:256], ("psG", 3)
                fw.pe_group([mm(acc, ident_b[:], av(SIN_HI + h * 256, 256), True, False),
                             mm(acc, ident_b[:], av(SIN_LO + h * 256, 256), False, False)],
                            reads=["sinhi", "sinlo", "identb"], writes=[acck])
                o_all = avf(O_OFF + pi * 2 * NT * 256, 2 * NT * 256)
                mv = small[:, pi * 32: pi * 32 + 2 * NT]
                yield
                for n in range(NT):
                    sbi = pi * 2 + n % 2
                    Sb = av(SB_O + sbi * 256, 256)
                    fw.op("act", actf(Sb, acc, AF.Copy), reads=[acck], writes=[("Sb", sbi)])
                    kt, ktk = kt_make(h, n, kT, s, pi)
                    fw.pe_group([mm(psX[:, 0:128], kT[:, n * 128:(n + 1) * 128], qT[:, n * 128:(n + 1) * 128], True, True)],
                                reads=[("hk", s), ("hq", s)], writes=["psX"])
                    sc = av(SC_O + sbi * 128, 128)
                    fw.op("dve", stt(sc, psX[:, 0:128], kdec_t[:, n * H + h:n * H + h + 1],
                                     mask_t[:, h * 128:(h + 1) * 128], ALU.mult, ALU.mult),
                          reads=["psX", "kdec", "mask"], writes=[("sc", sbi)])
                    yield
                    gi = nxt("g3", 3)
                    po = psG[gi][:, 0:256]
                    fw.pe_group([mm(po, sc, v[:, n * 256:(n + 1) * 256], True, False),
                                 mm(po, qT[:, n * 128:(n + 1) * 128], Sb, False, True)],
                                reads=[("sc", sbi), ("Sb", sbi), ("hv", s), ("hq", s)], writes=[("psG", gi)])
                    fw.pe_group([mm(acc, kt, v[:, n * 256:(n + 1) * 256], False, n == NT - 1)],
                                reads=[ktk, ("hv", s)], writes=[acck])
                    fw.op("act", actf(o_all[:, n * 256:(n + 1) * 256], po, AF.Copy, scale=odec_t[:, n * H + h:n * H + h + 1]),
                          reads=[("psG", gi), "odec"], writes=[("o", pi, n)])
                    st6 = small[:, 64 + sbi * 6: 64 + sbi * 6 + 6]
                    fw.op("dve", lambda E, n=n, st6=st6, o_all=o_all: E.bn_stats(out=st6, in_=o_all[:, n * 256:(n + 1) * 256]),
                          reads=[("o", pi, n)], writes=[("st6", sbi)])
                    fw.op("dve", lambda E, n=n, st6=st6, mv=mv: E.bn_aggr(out=mv[:, 2 * n:2 * n + 2], in_=st6),
                          reads=[("st6", sbi)], writes=[("rmv", pi)])
                    yield
                rs = small[:, 128 + pi * 32:128 + pi * 32 + NT]
                fw.op("act", actf(rs, mv.rearrange("p (t two) -> p t two", two=2)[:, :, 1], AF.Sqrt, bias=EPS),
                      reads=[("rmv", pi)], writes=[("rrs", pi)])
                fw.op("dve", lambda E, rs=rs: E.reciprocal(out=rs, in_=rs), reads=[("rrs", pi)], writes=[("rrs", pi)])
                gth = av(GTH + s * 2 * T, 2 * T)
                yield
                for n in range(NT):
                    ti2 = pi * 2 + n % 2
                    tf = avf(TF_O + ti2 * 512, 512)
                    fw.op("dve", tsc(tf, o_all[:, n * 256:(n + 1) * 256], mv[:, 2 * n:2 * n + 1], rs[:, n:n + 1],
                                     ALU.subtract, ALU.mult), reads=[("o", pi, n), ("rmv", pi), ("rrs", pi)], writes=[("tf", ti2)])
                    u = av(U_O + ti2 * 256, 256)
                    fw.op("dve", tt(u, tf, sg[:, n * 256:(n + 1) * 256], ALU.mult),
                          reads=[("tf", ti2), ("hg", s)], writes=[("u", ti2)])
                    ti = nxt("t", 2)
                    fw.pe_group([trp(psT[ti][:, 0:128], u[:, 0:128], ident_b[:]),
                                 trp(psT[ti][:, 128:256], u[:, 128:256], ident_b[:])],
                                reads=[("u", ti2), "identb"], writes=[("psT", ti)])
                    yield
                    for e2 in range(2):
                        fw.op("act", actf(gth[:, e2 * T + n * 128: e2 * T + (n + 1) * 128], psT[ti][:, e2 * 128:(e2 + 1) * 128],
                                          AF.Copy, scale=gnw_t[:, l * RC + h * 2 + e2: l * RC + h * 2 + e2 + 1]),
                              reads=[("psT", ti), "gnw"], writes=[("gth", s)])
                    yield
                ld("pool", GT[h * 2:h * 2 + 2].rearrange("k p t -> p k t"), gth.rearrange("p (k t) -> p k t", k=2),
                   reads=[("gth", s)], writes=["GT"])

            for h0 in range(0, H, 2):
                gens = [head_gen(h0 + pi, pi) for pi in range(min(2, H - h0))]
                alive = list(gens)
                while alive:
                    for g_ in list(alive):
                        try:
                            next(g_)
                        except StopIteration:
                            alive.remove(g_)

        def load_bufA(src, nch):
            for k0 in range(0, nch, 4):
                k1 = min(nch, k0 + 4)
                ld("sp", arena[:, BUFA + k0 * T: BUFA + k1 * T].rearrange("p (k t) -> p k t", t=T),
                   src[k0:k1].rearrange("k p t -> p k t"), writes=["bufA"])
            assert nch * T <= ARENA

        def phase3(l):
            load_bufA(GT, RC)

            def mk_units(gate, first):
                def units(blk, slot):
                    us = []
                    for m in range(WBC // 128):
                        ch = blk * (WBC // 128) + m
                        for tb in range(NB):
                            def epi(ps, gi, ch=ch, tb=tb):
                                li = nxt("lb", 2)
                                ld("sp", ld_b[li][:], gate[ch, :, tb * 512:(tb + 1) * 512], writes=[("ldb", li)])
                                if first:
                                    fw.op("dve", tt(mT(ch, tb * 512, 512), ps, ld_b[li][:], ALU.mult),
                                          reads=[("psG", gi), ("ldb", li)], writes=[("mT", ch, tb)])
                                else:
                                    fi = nxt("sf", 3)
                                    fw.op("dve", tt(stg_f[fi][:], ps, ld_b[li][:], ALU.mult),
                                          reads=[("psG", gi), ("ldb", li)], writes=[("stgf", fi)])
                                    fw.op("dve", tt(mT(ch, tb * 512, 512), mT(ch, tb * 512, 512), stg_f[fi][:], ALU.add),
                                          reads=[("stgf", fi), ("mT", ch, tb)], writes=[("mT", ch, tb)])
                                return None
                            us.append(dict(lhs=lambda k, slot=slot, m=m: wslice(slot, k, m * 128, 128),
                                           rhs=lambda k, tb=tb: hT(k, tb * 512, 512), n=512, epi=epi))
                    return us
                return units

            gemm(hT, RC, D // WBC, wload_cols(ret_proj[l], RC, lambda blk: blk * WBC), mk_units(GA, True), tag=("p3a", l))
            prefetch_w(wload_cols(sgu_proj[l], KC, lambda blk: blk * WBC), ("p3b", l))
            fw.barrier()
            load_bufA(ST, KC)
            gemm(hT, KC, D // WBC, wload_cols(sgu_proj[l], KC, lambda blk: blk * WBC), mk_units(GB, False), tag=("p3b", l))
            prefetch_w(wload_cols(w_out[l], KC, lambda blk: blk * WBC), ("p3c", l))
            fw.barrier()
            resid_gemm(mT, KC, w_out[l], tag=("p3c", l))
            prefetch_w(wl4(l), ("p4", l))

        def resid_gemm(Xfn, kc, Wl, tag=None):
            def units(blk, slot):
                us = []
                for m in range(WBC // 128):
                    ch = blk * (WBC // 128) + m
                    for tb in range(NB):
                        def epi(ps, gi, ch=ch, tb=tb):
                            li = nxt("lf", 2)
                            ld("sp", ld_f[li][:], XT[ch, :, tb * 512:(tb + 1) * 512], reads=[("XT", ch, tb)], writes=[("ldf", li)])
                            fi = nxt("sf", 3)
                            fw.op("dve", tt(stg_f[fi][:], ps, ld_f[li][:], ALU.add),
                                  reads=[("psG", gi), ("ldf", li)], writes=[("stgf", fi)])
                            ld("pool", XT[ch, :, tb * 512:(tb + 1) * 512], stg_f[fi][:], reads=[("stgf", fi)],
                               writes=[("XT", ch, tb)])
                            return None
                        us.append(dict(lhs=lambda k, slot=slot, m=m: wslice(slot, k, m * 128, 128),
                                       rhs=lambda k, tb=tb: Xfn(k, tb * 512, 512), n=512, epi=epi))
                return us
            gemm(Xfn, kc, D // WBC, wload_cols(Wl, kc, lambda blk: blk * WBC), units, tag=tag)

        def wl4(l):
            Wl = w_ffn_in[l]
            HB = WBC // 2

            return wload_cols(Wl, KC, lambda b: b * WBC)

        def phase4(l):
            HB = WBC // 2
            nblk = DFF // HB
            wl = wl4(l)

            def units(blk, slot):
                us = []
                for m in range(HB // 128):
                    ch = blk * (HB // 128) + m
                    for tb in range(NB):
                        hold = {}

                        def epi_a(ps, gi, hold=hold):
                            fi = nxt("sf", 3)
                            fw.op("act", actf(stg_f[fi][:], ps, AF.Silu), reads=[("psG", gi)], writes=[("stgf", fi)])
                            hold["fi"] = fi
                            return None

                        def epi_c(ps, gi, hold=hold, ch=ch, tb=tb):
                            fi = hold["fi"]
                            si = nxt("sb", NSB)
                            fw.op("dve", tt(stg_b[si][:], ps, stg_f[fi][:], ALU.mult),
                                  reads=[("psG", gi), ("stgf", fi)], writes=[("stgb", si)])
                            ld("pool", AT[ch, :, tb * 512:(tb + 1) * 512], stg_b[si][:], reads=[("stgb", si)], writes=["AT"])
                            return None
                        us.append(dict(lhs=lambda k, slot=slot, m=m: wslice(slot, k, m * 128, 128),
                                       rhs=lambda k, tb=tb: hT(k, tb * 512, 512), n=512, epi=epi_a))
                        us.append(dict(lhs=lambda k, slot=slot, m=m: wslice(slot, k, HB + m * 128, 128),
                                       rhs=lambda k, tb=tb: hT(k, tb * 512, 512), n=512, epi=epi_c))
                return us
            gemm(hT, KC, nblk, wl, units, tag=("p4", l))
            prefetch_w(wload_cols(w_ffn_out[l][0:KH * 128, :], KH, lambda blk: blk * WBC), ("p5", l, 0))

        def phase5(l):
            Wl = w_ffn_out[l]
            for half in range(2):
                load_bufA(AT[half * KH:(half + 1) * KH], KH)
                resid_gemm(hT, KH, Wl[half * KH * 128:(half + 1) * KH * 128, :], tag=("p5", l, half))
                if half == 0:
                    prefetch_w(wload_cols(Wl[KH * 128:2 * KH * 128, :], KH, lambda blk: blk * WBC), ("p5", l, 1))

        import os as _os
        STOP = int(_os.environ.get("K_STOP", "99"))

        def program():
            ingest()
            fw.barrier()
            if STOP <= 1:
                return
            for l in range(NL):
                load_layer_small(l)
                norm_pass(lambda k, l=l: nw1_t[:, l * KC + k: l * KC + k + 1])
                fw.barrier()
                if STOP <= 2:
                    return
                phase1(l)
                fw.barrier()
                if STOP <= 3:
                    return
                pass_a()
                if STOP <= 4:
                    return
                sgu(l)
                fw.barrier()
                if STOP <= 5:
                    return
                prefetch_w(wload_cols(ret_proj[l], RC, lambda blk: blk * WBC), ("p3a", l))
                pass_b(l)
                fw.barrier()
                if STOP <= 6:
                    return
                phase3(l)
                fw.barrier()
                if STOP <= 7:
                    return
                norm_pass(lambda k, l=l: nw2_t[:, l * KC + k: l * KC + k + 1])
                fw.barrier()
                if STOP <= 8:
                    return
                phase4(l)
                fw.barrier()
                if STOP <= 9:
                    return
                phase5(l)
                fw.barrier()
                if STOP <= 10:
                    return
            norm_pass(lambda k: nwf_t[:, k:k + 1], final=True)

        program()
        fw.finish()
        fw.run(block)
    return nc


def _run(cfg, x, norm_mix_w, w_in, ret_gn_w, ret_proj, sgu_ln_w, sgu_ln_b, sgu_w_s, sgu_b_s, sgu_proj, w_out,
         norm_ffn_w, w_ffn_in, w_ffn_out, final_norm_w, trace=False):
    c = cfg
    f = lambda a: np.ascontiguousarray(np.asarray(a, dtype=np.float32))
    x = f(x).reshape(c.BATCH * c.SEQ, c.D)
    shared = {
        "w_in": f(w_in), "ret_proj": f(ret_proj), "sgu_proj": f(sgu_proj), "w_out": f(w_out),
        "w_ffn_in": f(f(w_ffn_in).reshape(c.NL, c.D, 2, c.FC, 128).transpose(0, 1, 3, 2, 4).reshape(c.NL, c.D, 2 * c.DFF)),
        "w_ffn_out": f(w_ffn_out),
        "norm_mix_w": f(f(norm_mix_w).reshape(c.NL, c.KC, 128).transpose(0, 2, 1)),
        "norm_ffn_w": f(f(norm_ffn_w).reshape(c.NL, c.KC, 128).transpose(0, 2, 1)),
        "final_norm_w": f(f(final_norm_w).reshape(c.KC, 128).transpose(1, 0)),
        "ret_gn_w": f(f(ret_gn_w).reshape(c.NL, c.RC, 128).transpose(0, 2, 1)),
        "sgu_ln_w": f(sgu_ln_w), "sgu_ln_b": f(sgu_ln_b), "sgu_w_s": f(sgu_w_s),
        "sgu_b_s": f(f(sgu_b_s).reshape(c.NL, c.SG * 128)),
    }
    nc = _build(c)
    in_maps = []
    for core in range(c.NCORES):
        m = dict(shared)
        m["x"] = x[core * c.T:(core + 1) * c.T]
        m.update(_tables(c, core))
        in_maps.append(m)
    res = run_bass_kernel_spmd(nc, in_maps, core_ids=list(range(c.NCORES)), **({"trace": True} if trace else {}))
    y = np.concatenate([res.results[i]["y"] for i in range(c.NCORES)], 0)
    return y.reshape(c.BATCH, c.SEQ, c.D).astype(np.float32), res


def kernel(**inputs):
    cfg = Cfg()
    y, _ = _run(cfg, **inputs)
    return y
```

```python
import math
from contextlib import ExitStack

import numpy as np
import concourse.bass as bass
import concourse.mybir as mybir
from concourse.bass_utils import run_bass_kernel_spmd

F32 = mybir.dt.float32
BF16 = mybir.dt.bfloat16
AF = mybir.ActivationFunctionType
ALU = mybir.AluOpType
EPS = 1e-6
ENGS = ("pe", "act", "dve", "pool", "sp")


class Fw:
    def __init__(self, nc, stack, n_dma_slots=8):
        self.nc = nc
        self.q = {e: [] for e in ENGS}
        self.sem = {e: stack.enter_context(nc.semaphore("s_" + e)) for e in ENGS}
        self.cnt = {e: 0 for e in ENGS}
        self.waited = {e: {} for e in ENGS}
        self.last_w = {}
        self.readers = {}
        self.slots = {}
        self.slot_rr = {}
        for qn in ("sp", "pool"):
            self.slots[qn] = [[stack.enter_context(nc.semaphore("d_%s%d" % (qn, i))), 0]
                              for i in range(n_dma_slots)]
            self.slot_rr[qn] = 0
        self.cc_sem = stack.enter_context(nc.semaphore("cc_sem"))
        self.coll_cnt = 0

    def _wait(self, eng, ev):
        if ev is None:
            return
        sem, val, src = ev[0], ev[1], ev[2]
        if src == eng and eng == "pe":
            return
        key = id(sem)
        if self.waited[eng].get(key, 0) >= val:
            return
        self.waited[eng][key] = val
        self.q[eng].append(lambda E, sem=sem, val=val: E.wait_ge(sem, val))

    def _deps(self, eng, reads, writes, group=None):
        for r in reads:
            for ev in self.last_w.get(r, ()):
                self._wait(eng, ev)
        for w in writes:
            for ev in self.last_w.get(w, ()):
                if group is not None and len(ev) > 3 and ev[3] == group:
                    continue
                self._wait(eng, ev)
            for ev in self.readers.get(w, ()):
                self._wait(eng, ev)

    def _record(self, ev, reads, writes):
        for w in writes:
            if ev[2] is None:
                lst = [e for e in self.last_w.get(w, ()) if e[2] is None and e[0] is not ev[0]]
                lst.append(ev)
                self.last_w[w] = lst
            else:
                self.last_w[w] = [ev]
            self.readers[w] = []
        for r in reads:
            lst = self.readers.setdefault(r, [])
            lst[:] = [e for e in lst if e[0] is not ev[0]]
            lst.append(ev)

    def op(self, eng, fn, reads=(), writes=()):
        self._deps(eng, reads, writes)
        self.cnt[eng] += 1
        sem = self.sem[eng]
        ev = (sem, self.cnt[eng], eng)
        self.q[eng].append(lambda E, fn=fn, sem=sem: fn(E).then_inc(sem, 1))
        self._record(ev, reads, writes)
        return ev

    def pe_group(self, fns, reads=(), writes=()):
        self._deps("pe", reads, writes)
        for fn in fns[:-1]:
            self.q["pe"].append(fn)
        self.cnt["pe"] += 1
        sem = self.sem["pe"]
        ev = (sem, self.cnt["pe"], "pe")
        last = fns[-1]
        self.q["pe"].append(lambda E, fn=last, sem=sem: fn(E).then_inc(sem, 1))
        self._record(ev, reads, writes)
        return ev

    def coll(self, fn, reads=(), writes=()):
        self._deps("pool", reads, writes)
        self.coll_cnt += 1
        n = self.coll_cnt
        sem = self.cc_sem
        self.q["pool"].append(lambda E, fn=fn, sem=sem: fn(E).then_inc(sem, 1))
        self.q["pool"].append(lambda E, sem=sem, n=n: E.wait_ge(sem, n))
        self.cnt["pool"] += 1
        psem = self.sem["pool"]
        ev = (psem, self.cnt["pool"], "pool")
        self.q["pool"].append(lambda E, psem=psem: E.sem_inc(psem, 1))
        self._record(ev, reads, writes)
        return ev

    def dma(self, qn, fn, reads=(), writes=(), group=None):
        self._deps(qn, reads, writes, group)
        i = self.slot_rr[qn]
        self.slot_rr[qn] = (i + 1) % len(self.slots[qn])
        slot = self.slots[qn][i]
        sem = slot[0]
        if slot[1] > 0:
            self._wait(qn, (sem, 16 * slot[1], None))
        slot[1] += 1
        ev = (sem, 16 * slot[1], None, group)
        self.q[qn].append(lambda E, fn=fn, sem=sem: fn(E).then_inc(sem, 16))
        self._record(ev, reads, writes)
        return ev

    def _sp_wait_all(self):
        for q2 in self.slots:
            for sem, c in self.slots[q2]:
                if c > 0:
                    self._wait("sp", (sem, 16 * c, None))
        for e in ENGS:
            if self.cnt[e] > 0 and e != "sp":
                self._wait("sp", (self.sem[e], self.cnt[e], e))

    def barrier(self):
        self._sp_wait_all()
        self.cnt["sp"] += 1
        sem = self.sem["sp"]
        ev = (sem, self.cnt["sp"], "sp")
        self.q["sp"].append(lambda E, sem=sem: E.sem_inc(sem, 1))
        for e in ENGS:
            if e != "sp":
                self._wait(e, ev)

    def finish(self):
        self._sp_wait_all()

    def run(self, block):
        q = self.q

        @block.tensor
        def _(E):
            for f in q["pe"]:
                f(E)

        @block.scalar
        def _(E):
            for f in q["act"]:
                f(E)

        @block.vector
        def _(E):
            for f in q["dve"]:
                f(E)

        @block.gpsimd
        def _(E):
            for f in q["pool"]:
                f(E)

        @block.sync
        def _(E):
            for f in q["sp"]:
                f(E)


class Cfg:
    def __init__(self, D=2048, H=8, DFF=5632, T=2048, NL=4, NCORES=8, SEQ=8192, BATCH=2):
        self.D, self.H, self.DFF, self.T, self.NL = D, H, DFF, T, NL
        self.NCORES, self.SEQ, self.BATCH = NCORES, SEQ, BATCH
        self.G = SEQ // T
        assert self.G * BATCH == NCORES
        self.KC = D // 128
        self.QK = H * 128
        self.RV = H * 256
        self.RC = self.RV // 128
        self.SG = D // 256
        self.NT = T // 128
        self.NB = T // 512
        self.FC = DFF // 128
        self.INC = 2 * self.QK + 2 * self.RV + 4 * D
        self.oq, self.ok = 0, self.QK
        self.ov = 2 * self.QK
        self.og = self.ov + self.RV
        self.osu = self.og + self.RV
        self.osv = self.osu + D
        self.oga = self.osv + D
        self.ogb = self.oga + D


def _tables(cfg, core):
    H, T, NT, G = cfg.H, cfg.T, cfg.NT, cfg.G
    rank = core % G
    pos0 = rank * T
    half = 64
    inv = (10000.0 ** (-np.arange(half, dtype=np.float32) / np.float32(half))).astype(np.float32)
    pos = (pos0 + np.arange(T)).astype(np.float32)
    ang = (pos[None, :] * inv[:, None]).astype(np.float32)
    cos = np.cos(ang).astype(np.float32)
    sin = np.sin(ang).astype(np.float32)
    tb_cos = np.concatenate([cos, cos], 0)
    tb_sin = np.concatenate([-sin, sin], 0)
    logg = np.log1p(-(2.0 ** (-5.0 - np.arange(H, dtype=np.float64))))
    p = np.arange(128, dtype=np.float64)
    n = np.arange(NT, dtype=np.float64)
    loc = n[None, :, None] * 128 + p[:, None, None]
    kdec = np.exp(-logg[None, None, :] * loc)
    odec = np.exp(logg[None, None, :] * loc) * (128.0 ** -0.5)
    coef = np.zeros((128, G, H), np.float64)
    for s in range(G):
        if s < rank:
            coef[:, s, :] = np.exp(logg * T * (rank - s - 1))[None, :]
    oneh = np.zeros((128, G, H), np.float64)
    oneh[:, rank, :] = np.exp(logg * T)[None, :]
    gT = np.broadcast_to(np.exp(logg * T)[None, :], (128, H))
    j = np.arange(128)[:, None]
    i = np.arange(128)[None, :]
    cj, ci = j // 64, i // 64
    mask = np.zeros((128, H, 128), np.float64)
    for h in range(H):
        m = np.where(i >= j, 1.0, np.exp(logg[h] * 2.0 * (j - i)))
        m = np.where(cj > ci, 0.0, m)
        m = np.where(cj < ci, 1.0, m)
        mask[:, h, :] = m
    cmask = (cj <= ci).astype(np.float32)
    ident = np.eye(128, dtype=np.float32)
    pm = np.zeros((128, 128), np.float32)
    for d in range(128):
        pm[(d + 64) % 128, d] = 1.0
    f = lambda a: np.ascontiguousarray(np.asarray(a, dtype=np.float32))
    return {
        "tb_cos": f(tb_cos), "tb_sin": f(tb_sin),
        "tb_kdec": f(kdec.reshape(128, NT * H)), "tb_odec": f(odec.reshape(128, NT * H)),
        "tb_coef": f(coef.reshape(128, G * H)), "tb_oneh": f(oneh.reshape(128, G * H)), "tb_gT": f(gT),
        "tb_mask": f(mask.reshape(128, H * 128)), "tb_cmask": f(cmask),
        "tb_ident": f(ident), "tb_pm": f(pm), "tb_ones": np.ones((128, 128), np.float32),
    }


def _build(cfg):
    c = cfg
    D, H, T, NL, KC, NT, NB, FC, RC, SG, G, DFF = c.D, c.H, c.T, c.NL, c.KC, c.NT, c.NB, c.FC, c.RC, c.SG, c.G, c.DFF
    nc = bass.Bass("TRN2", target_bir_lowering=False)

    def din(name, shape):
        return nc.dram_tensor(name, list(shape), F32, kind="ExternalInput").ap()

    def dscr(name, shape, dt):
        return nc.dram_tensor(name, list(shape), dt, kind="Internal").ap()

    x_in = din("x", [T, D])
    w_in = din("w_in", [NL, D, c.INC])
    ret_proj = din("ret_proj", [NL, c.RV, D])
    sgu_proj = din("sgu_proj", [NL, D, D])
    w_out = din("w_out", [NL, D, D])
    w_ffn_in = din("w_ffn_in", [NL, D, 2 * DFF])
    w_ffn_out = din("w_ffn_out", [NL, DFF, D])
    nw1_d = din("norm_mix_w", [NL, 128, KC])
    nw2_d = din("norm_ffn_w", [NL, 128, KC])
    nwf_d = din("final_norm_w", [128, KC])
    gnw_d = din("ret_gn_w", [NL, 128, RC])
    lnw_d = din("sgu_ln_w", [NL, D])
    lnb_d = din("sgu_ln_b", [NL, D])
    ws_d = din("sgu_w_s", [NL, SG, 128, 128])
    bs_d = din("sgu_b_s", [NL, SG * 128])
    tbs = {}
    for nm, shp in (("tb_cos", [128, T]), ("tb_sin", [128, T]), ("tb_kdec", [128, NT * H]),
                    ("tb_odec", [128, NT * H]), ("tb_coef", [128, G * H]), ("tb_oneh", [128, G * H]),
                    ("tb_gT", [128, H]), ("tb_mask", [128, H * 128]), ("tb_cmask", [128, 128]),
                    ("tb_ident", [128, 128]), ("tb_pm", [128, 128]), ("tb_ones", [128, 128])):
        tbs[nm] = din(nm, shp)
    y_out = nc.dram_tensor("y", [T, D], F32, kind="ExternalOutput").ap()

    XT = dscr("XT", [KC, 128, T], F32)
    ZQ = dscr("ZQ", [H, 128, T], BF16)
    ZK = dscr("ZK", [H, 128, T], BF16)
    ZVv = dscr("ZVv", [128, NT, c.RV], BF16)
    ZG = dscr("ZG", [128, NT, c.RV], BF16)
    ZU = dscr("ZU", [KC, 128, T], BF16)
    ZS = dscr("ZS", [128, NT, D], BF16)
    GA = dscr("GA", [KC, 128, T], BF16)
    GB = dscr("GB", [KC, 128, T], BF16)
    GT = dscr("GT", [RC, 128, T], BF16)
    ST = dscr("ST", [KC, 128, T], BF16)
    AT = dscr("AT", [FC, 128, T], BF16)
    EXI = dscr("EXI", [G * H * 128, 256], F32)
    EXO = dscr("EXO", [G * H * 128, 256], F32)

    WBC = 256
    ARENA = 65536
    with ExitStack() as st:
        fw = Fw(nc, st)
        sb = lambda name, shape, dt: st.enter_context(nc.sbuf_tensor(name, list(shape), dt))
        arena = sb("arena", [128, ARENA], BF16)
        KH = FC // 2
        assert FC % 2 == 0
        wb = [sb("wb%d" % i, [128, max(KC, KH) * WBC], BF16) for i in range(2)]
        WCTX = dict(slots=[w[:] for w in wb], wbc=WBC, key="wb")
        kdec_t = sb("kdec_t", [128, NT * H], F32)
        odec_t = sb("odec_t", [128, NT * H], F32)
        coef_t = sb("coef_t", [128, G * H], F32)
        oneh_t = sb("oneh_t", [128, G * H], F32)
        gT_t = sb("gT_t", [128, H], F32)
        mask_t = sb("mask_t", [128, H * 128], F32)
        cmask_t = sb("cmask_t", [128, 128], F32)
        ident_b = sb("ident_b", [128, 128], BF16)
        ident_f = sb("ident_f", [128, 128], F32)
        pm_b = sb("pm_b", [128, 128], BF16)
        ones_b = sb("ones_b", [128, 128], BF16)
        nw1_t = sb("nw1_t", [128, NL * KC], F32)
        nw2_t = sb("nw2_t", [128, NL * KC], F32)
        nwf_t = sb("nwf_t", [128, KC], F32)
        gnw_t = sb("gnw_t", [128, NL * RC], F32)
        lnw_t = sb("lnw_t", [128, D], BF16)
        lnb_t = sb("lnb_t", [128, D], BF16)
        wsb_t = sb("wsb_t", [128, SG * 128], BF16)
        wT_t = sb("wT_t", [128, SG * 128], BF16)
        bsf_t = sb("bsf_t", [1, SG * 128], F32)
        bsh_t = sb("bsh_t", [1, SG * 128], BF16)
        bsr_t = sb("bsr_t", [1, SG * 128], F32)
        bsl_t = sb("bsl_t", [1, SG * 128], BF16)
        NSB = 4
        stg_b = [sb("stgb%d" % i, [128, 512], BF16) for i in range(NSB)]
        stg_f = [sb("stgf%d" % i, [128, 512], F32) for i in range(3)]
        ld_b = [sb("ldb%d" % i, [128, 512], BF16) for i in range(2)]
        ld_f = [sb("ldf%d" % i, [128, 512], F32) for i in range(2)]
        rstd_t = sb("rstd_t", [128, 512], F32)
        small = sb("small", [128, 256], F32)
        s0_t = sb("s0_t", [128, 256], F32)
        psG = [st.enter_context(nc.psum_tensor("psG%d" % i, [128, 512], F32)) for i in range(4)]
        psT = [st.enter_context(nc.psum_tensor("psT%d" % i, [128, 1024], BF16)) for i in range(2)]
        psX = st.enter_context(nc.psum_tensor("psX", [128, 512], F32))
        psS = st.enter_context(nc.psum_tensor("psS", [128, 512], F32))
        block = st.enter_context(nc.Block())

        rr = {"g": 0, "g3": 0, "sb": 0, "sf": 0, "lb": 0, "lf": 0, "t": 0}

        def nxt(kind, n):
            i = rr[kind]
            rr[kind] = (i + 1) % n
            return i

        def av(off, n):
            return arena[:, off:off + n]

        def avf(off, n):
            return arena[:, off:off + n].bitcast(F32)

        def mm(ps, lhsT, rhs, start, stop):
            return lambda E: E.matmul(ps, lhsT=lhsT, rhs=rhs, start=start, stop=stop)

        def trp(out, in_, ident):
            return lambda E: E.transpose(out=out, in_=in_, identity=ident)

        def actf(out, in_, func, scale=1.0, bias=0.0):
            return lambda E: E.activation(out=out, in_=in_, func=func, bias=bias, scale=scale)

        def tt(out, in0, in1, op):
            return lambda E: E.tensor_tensor(out=out, in0=in0, in1=in1, op=op)

        def stt(out, in0, scalar, in1, op0, op1):
            return lambda E: E.scalar_tensor_tensor(out=out, in0=in0, scalar=scalar, in1=in1, op0=op0, op1=op1)

        def tsc(out, in0, s1, s2, op0, op1):
            return lambda E: E.tensor_scalar(out=out, in0=in0, scalar1=s1, scalar2=s2, op0=op0, op1=op1)

        gid = [0]

        def newgroup():
            gid[0] += 1
            return gid[0]

        def ld(qn, out, in_, reads=(), writes=(), group=None):
            return fw.dma(qn, lambda E: E.dma_start(out=out, in_=in_), reads=reads, writes=writes, group=group)

        ld("pool", ident_b[:], tbs["tb_ident"], writes=["identb"])
        ld("pool", pm_b[:], tbs["tb_pm"], writes=["pmb"])
        ld("pool", ones_b[:], tbs["tb_ones"], writes=["onesb"])
        ld("sp", ident_f[:], tbs["tb_ident"], writes=["identf"])
        ld("sp", kdec_t[:], tbs["tb_kdec"], writes=["kdec"])
        ld("sp", odec_t[:], tbs["tb_odec"], writes=["odec"])
        ld("sp", coef_t[:], tbs["tb_coef"], writes=["coef"])
        ld("sp", oneh_t[:], tbs["tb_oneh"], writes=["oneh"])
        ld("sp", gT_t[:], tbs["tb_gT"], writes=["gT"])
        ld("sp", mask_t[:], tbs["tb_mask"], writes=["mask"])
        ld("sp", cmask_t[:], tbs["tb_cmask"], writes=["cmask"])
        ld("sp", nw1_t[:].rearrange("p (l k) -> p l k", l=NL), nw1_d.rearrange("l p k -> p l k"), writes=["nw1"])
        ld("sp", nw2_t[:].rearrange("p (l k) -> p l k", l=NL), nw2_d.rearrange("l p k -> p l k"), writes=["nw2"])
        ld("sp", gnw_t[:].rearrange("p (l k) -> p l k", l=NL), gnw_d.rearrange("l p k -> p l k"), writes=["gnw"])
        ld("sp", nwf_t[:], nwf_d, writes=["nwf"])
        CONST_KEYS = ["cos", "sin", "identb", "pmb", "onesb", "identf", "kdec", "odec", "coef", "oneh", "gT",
                      "mask", "cmask", "nw1", "nw2", "gnw", "nwf"]

        def ingest():
            XS = 0
            for n in range(NT):
                s = n % 2
                xt = avf(XS + s * 2 * D, 2 * D)
                ld("sp", xt, x_in[n * 128:(n + 1) * 128, :], writes=[("xs", s)])
                for k0 in range(0, KC, 4):
                    gi = nxt("g", 4)
                    ps = psG[gi]
                    fw.pe_group([trp(ps[:, j * 128:(j + 1) * 128], xt[:, (k0 + j) * 128:(k0 + j + 1) * 128], ident_f[:])
                                 for j in range(4)], reads=[("xs", s), "identf"], writes=[("psG", gi)])
                    si = nxt("sf", 3)
                    fw.op("act", actf(stg_f[si][:], ps[:], AF.Copy), reads=[("psG", gi)], writes=[("stgf", si)])
                    ld("pool", XT[k0:k0 + 4, :, n * 128:(n + 1) * 128].rearrange("k p t -> p k t"),
                       stg_f[si][:].rearrange("p (k t) -> p k t", k=4), reads=[("stgf", si)], writes=["XT"], group="g_XT")

        BUFA = 0
        BUFB = KC * T

        def hT(k, t0, n):
            return arena[:, BUFA + k * T + t0: BUFA + k * T + t0 + n]

        def mT(k, t0, n):
            return arena[:, BUFB + k * T + t0: BUFB + k * T + t0 + n]

        def norm_pass(nw_ap_fn, final=False):
            XB = BUFB
            assert XB + 2 * KC * 1024 <= ARENA
            for tb in range(NB):
                xbk = ("xb", tb % 2)
                xb = avf(XB + (tb % 2) * KC * 1024, KC * 1024).rearrange("p (k t) -> p k t", k=KC)
                ld("sp", xb, XT[:, :, tb * 512:(tb + 1) * 512].rearrange("k p t -> p k t"),
                   reads=["XT"], writes=[xbk])
                fns = []
                for k in range(KC):
                    si = nxt("sb", NSB)
                    fw.op("act", actf(stg_b[si][:], xb[:, k, :], AF.Square), reads=[xbk], writes=[("stgb", si)])
                    fw.pe_group([mm(psX[:], ones_b[:], stg_b[si][:], k == 0, k == KC - 1)],
                                reads=[("stgb", si), "onesb"], writes=["psX"])
                fw.op("act", actf(rstd_t[:], psX[:], AF.Sqrt, scale=1.0 / D, bias=EPS), reads=["psX"], writes=["rstd"])
                fw.op("dve", lambda E: E.reciprocal(out=rstd_t[:], in_=rstd_t[:]), reads=["rstd"], writes=["rstd"])
                if not final:
                    for k in range(KC):
                        fw.op("dve", stt(hT(k, tb * 512, 512), xb[:, k, :], nw_ap_fn(k), rstd_t[:], ALU.mult, ALU.mult),
                              reads=[xbk, "rstd", "nw1", "nw2"], writes=["bufA"])
                else:
                    for k in range(KC):
                        fw.op("dve", stt(xb[:, k, :], xb[:, k, :], nw_ap_fn(k), rstd_t[:], ALU.mult, ALU.mult),
                              reads=[xbk, "rstd", "nwf"], writes=[xbk])
                    for tl in range(4):
                        n = tb * 4 + tl
                        s = n % 2
                        yt = avf(BUFA + s * 2 * D, 2 * D)
                        for k0 in range(0, KC, 4):
                            gi = nxt("g", 4)
                            ps = psG[gi]
                            fw.pe_group([trp(ps[:, j * 128:(j + 1) * 128], xb[:, k0 + j, tl * 128:(tl + 1) * 128], ident_f[:])
                                         for j in range(4)], reads=[xbk, "identf"], writes=[("psG", gi)])
                            fw.op("act", actf(yt[:, k0 * 128:(k0 + 4) * 128], ps[:], AF.Copy),
                                  reads=[("psG", gi)], writes=[("yt", s)])
                        ld("pool", y_out[n * 128:(n + 1) * 128, :], yt, reads=[("yt", s)], writes=["y"], group="g_y")

        pre = {"done": None}

        def prefetch_w(wload, tag):
            wload(0, 0)
            pre["done"] = tag

        def gemm(Xfn, kc, nblk, wload, units_of_block, wctx=None, tag=None):
            wctx = wctx or WCTX
            deferred = []
            if tag is not None and pre["done"] == tag:
                pre["done"] = None
            else:
                wload(0, 0)
            for blk in range(nblk):
                slot = blk % 2
                if blk + 1 < nblk:
                    wload(blk + 1, (blk + 1) % 2)
                for u in units_of_block(blk, slot):
                    gi = nxt("g", 4)
                    ps = psG[gi][:, 0:u["n"]]
                    fw.pe_group([mm(ps, u["lhs"](k), u["rhs"](k), k == 0, k == kc - 1) for k in range(kc)],
                                reads=[(wctx["key"], slot), "bufA", "bufB"], writes=[("psG", gi)])
                    for dfn in deferred:
                        dfn()
                    deferred = []
                    d = u["epi"](ps, gi)
                    if d is not None:
                        deferred.append(d)
            for dfn in deferred:
                dfn()

        def wslice(slot, k, c0, n, wctx=None):
            wctx = wctx or WCTX
            return wctx["slots"][slot][:, k * wctx["wbc"] + c0: k * wctx["wbc"] + c0 + n]

        def wload_cols(Wl, kc_total, col_of_blk, ncols=None, dst_off=0, wctx=None):
            wctx = wctx or WCTX
            wbc = wctx["wbc"]
            ncols = ncols or wbc

            def f(blk, slot):
                col = col_of_blk(blk)
                src = Wl.rearrange("(k p) n -> p k n", p=128)
                dst = wctx["slots"][slot].rearrange("p (k n) -> p k n", n=wbc)
                step = 4
                grp = newgroup()
                for k0 in range(0, kc_total, step):
                    k1 = min(kc_total, k0 + step)
                    ld("pool", dst[:, k0:k1, dst_off:dst_off + ncols], src[:, k0:k1, col:col + ncols],
                       writes=[(wctx["key"], slot)], group=grp)
            return f

        def phase1(l):
            Wl = w_in[l]
            W1 = 512 if c.QK % 512 == 0 else 256
            W1S = KC * W1
            w1ctx = dict(slots=[arena[:, BUFB + i * W1S: BUFB + (i + 1) * W1S] for i in range(2)], wbc=W1, key="wb1")
            CS = BUFB + 2 * W1S
            assert CS + 2 * T <= ARENA
            cos_v = arena[:, CS:CS + T]
            sin_v = arena[:, CS + T:CS + 2 * T]
            ld("pool", cos_v, tbs["tb_cos"], writes=["cos"])
            ld("pool", sin_v, tbs["tb_sin"], writes=["sin"])
            nblk = c.INC // W1

            def kind_of(col):
                if col < c.ok:
                    return "q"
                if col < c.ov:
                    return "k"
                if col < c.og:
                    return "v"
                if col < c.osu:
                    return "g"
                if col < c.osv:
                    return "su"
                if col < c.oga:
                    return "sv"
                if col < c.ogb:
                    return "ga"
                return "gb"

            def units(blk, slot):
                col = blk * W1
                kind = kind_of(col)
                us = []
                if kind in ("v", "g", "sv"):
                    base = {"v": c.ov, "g": c.og, "sv": c.osv}[kind]
                    dst = {"v": ZVv, "g": ZG, "sv": ZS}[kind]
                    func = {"v": AF.Copy, "g": AF.Silu, "sv": AF.Gelu}[kind]
                    for n in range(NT):
                        def epi(ps, gi, n=n, dst=dst, func=func, c0=col - base):
                            si = nxt("sb", NSB)
                            fw.op("act", actf(stg_b[si][:, 0:W1], ps, func), reads=[("psG", gi)], writes=[("stgb", si)])
                            ld("pool", dst[:, n, c0:c0 + W1], stg_b[si][:, 0:W1], reads=[("stgb", si)], writes=["Z"], group="g_Z")
                            return None
                        us.append(dict(lhs=lambda k, n=n: hT(k, n * 128, 128),
                                       rhs=lambda k, slot=slot: wslice(slot, k, 0, W1, w1ctx), n=W1, epi=epi))
                else:
                    base = {"q": c.oq, "k": c.ok, "su": c.osu, "ga": c.oga, "gb": c.ogb}[kind]
                    for m in range(W1 // 128):
                        ch = (col - base) // 128 + m
                        for tb in range(NB):
                            if kind in ("q", "k"):
                                dst = ZQ if kind == "q" else ZK

                                def epi(ps, gi, ch=ch, tb=tb, dst=dst):
                                    si = nxt("sb", NSB)
                                    zb = stg_b[si]
                                    fw.op("act", actf(zb[:], ps, AF.Copy), reads=[("psG", gi)], writes=[("stgb", si)])

                                    def later():
                                        fw.pe_group([mm(psX[:], pm_b[:], zb[:], True, True)],
                                                    reads=[("stgb", si), "pmb"], writes=["psX"])
                                        f1 = nxt("sf", 3)
                                        fw.op("dve", tt(stg_f[f1][:], zb[:], cos_v[:, tb * 512:(tb + 1) * 512], ALU.mult),
                                              reads=[("stgb", si), "cos"], writes=[("stgf", f1)])
                                        f2 = nxt("sf", 3)
                                        fw.op("dve", tt(stg_f[f2][:], psX[:], sin_v[:, tb * 512:(tb + 1) * 512], ALU.mult),
                                              reads=["psX", "sin"], writes=[("stgf", f2)])
                                        so = nxt("sb", NSB)
                                        fw.op("dve", tt(stg_b[so][:], stg_f[f1][:], stg_f[f2][:], ALU.add),
                                              reads=[("stgf", f1), ("stgf", f2)], writes=[("stgb", so)])
                                        ld("pool", dst[ch, :, tb * 512:(tb + 1) * 512], stg_b[so][:],
                                           reads=[("stgb", so)], writes=["Z"], group="g_Z")
                                    return later
                            else:
                                dst = {"su": ZU, "ga": GA, "gb": GB}[kind]
                                func = AF.Gelu if kind == "su" else AF.Sigmoid

                                def epi(ps, gi, ch=ch, tb=tb, dst=dst, func=func):
                                    si = nxt("sb", NSB)
                                    fw.op("act", actf(stg_b[si][:], ps, func), reads=[("psG", gi)], writes=[("stgb", si)])
                                    ld("pool", dst[ch, :, tb * 512:(tb + 1) * 512], stg_b[si][:],
                                       reads=[("stgb", si)], writes=["Z"], group="g_Z")
                                    return None
                            us.append(dict(lhs=lambda k, slot=slot, m=m: wslice(slot, k, m * 128, 128, w1ctx),
                                           rhs=lambda k, tb=tb: hT(k, tb * 512, 512), n=512, epi=epi))
                return us

            gemm(hT, KC, nblk, wload_cols(Wl, KC, lambda blk: blk * W1, wctx=w1ctx), units, wctx=w1ctx)

        R_SLOT = 2 * T + 2 * NT * 256
        R0 = 0
        RTMP = R0 + 2 * R_SLOT
        RTMP_SZ = 14336
        PA_END = 512 + 4 * G * 256
        O_OFF = RTMP + RTMP_SZ
        GTH = O_OFF + 2 * 2 * NT * 256
        assert GTH + 4 * T <= ARENA, "arena overflow (retention)"
        S_OFF = RTMP + PA_END
        assert S_OFF + 4 * D + 2 * D + KC * 512 * 2 <= ARENA, "arena overflow (sgu)"

        def load_layer_small(l):
            ld("pool", lnw_t[:], lnw_d[l:l + 1, :].partition_broadcast(128), writes=["lnw"])
            ld("pool", lnb_t[:], lnb_d[l:l + 1, :].partition_broadcast(128), writes=["lnb"])
            ld("pool", wsb_t[:].rearrange("p (g j) -> p g j", g=SG), ws_d[l].rearrange("g i j -> i g j"), writes=["wsb"])
            ld("sp", bsf_t[:], bs_d[l:l + 1, :], writes=["bsf"])
            for g in range(SG):
                ti = nxt("t", 2)
                fw.pe_group([trp(psT[ti][:, 0:128], wsb_t[:, g * 128:(g + 1) * 128], ident_b[:])],
                            reads=["wsb", "identb"], writes=[("psT", ti)])
                fw.op("dve", tt(wT_t[:, g * 128:(g + 1) * 128], psT[ti][:, 0:128], cmask_t[:], ALU.mult),
                      reads=[("psT", ti), "cmask"], writes=["wT"])
            fw.op("dve", lambda E: E.tensor_copy(out=bsh_t[:], in_=bsf_t[:]), reads=["bsf"], writes=["bsh"])
            fw.op("dve", tt(bsr_t[:], bsf_t[:], bsh_t[:], ALU.subtract), reads=["bsf", "bsh"], writes=["bsr"])
            fw.op("dve", lambda E: E.tensor_copy(out=bsl_t[:], in_=bsr_t[:]), reads=["bsr"], writes=["bsl"])

        def head_views(s):
            b = R0 + s * R_SLOT
            qT = av(b, T)
            kT = av(b + T, T)
            v = av(b + 2 * T, NT * 256)
            sg = av(b + 2 * T + NT * 256, NT * 256)
            return qT, kT, v, sg

        def kt_make(h, n, kT, s, pi=0):
            ti = nxt("t", 2)
            fw.pe_group([trp(psT[ti][:, 0:128], kT[:, n * 128:(n + 1) * 128], ident_b[:])],
                        reads=[("hk", s), "identb"], writes=[("psT", ti)])
            kk = pi * 2 + n % 2
            kt = av(RTMP + kk * 128, 128)
            fw.op("act", actf(kt, psT[ti][:, 0:128], AF.Copy, scale=kdec_t[:, n * H + h:n * H + h + 1]),
                  reads=[("psT", ti), "kdec"], writes=[("kt", kk)])
            return kt, ("kt", kk)

        def pass_a():
            exg = newgroup()
            for h in range(H):
                s = h % 2
                qT, kT, v, sg = head_views(s)
                ld("sp", kT, ZK[h], writes=[("hk", s)])
                ld("sp", v.rearrange("p (n e) -> p n e", n=NT), ZVv[:, :, h * 256:(h + 1) * 256], writes=[("hv", s)])
                acc = psG[h % 2][:, 0:256]
                import os as _os
                PA = int(_os.environ.get("K_PA", "9"))
                for n in range(NT):
                    kt, ktk = kt_make(h, n, kT, s)
                    if PA >= 2:
                        fw.pe_group([mm(acc, kt, v[:, n * 256:(n + 1) * 256], n == 0, n == NT - 1)],
                                    reads=[ktk, ("hv", s)], writes=[("psG", h % 2)])
                if PA < 3:
                    continue
                ex = avf(RTMP + 512 + (h % 2) * 2 * G * 256, 2 * G * 256)
                for sl in range(G):
                    fw.op("dve", lambda E, ex=ex, sl=sl, acc=acc, h=h: E.tensor_scalar_mul(
                        out=ex[:, sl * 256:(sl + 1) * 256], in0=acc, scalar1=oneh_t[:, sl * H + h: sl * H + h + 1]),
                        reads=[("psG", h % 2), "oneh"], writes=[("ex", h % 2)])
                if PA < 4:
                    continue
                ld("pool", EXI.rearrange("(g h p) e -> p g h e", g=G, h=H)[:, :, h, :],
                   ex.rearrange("p (g e) -> p g e", g=G), reads=[("ex", h % 2)], writes=["EXI"], group=exg)
            import os as _os
            if _os.environ.get("K_NOCOLL"):
                return
            if _os.environ.get("K_COLL8"):
                rgs = [list(range(c.NCORES))]
            else:
                rgs = [list(range(b * G, (b + 1) * G)) for b in range(c.BATCH)]
            fw.coll(lambda E: E.collective_compute(
                "AllReduce", ALU.add, replica_groups=rgs,
                ins=[EXI.opt()], outs=[EXO.opt()]), reads=["EXI"], writes=["EXO"])

        def sgu(l):
            ZV_O = S_OFF
            TMP_O = ZV_O + 4 * D
            ZU_O = TMP_O + 2 * D
            OUT_O = ZU_O + KC * 512
            for tb in range(NB):
                zv = av(ZV_O, 4 * D)
                vn = zv
                zu = av(ZU_O, KC * 512).rearrange("p (k t) -> p k t", k=KC)
                out = av(OUT_O, KC * 512).rearrange("p (k t) -> p k t", k=KC)
                ld("sp", zv.rearrange("p (n d) -> p n d", n=4), ZS[:, tb * 4:(tb + 1) * 4, :], writes=[("zv", 0), ("zv", 1), ("zv", 2), ("zv", 3)])
                ld("sp", zu, ZU[:, :, tb * 512:(tb + 1) * 512].rearrange("k p t -> p k t"), writes=["zu"])
                mv = small[:, 0:8]
                nchk = max(1, D // 512)
                for tl in range(4):
                    stats = small[:, 16 + tl * 6 * nchk: 16 + (tl + 1) * 6 * nchk]
                    for cc in range(nchk):
                        w = min(512, D)
                        fw.op("dve", lambda E, tl=tl, cc=cc, stats=stats, w=w: E.bn_stats(
                            out=stats[:, cc * 6:(cc + 1) * 6], in_=zv[:, tl * D + cc * w: tl * D + (cc + 1) * w]),
                            reads=[("zv", tl)], writes=[("sgst", tl)])
                    fw.op("dve", lambda E, tl=tl, stats=stats: E.bn_aggr(out=mv[:, tl * 2:tl * 2 + 2], in_=stats.rearrange("p (c s) -> p c s", s=6)),
                          reads=[("sgst", tl)], writes=["sgmv"])
                rs = small[:, 8:12]
                mv3 = mv.rearrange("p (t two) -> p t two", two=2)
                fw.op("act", actf(rs, mv3[:, :, 1], AF.Sqrt, bias=EPS), reads=["sgmv"], writes=["sgrs"])
                fw.op("dve", lambda E: E.reciprocal(out=rs, in_=rs), reads=["sgrs"], writes=["sgrs"])
                for tl in range(4):
                    tmp = avf(TMP_O, 2 * D)
                    fw.op("dve", stt(tmp, zv[:, tl * D:(tl + 1) * D], mv[:, tl * 2:tl * 2 + 1], lnw_t[:],
                                     ALU.subtract, ALU.mult), reads=[("zv", tl), "sgmv", "lnw"], writes=["sgtmp"])
                    fw.op("dve", stt(vn[:, tl * D:(tl + 1) * D], tmp, rs[:, tl:tl + 1], lnb_t[:], ALU.mult, ALU.add),
                          reads=["sgtmp", "sgrs", "lnb"], writes=[("zv", tl)])
                for tl in range(4):
                    for k0 in range(0, KC, 4):
                        gi = nxt("g", 4)
                        ps = psG[gi]
                        fns = []
                        for j in range(4):
                            k = k0 + j
                            g = k // 2
                            o = ps[:, j * 128:(j + 1) * 128]
                            fns.append(mm(o, vn[:, tl * D + k * 128: tl * D + (k + 1) * 128], wT_t[:, g * 128:(g + 1) * 128], True, False))
                            fns.append(mm(o, ones_b[0:1, :], bsh_t[0:1, g * 128:(g + 1) * 128], False, False))
                            fns.append(mm(o, ones_b[0:1, :], bsl_t[0:1, g * 128:(g + 1) * 128], False, True))
                        fw.pe_group(fns, reads=[("zv", tl), "wT", "bsh", "bsl", "onesb"], writes=[("psG", gi)])
                        fw.op("dve", tt(out[:, k0:k0 + 4, tl * 128:(tl + 1) * 128],
                                        ps[:].rearrange("p (k t) -> p k t", k=4),
                                        zu[:, k0:k0 + 4, tl * 128:(tl + 1) * 128], ALU.mult),
                              reads=[("psG", gi), "zu"], writes=["sgout"])
                ld("pool", ST[:, :, tb * 512:(tb + 1) * 512].rearrange("k p t -> p k t"), out, reads=["sgout"], writes=["ST"], group="g_ST")

        def pass_b(l):
            SIN_HI = RTMP + 512
            SIN_LO = SIN_HI + H * 256
            EXL = SIN_LO + H * 256
            for h in range(H):
                e = h % 2
                exl = avf(EXL + e * 2 * G * 256, 2 * G * 256)
                ld("sp", exl.rearrange("p (g e) -> p g e", g=G),
                   EXO.rearrange("(g h p) e -> p g h e", g=G, h=H)[:, :, h, :], reads=["EXO"], writes=[("exl", e)])
                s0 = s0_t[:]
                fw.op("dve", lambda E, exl=exl, h=h, s0=s0: E.tensor_scalar_mul(out=s0, in0=exl[:, 0:256], scalar1=coef_t[:, h:h + 1]),
                      reads=[("exl", e), "coef"], writes=["s0"])
                for sl in range(1, G):
                    fw.op("dve", stt(s0, exl[:, sl * 256:(sl + 1) * 256], coef_t[:, sl * H + h: sl * H + h + 1], s0,
                                     ALU.mult, ALU.add), reads=[("exl", e), "coef", "s0"], writes=["s0"])
                hi = av(SIN_HI + h * 256, 256)
                lo = av(SIN_LO + h * 256, 256)
                fw.op("dve", lambda E, hi=hi, s0=s0: E.tensor_copy(out=hi, in_=s0), reads=["s0"], writes=["sinhi"])
                fw.op("dve", tt(s0, s0, hi, ALU.subtract), reads=["s0", "sinhi"], writes=["s0"])
                fw.op("dve", lambda E, lo=lo, s0=s0: E.tensor_copy(out=lo, in_=s0), reads=["s0"], writes=["sinlo"])
            SB_O = EXL + 4 * G * 256
            SC_O = SB_O + 1024
            TF_O = SC_O + 512
            U_O = TF_O + 2048
            assert U_O + 1024 <= RTMP + RTMP_SZ, "RTMP overflow"

            def head_gen(h, pi):
                s = pi
                qT, kT, v, sg = head_views(s)
                ld("sp", qT, ZQ[h], writes=[("hq", s)])
                ld("sp", kT, ZK[h], writes=[("hk", s)])
                ld("sp", v.rearrange("p (n e) -> p n e", n=NT), ZVv[:, :, h * 256:(h + 1) * 256], writes=[("hv", s)])
                ld("sp", sg.rearrange("p (n e) -> p n e", n=NT), ZG[:, :, h * 256:(h + 1) * 256], writes=[("hg", s)])
                if pi == 0:
                    acc, acck = psS[:, 0:256], ("psS", 0)
                else:
                    acc, acck = psG[3][:, 0:256], ("psG", 3)
                fw.pe_group([mm(acc, ident_b[:], av(SIN_HI + h * 256, 256), True, False),
                             mm(acc, ident_b[:], av(SIN_LO + h * 256, 256), False, False)],
                            reads=["sinhi", "sinlo", "identb"], writes=[acck])
                o_all = avf(O_OFF + pi * 2 * NT * 256, 2 * NT * 256)
                mv = small[:, pi * 32: pi * 32 + 2 * NT]
                yield
                for n in range(NT):
                    sbi = pi * 2 + n % 2
                    Sb = av(SB_O + sbi * 256, 256)
                    fw.op("act", actf(Sb, acc, AF.Copy), reads=[acck], writes=[("Sb", sbi)])
                    kt, ktk = kt_make(h, n, kT, s, pi)
                    fw.pe_group([mm(psX[:, 0:128], kT[:, n * 128:(n + 1) * 128], qT[:, n * 128:(n + 1) * 128], True, True)],
                                reads=[("hk", s), ("hq", s)], writes=["psX"])
                    sc = av(SC_O + sbi * 128, 128)
                    fw.op("dve", stt(sc, psX[:, 0:128], kdec_t[:, n * H + h:n * H + h + 1],
                                     mask_t[:, h * 128:(h + 1) * 128], ALU.mult, ALU.mult),
                          reads=["psX", "kdec", "mask"], writes=[("sc", sbi)])
                    yield
                    gi = nxt("g3", 3)
                    po = psG[gi][:, 0:256]
                    fw.pe_group([mm(po, sc, v[:, n * 256:(n + 1) * 256], True, False),
                                 mm(po, qT[:, n * 128:(n + 1) * 128], Sb, False, True)],
                                reads=[("sc", sbi), ("Sb", sbi), ("hv", s), ("hq", s)], writes=[("psG", gi)])
                    fw.pe_group([mm(acc, kt, v[:, n * 256:(n + 1) * 256], False, n == NT - 1)],
                                reads=[ktk, ("hv", s)], writes=[acck])
                    fw.op("act", actf(o_all[:, n * 256:(n + 1) * 256], po, AF.Copy, scale=odec_t[:, n * H + h:n * H + h + 1]),
                          reads=[("psG", gi), "odec"], writes=[("o", pi, n)])
                    st6 = small[:, 64 + sbi * 6: 64 + sbi * 6 + 6]
                    fw.op("dve", lambda E, n=n, st6=st6, o_all=o_all: E.bn_stats(out=st6, in_=o_all[:, n * 256:(n + 1) * 256]),
                          reads=[("o", pi, n)], writes=[("st6", sbi)])
                    fw.op("dve", lambda E, n=n, st6=st6, mv=mv: E.bn_aggr(out=mv[:, 2 * n:2 * n + 2], in_=st6),
                          reads=[("st6", sbi)], writes=[("rmv", pi)])
                    yield
                rs = small[:, 128 + pi * 32:128 + pi * 32 + NT]
                fw.op("act", actf(rs, mv.rearrange("p (t two) -> p t two", two=2)[:, :, 1], AF.Sqrt, bias=EPS),
                      reads=[("rmv", pi)], writes=[("rrs", pi)])
                fw.op("dve", lambda E, rs=rs: E.reciprocal(out=rs, in_=rs), reads=[("rrs", pi)], writes=[("rrs", pi)])
                gth = av(GTH + s * 2 * T, 2 * T)
                yield
                for n in range(NT):
                    ti2 = pi * 2 + n % 2
                    tf = avf(TF_O + ti2 * 512, 512)
                    fw.op("dve", tsc(tf, o_all[:, n * 256:(n + 1) * 256], mv[:, 2 * n:2 * n + 1], rs[:, n:n + 1],
                                     ALU.subtract, ALU.mult), reads=[("o", pi, n), ("rmv", pi), ("rrs", pi)], writes=[("tf", ti2)])
                    u = av(U_O + ti2 * 256, 256)
                    fw.op("dve", tt(u, tf, sg[:, n * 256:(n + 1) * 256], ALU.mult),
                          reads=[("tf", ti2), ("hg", s)], writes=[("u", ti2)])
                    ti = nxt("t", 2)
                    fw.pe_group([trp(psT[ti][:, 0:128], u[:, 0:128], ident_b[:]),
                                 trp(psT[ti][:, 128:256], u[:, 128:256], ident_b[:])],
                                reads=[("u", ti2), "identb"], writes=[("psT", ti)])
                    yield
                    for e2 in range(2):
                        fw.op("act", actf(gth[:, e2 * T + n * 128: e2 * T + (n + 1) * 128], psT[ti][:, e2 * 128:(e2 + 1) * 128],
                                          AF.Copy, scale=gnw_t[:, l * RC + h * 2 + e2: l * RC + h * 2 + e2 + 1]),
                              reads=[("psT", ti), "gnw"], writes=[("gth", s)])
                    yield
                ld("pool", GT[h * 2:h * 2 + 2].rearrange("k p t -> p k t"), gth.rearrange("p (k t) -> p k t", k=2),
                   reads=[("gth", s)], writes=["GT"], group="g_GT")

            for h0 in range(0, H, 2):
                gens = [head_gen(h0 + pi, pi) for pi in range(min(2, H - h0))]
                alive = list(gens)
                while alive:
                    for g_ in list(alive):
                        try:
                            next(g_)
                        except StopIteration:
                            alive.remove(g_)

        def load_bufA(src, nch):
            grp = newgroup()
            for k0 in range(0, nch, 4):
                k1 = min(nch, k0 + 4)
                ld("sp", arena[:, BUFA + k0 * T: BUFA + k1 * T].rearrange("p (k t) -> p k t", t=T),
                   src[k0:k1].rearrange("k p t -> p k t"), writes=["bufA"], group=grp)
            assert nch * T <= ARENA

        def phase3(l):
            load_bufA(GT, RC)

            def mk_units(gate, first):
                def units(blk, slot):
                    us = []
                    for m in range(WBC // 128):
                        ch = blk * (WBC // 128) + m
                        for tb in range(NB):
                            def epi(ps, gi, ch=ch, tb=tb):
                                li = nxt("lb", 2)
                                ld("sp", ld_b[li][:], gate[ch, :, tb * 512:(tb + 1) * 512], writes=[("ldb", li)])
                                if first:
                                    fw.op("dve", tt(mT(ch, tb * 512, 512), ps, ld_b[li][:], ALU.mult),
                                          reads=[("psG", gi), ("ldb", li)], writes=[("mT", ch, tb)])
                                else:
                                    fi = nxt("sf", 3)
                                    fw.op("dve", tt(stg_f[fi][:], ps, ld_b[li][:], ALU.mult),
                                          reads=[("psG", gi), ("ldb", li)], writes=[("stgf", fi)])
                                    fw.op("dve", tt(mT(ch, tb * 512, 512), mT(ch, tb * 512, 512), stg_f[fi][:], ALU.add),
                                          reads=[("stgf", fi), ("mT", ch, tb)], writes=[("mT", ch, tb)])
                                return None
                            us.append(dict(lhs=lambda k, slot=slot, m=m: wslice(slot, k, m * 128, 128),
                                           rhs=lambda k, tb=tb: hT(k, tb * 512, 512), n=512, epi=epi))
                    return us
                return units

            gemm(hT, RC, D // WBC, wload_cols(ret_proj[l], RC, lambda blk: blk * WBC), mk_units(GA, True), tag=("p3a", l))
            prefetch_w(wload_cols(sgu_proj[l], KC, lambda blk: blk * WBC), ("p3b", l))
            fw.barrier()
            load_bufA(ST, KC)
            gemm(hT, KC, D // WBC, wload_cols(sgu_proj[l], KC, lambda blk: blk * WBC), mk_units(GB, False), tag=("p3b", l))
            prefetch_w(wload_cols(w_out[l], KC, lambda blk: blk * WBC), ("p3c", l))
            fw.barrier()
            resid_gemm(mT, KC, w_out[l], tag=("p3c", l))
            prefetch_w(wl4(l), ("p4", l))

        def resid_gemm(Xfn, kc, Wl, tag=None):
            def units(blk, slot):
                us = []
                for m in range(WBC // 128):
                    ch = blk * (WBC // 128) + m
                    for tb in range(NB):
                        def epi(ps, gi, ch=ch, tb=tb):
                            li = nxt("lf", 2)
                            ld("sp", ld_f[li][:], XT[ch, :, tb * 512:(tb + 1) * 512], reads=[("XT", ch, tb)], writes=[("ldf", li)])
                            fi = nxt("sf", 3)
                            fw.op("dve", tt(stg_f[fi][:], ps, ld_f[li][:], ALU.add),
                                  reads=[("psG", gi), ("ldf", li)], writes=[("stgf", fi)])
                            ld("pool", XT[ch, :, tb * 512:(tb + 1) * 512], stg_f[fi][:], reads=[("stgf", fi)],
                               writes=[("XT", ch, tb)])
                            return None
                        us.append(dict(lhs=lambda k, slot=slot, m=m: wslice(slot, k, m * 128, 128),
                                       rhs=lambda k, tb=tb: Xfn(k, tb * 512, 512), n=512, epi=epi))
                return us
            gemm(Xfn, kc, D // WBC, wload_cols(Wl, kc, lambda blk: blk * WBC), units, tag=tag)

        def wl4(l):
            Wl = w_ffn_in[l]
            HB = WBC // 2

            return wload_cols(Wl, KC, lambda b: b * WBC)

        def phase4(l):
            HB = WBC // 2
            nblk = DFF // HB
            wl = wl4(l)

            def units(blk, slot):
                us = []
                for m in range(HB // 128):
                    ch = blk * (HB // 128) + m
                    for tb in range(NB):
                        hold = {}

                        def epi_a(ps, gi, hold=hold):
                            fi = nxt("sf", 3)
                            fw.op("act", actf(stg_f[fi][:], ps, AF.Silu), reads=[("psG", gi)], writes=[("stgf", fi)])
                            hold["fi"] = fi
                            return None

                        def epi_c(ps, gi, hold=hold, ch=ch, tb=tb):
                            fi = hold["fi"]
                            si = nxt("sb", NSB)
                            fw.op("dve", tt(stg_b[si][:], ps, stg_f[fi][:], ALU.mult),
                                  reads=[("psG", gi), ("stgf", fi)], writes=[("stgb", si)])
                            ld("pool", AT[ch, :, tb * 512:(tb + 1) * 512], stg_b[si][:], reads=[("stgb", si)], writes=["AT"], group="g_AT")
                            return None
                        us.append(dict(lhs=lambda k, slot=slot, m=m: wslice(slot, k, m * 128, 128),
                                       rhs=lambda k, tb=tb: hT(k, tb * 512, 512), n=512, epi=epi_a))
                        us.append(dict(lhs=lambda k, slot=slot, m=m: wslice(slot, k, HB + m * 128, 128),
                                       rhs=lambda k, tb=tb: hT(k, tb * 512, 512), n=512, epi=epi_c))
                return us
            gemm(hT, KC, nblk, wl, units, tag=("p4", l))
            prefetch_w(wload_cols(w_ffn_out[l][0:KH * 128, :], KH, lambda blk: blk * WBC), ("p5", l, 0))

        def phase5(l):
            Wl = w_ffn_out[l]
            for half in range(2):
                load_bufA(AT[half * KH:(half + 1) * KH], KH)
                resid_gemm(hT, KH, Wl[half * KH * 128:(half + 1) * KH * 128, :], tag=("p5", l, half))
                if half == 0:
                    prefetch_w(wload_cols(Wl[KH * 128:2 * KH * 128, :], KH, lambda blk: blk * WBC), ("p5", l, 1))

        import os as _os
        STOP = int(_os.environ.get("K_STOP", "99"))

        def program():
            ingest()
            fw.barrier()
            if STOP <= 1:
                return
            for l in range(NL):
                load_layer_small(l)
                norm_pass(lambda k, l=l: nw1_t[:, l * KC + k: l * KC + k + 1])
                fw.barrier()
                if STOP <= 2:
                    return
                phase1(l)
                fw.barrier()
                if STOP <= 3:
                    return
                pass_a()
                if STOP <= 4:
                    return
                sgu(l)
                fw.barrier()
                if STOP <= 5:
                    return
                prefetch_w(wload_cols(ret_proj[l], RC, lambda blk: blk * WBC), ("p3a", l))
                pass_b(l)
                fw.barrier()
                if STOP <= 6:
                    return
                phase3(l)
                fw.barrier()
                if STOP <= 7:
                    return
                norm_pass(lambda k, l=l: nw2_t[:, l * KC + k: l * KC + k + 1])
                fw.barrier()
                if STOP <= 8:
                    return
                phase4(l)
                fw.barrier()
                if STOP <= 9:
                    return
                phase5(l)
                fw.barrier()
                if STOP <= 10:
                    return
            norm_pass(lambda k: nwf_t[:, k:k + 1], final=True)

        program()
        fw.finish()
        fw.run(block)
    return nc


def _run(cfg, x, norm_mix_w, w_in, ret_gn_w, ret_proj, sgu_ln_w, sgu_ln_b, sgu_w_s, sgu_b_s, sgu_proj, w_out,
         norm_ffn_w, w_ffn_in, w_ffn_out, final_norm_w, trace=False):
    c = cfg
    f = lambda a: np.ascontiguousarray(np.asarray(a, dtype=np.float32))
    x = f(x).reshape(c.BATCH * c.SEQ, c.D)
    shared = {
        "w_in": f(w_in), "ret_proj": f(ret_proj), "sgu_proj": f(sgu_proj), "w_out": f(w_out),
        "w_ffn_in": f(f(w_ffn_in).reshape(c.NL, c.D, 2, c.FC, 128).transpose(0, 1, 3, 2, 4).reshape(c.NL, c.D, 2 * c.DFF)),
        "w_ffn_out": f(w_ffn_out),
        "norm_mix_w": f(f(norm_mix_w).reshape(c.NL, c.KC, 128).transpose(0, 2, 1)),
        "norm_ffn_w": f(f(norm_ffn_w).reshape(c.NL, c.KC, 128).transpose(0, 2, 1)),
        "final_norm_w": f(f(final_norm_w).reshape(c.KC, 128).transpose(1, 0)),
        "ret_gn_w": f(f(ret_gn_w).reshape(c.NL, c.RC, 128).transpose(0, 2, 1)),
        "sgu_ln_w": f(sgu_ln_w), "sgu_ln_b": f(sgu_ln_b), "sgu_w_s": f(sgu_w_s),
        "sgu_b_s": f(f(sgu_b_s).reshape(c.NL, c.SG * 128)),
    }
    nc = _build(c)
    in_maps = []
    for core in range(c.NCORES):
        m = dict(shared)
        m["x"] = x[core * c.T:(core + 1) * c.T]
        m.update(_tables(c, core))
        in_maps.append(m)
    res = run_bass_kernel_spmd(nc, in_maps, core_ids=list(range(c.NCORES)), **({"trace": True} if trace else {}))
    y = np.concatenate([res.results[i]["y"] for i in range(c.NCORES)], 0)
    return y.reshape(c.BATCH, c.SEQ, c.D).astype(np.float32), res


def kernel(**inputs):
    cfg = Cfg()
    y, _ = _run(cfg, **inputs)
    return y
```

```python
import math
from contextlib import ExitStack

import numpy as np
import concourse.bass as bass
import concourse.mybir as mybir
from concourse.bass_utils import run_bass_kernel_spmd

F32 = mybir.dt.float32
BF16 = mybir.dt.bfloat16
AF = mybir.ActivationFunctionType
ALU = mybir.AluOpType
EPS = 1e-6
ENGS = ("pe", "act", "dve", "pool", "sp")


class Fw:
    def __init__(self, nc, stack, n_dma_slots=8):
        self.nc = nc
        self.q = {e: [] for e in ENGS}
        self.sem = {e: stack.enter_context(nc.semaphore("s_" + e)) for e in ENGS}
        self.cnt = {e: 0 for e in ENGS}
        self.waited = {e: {} for e in ENGS}
        self.last_w = {}
        self.readers = {}
        self.slots = {}
        self.slot_rr = {}
        for qn in ("sp", "pool"):
            self.slots[qn] = [[stack.enter_context(nc.semaphore("d_%s%d" % (qn, i))), 0]
                              for i in range(n_dma_slots)]
            self.slot_rr[qn] = 0
        self.cc_sem = stack.enter_context(nc.semaphore("cc_sem"))
        self.coll_cnt = 0

    def _wait(self, eng, ev):
        if ev is None:
            return
        sem, val, src = ev[0], ev[1], ev[2]
        if src == eng and eng == "pe":
            return
        key = id(sem)
        if self.waited[eng].get(key, 0) >= val:
            return
        self.waited[eng][key] = val
        self.q[eng].append(lambda E, sem=sem, val=val: E.wait_ge(sem, val))

    def _deps(self, eng, reads, writes, group=None):
        for r in reads:
            for ev in self.last_w.get(r, ()):
                self._wait(eng, ev)
        for w in writes:
            for ev in self.last_w.get(w, ()):
                if group is not None and len(ev) > 3 and ev[3] == group:
                    continue
                self._wait(eng, ev)
            for ev in self.readers.get(w, ()):
                self._wait(eng, ev)

    def _record(self, ev, reads, writes):
        for w in writes:
            if ev[2] is None:
                lst = [e for e in self.last_w.get(w, ()) if e[2] is None and e[0] is not ev[0]]
                lst.append(ev)
                self.last_w[w] = lst
            else:
                self.last_w[w] = [ev]
            self.readers[w] = []
        for r in reads:
            lst = self.readers.setdefault(r, [])
            lst[:] = [e for e in lst if e[0] is not ev[0]]
            lst.append(ev)

    def op(self, eng, fn, reads=(), writes=()):
        self._deps(eng, reads, writes)
        self.cnt[eng] += 1
        sem = self.sem[eng]
        ev = (sem, self.cnt[eng], eng)
        self.q[eng].append(lambda E, fn=fn, sem=sem: fn(E).then_inc(sem, 1))
        self._record(ev, reads, writes)
        return ev

    def pe_group(self, fns, reads=(), writes=()):
        self._deps("pe", reads, writes)
        for fn in fns[:-1]:
            self.q["pe"].append(fn)
        self.cnt["pe"] += 1
        sem = self.sem["pe"]
        ev = (sem, self.cnt["pe"], "pe")
        last = fns[-1]
        self.q["pe"].append(lambda E, fn=last, sem=sem: fn(E).then_inc(sem, 1))
        self._record(ev, reads, writes)
        return ev

    def coll(self, fn, reads=(), writes=()):
        self._deps("pool", reads, writes)
        self.coll_cnt += 1
        n = self.coll_cnt
        sem = self.cc_sem
        self.q["pool"].append(lambda E, fn=fn, sem=sem: fn(E).then_inc(sem, 1))
        self.q["pool"].append(lambda E, sem=sem, n=n: E.wait_ge(sem, n))
        self.cnt["pool"] += 1
        psem = self.sem["pool"]
        ev = (psem, self.cnt["pool"], "pool")
        self.q["pool"].append(lambda E, psem=psem: E.sem_inc(psem, 1))
        self._record(ev, reads, writes)
        return ev

    def dma(self, qn, fn, reads=(), writes=(), group=None):
        self._deps(qn, reads, writes, group)
        i = self.slot_rr[qn]
        self.slot_rr[qn] = (i + 1) % len(self.slots[qn])
        slot = self.slots[qn][i]
        sem = slot[0]
        if slot[1] > 0:
            self._wait(qn, (sem, 16 * slot[1], None))
        slot[1] += 1
        ev = (sem, 16 * slot[1], None, group)
        self.q[qn].append(lambda E, fn=fn, sem=sem: fn(E).then_inc(sem, 16))
        self._record(ev, reads, writes)
        return ev

    def _sp_wait_all(self):
        for q2 in self.slots:
            for sem, c in self.slots[q2]:
                if c > 0:
                    self._wait("sp", (sem, 16 * c, None))
        for e in ENGS:
            if self.cnt[e] > 0 and e != "sp":
                self._wait("sp", (self.sem[e], self.cnt[e], e))

    def barrier(self):
        self._sp_wait_all()
        self.cnt["sp"] += 1
        sem = self.sem["sp"]
        ev = (sem, self.cnt["sp"], "sp")
        self.q["sp"].append(lambda E, sem=sem: E.sem_inc(sem, 1))
        for e in ENGS:
            if e != "sp":
                self._wait(e, ev)

    def finish(self):
        self._sp_wait_all()

    def run(self, block):
        q = self.q

        @block.tensor
        def _(E):
            for f in q["pe"]:
                f(E)

        @block.scalar
        def _(E):
            for f in q["act"]:
                f(E)

        @block.vector
        def _(E):
            for f in q["dve"]:
                f(E)

        @block.gpsimd
        def _(E):
            for f in q["pool"]:
                f(E)

        @block.sync
        def _(E):
            for f in q["sp"]:
                f(E)


class Cfg:
    def __init__(self, D=2048, H=8, DFF=5632, T=2048, NL=4, NCORES=8, SEQ=8192, BATCH=2):
        self.D, self.H, self.DFF, self.T, self.NL = D, H, DFF, T, NL
        self.NCORES, self.SEQ, self.BATCH = NCORES, SEQ, BATCH
        self.G = SEQ // T
        assert self.G * BATCH == NCORES
        self.KC = D // 128
        self.QK = H * 128
        self.RV = H * 256
        self.RC = self.RV // 128
        self.SG = D // 256
        self.NT = T // 128
        self.NB = T // 512
        self.FC = DFF // 128
        self.INC = 2 * self.QK + 2 * self.RV + 4 * D
        self.oq, self.ok = 0, self.QK
        self.ov = 2 * self.QK
        self.og = self.ov + self.RV
        self.osu = self.og + self.RV
        self.osv = self.osu + D
        self.oga = self.osv + D
        self.ogb = self.oga + D


def _tables(cfg, core):
    H, T, NT, G = cfg.H, cfg.T, cfg.NT, cfg.G
    rank = core % G
    pos0 = rank * T
    half = 64
    inv = (10000.0 ** (-np.arange(half, dtype=np.float32) / np.float32(half))).astype(np.float32)
    pos = (pos0 + np.arange(T)).astype(np.float32)
    ang = (pos[None, :] * inv[:, None]).astype(np.float32)
    cos = np.cos(ang).astype(np.float32)
    sin = np.sin(ang).astype(np.float32)
    tb_cos = np.concatenate([cos, cos], 0)
    tb_sin = np.concatenate([-sin, sin], 0)
    logg = np.log1p(-(2.0 ** (-5.0 - np.arange(H, dtype=np.float64))))
    p = np.arange(128, dtype=np.float64)
    n = np.arange(NT, dtype=np.float64)
    loc = n[None, :, None] * 128 + p[:, None, None]
    kdec = np.exp(-logg[None, None, :] * loc)
    odec = np.exp(logg[None, None, :] * loc) * (128.0 ** -0.5)
    coef = np.zeros((128, G, H), np.float64)
    for s in range(G):
        if s < rank:
            coef[:, s, :] = np.exp(logg * T * (rank - s - 1))[None, :]
    oneh = np.zeros((128, G, H), np.float64)
    oneh[:, rank, :] = np.exp(logg * T)[None, :]
    gT = np.broadcast_to(np.exp(logg * T)[None, :], (128, H))
    j = np.arange(128)[:, None]
    i = np.arange(128)[None, :]
    cj, ci = j // 64, i // 64
    mask = np.zeros((128, H, 128), np.float64)
    for h in range(H):
        m = np.where(i >= j, 1.0, np.exp(logg[h] * 2.0 * (j - i)))
        m = np.where(cj > ci, 0.0, m)
        m = np.where(cj < ci, 1.0, m)
        mask[:, h, :] = m
    cmask = (cj <= ci).astype(np.float32)
    ident = np.eye(128, dtype=np.float32)
    pm = np.zeros((128, 128), np.float32)
    for d in range(128):
        pm[(d + 64) % 128, d] = 1.0
    f = lambda a: np.ascontiguousarray(np.asarray(a, dtype=np.float32))
    return {
        "tb_cos": f(tb_cos), "tb_sin": f(tb_sin),
        "tb_kdec": f(kdec.reshape(128, NT * H)), "tb_odec": f(odec.reshape(128, NT * H)),
        "tb_coef": f(coef.reshape(128, G * H)), "tb_oneh": f(oneh.reshape(128, G * H)), "tb_gT": f(gT),
        "tb_mask": f(mask.reshape(128, H * 128)), "tb_cmask": f(cmask),
        "tb_ident": f(ident), "tb_pm": f(pm), "tb_ones": np.ones((128, 128), np.float32),
    }


def _build(cfg):
    c = cfg
    D, H, T, NL, KC, NT, NB, FC, RC, SG, G, DFF = c.D, c.H, c.T, c.NL, c.KC, c.NT, c.NB, c.FC, c.RC, c.SG, c.G, c.DFF
    nc = bass.Bass("TRN2", target_bir_lowering=False)

    def din(name, shape):
        return nc.dram_tensor(name, list(shape), F32, kind="ExternalInput").ap()

    def dscr(name, shape, dt):
        return nc.dram_tensor(name, list(shape), dt, kind="Internal").ap()

    x_in = din("x", [T, D])
    w_in = din("w_in", [NL, D, c.INC])
    ret_proj = din("ret_proj", [NL, c.RV, D])
    sgu_proj = din("sgu_proj", [NL, D, D])
    w_out = din("w_out", [NL, D, D])
    w_ffn_in = din("w_ffn_in", [NL, D, 2 * DFF])
    w_ffn_out = din("w_ffn_out", [NL, DFF, D])
    nw1_d = din("norm_mix_w", [NL, 128, KC])
    nw2_d = din("norm_ffn_w", [NL, 128, KC])
    nwf_d = din("final_norm_w", [128, KC])
    gnw_d = din("ret_gn_w", [NL, 128, RC])
    lnw_d = din("sgu_ln_w", [NL, D])
    lnb_d = din("sgu_ln_b", [NL, D])
    ws_d = din("sgu_w_s", [NL, SG, 128, 128])
    bs_d = din("sgu_b_s", [NL, SG * 128])
    tbs = {}
    for nm, shp in (("tb_cos", [128, T]), ("tb_sin", [128, T]), ("tb_kdec", [128, NT * H]),
                    ("tb_odec", [128, NT * H]), ("tb_coef", [128, G * H]), ("tb_oneh", [128, G * H]),
                    ("tb_gT", [128, H]), ("tb_mask", [128, H * 128]), ("tb_cmask", [128, 128]),
                    ("tb_ident", [128, 128]), ("tb_pm", [128, 128]), ("tb_ones", [128, 128])):
        tbs[nm] = din(nm, shp)
    y_out = nc.dram_tensor("y", [T, D], F32, kind="ExternalOutput").ap()

    XT = dscr("XT", [KC, 128, T], F32)
    ZQ = dscr("ZQ", [H, 128, T], BF16)
    ZK = dscr("ZK", [H, 128, T], BF16)
    ZVv = dscr("ZVv", [128, NT, c.RV], BF16)
    ZG = dscr("ZG", [128, NT, c.RV], BF16)
    ZU = dscr("ZU", [KC, 128, T], BF16)
    ZS = dscr("ZS", [128, NT, D], BF16)
    GA = dscr("GA", [KC, 128, T], BF16)
    GB = dscr("GB", [KC, 128, T], BF16)
    GT = dscr("GT", [RC, 128, T], BF16)
    ST = dscr("ST", [KC, 128, T], BF16)
    AT = dscr("AT", [FC, 128, T], BF16)
    EXI = dscr("EXI", [G * H * 128, 256], F32)
    EXO = dscr("EXO", [G * H * 128, 256], F32)

    WBC = 256
    ARENA = 65536
    with ExitStack() as st:
        fw = Fw(nc, st)
        sb = lambda name, shape, dt: st.enter_context(nc.sbuf_tensor(name, list(shape), dt))
        arena = sb("arena", [128, ARENA], BF16)
        KH = FC // 2
        assert FC % 2 == 0
        wb = [sb("wb%d" % i, [128, max(KC, KH) * WBC], BF16) for i in range(2)]
        WCTX = dict(slots=[w[:] for w in wb], wbc=WBC, key="wb")
        kdec_t = sb("kdec_t", [128, NT * H], F32)
        odec_t = sb("odec_t", [128, NT * H], F32)
        coef_t = sb("coef_t", [128, G * H], F32)
        oneh_t = sb("oneh_t", [128, G * H], F32)
        gT_t = sb("gT_t", [128, H], F32)
        mask_t = sb("mask_t", [128, H * 128], F32)
        cmask_t = sb("cmask_t", [128, 128], F32)
        ident_b = sb("ident_b", [128, 128], BF16)
        ident_f = sb("ident_f", [128, 128], F32)
        pm_b = sb("pm_b", [128, 128], BF16)
        ones_b = sb("ones_b", [128, 128], BF16)
        nw1_t = sb("nw1_t", [128, NL * KC], F32)
        nw2_t = sb("nw2_t", [128, NL * KC], F32)
        nwf_t = sb("nwf_t", [128, KC], F32)
        gnw_t = sb("gnw_t", [128, NL * RC], F32)
        lnw_t = sb("lnw_t", [128, D], BF16)
        lnb_t = sb("lnb_t", [128, D], BF16)
        wsb_t = sb("wsb_t", [128, SG * 128], BF16)
        wT_t = sb("wT_t", [128, SG * 128], BF16)
        bsf_t = sb("bsf_t", [1, SG * 128], F32)
        bsh_t = sb("bsh_t", [1, SG * 128], BF16)
        bsr_t = sb("bsr_t", [1, SG * 128], F32)
        bsl_t = sb("bsl_t", [1, SG * 128], BF16)
        NSB = 4
        stg_b = [sb("stgb%d" % i, [128, 512], BF16) for i in range(NSB)]
        stg_f = [sb("stgf%d" % i, [128, 512], F32) for i in range(3)]
        ld_b = [sb("ldb%d" % i, [128, 512], BF16) for i in range(2)]
        ld_f = [sb("ldf%d" % i, [128, 512], F32) for i in range(2)]
        rstd_t = sb("rstd_t", [128, 512], F32)
        small = sb("small", [128, 256], F32)
        s0_t = sb("s0_t", [128, 256], F32)
        psG = [st.enter_context(nc.psum_tensor("psG%d" % i, [128, 512], F32)) for i in range(4)]
        psT = [st.enter_context(nc.psum_tensor("psT%d" % i, [128, 1024], BF16)) for i in range(2)]
        psX = st.enter_context(nc.psum_tensor("psX", [128, 512], F32))
        psS = st.enter_context(nc.psum_tensor("psS", [128, 512], F32))
        block = st.enter_context(nc.Block())

        rr = {"g": 0, "g3": 0, "sb": 0, "sf": 0, "lb": 0, "lf": 0, "t": 0}

        def nxt(kind, n):
            i = rr[kind]
            rr[kind] = (i + 1) % n
            return i

        def av(off, n):
            return arena[:, off:off + n]

        def avf(off, n):
            return arena[:, off:off + n].bitcast(F32)

        def mm(ps, lhsT, rhs, start, stop):
            return lambda E: E.matmul(ps, lhsT=lhsT, rhs=rhs, start=start, stop=stop)

        def trp(out, in_, ident):
            return lambda E: E.transpose(out=out, in_=in_, identity=ident)

        def actf(out, in_, func, scale=1.0, bias=0.0):
            return lambda E: E.activation(out=out, in_=in_, func=func, bias=bias, scale=scale)

        def tt(out, in0, in1, op):
            return lambda E: E.tensor_tensor(out=out, in0=in0, in1=in1, op=op)

        def stt(out, in0, scalar, in1, op0, op1):
            return lambda E: E.scalar_tensor_tensor(out=out, in0=in0, scalar=scalar, in1=in1, op0=op0, op1=op1)

        def tsc(out, in0, s1, s2, op0, op1):
            return lambda E: E.tensor_scalar(out=out, in0=in0, scalar1=s1, scalar2=s2, op0=op0, op1=op1)

        gid = [0]

        def newgroup():
            gid[0] += 1
            return gid[0]

        def ld(qn, out, in_, reads=(), writes=(), group=None):
            return fw.dma(qn, lambda E: E.dma_start(out=out, in_=in_), reads=reads, writes=writes, group=group)

        ld("pool", ident_b[:], tbs["tb_ident"], writes=["identb"])
        ld("pool", pm_b[:], tbs["tb_pm"], writes=["pmb"])
        ld("pool", ones_b[:], tbs["tb_ones"], writes=["onesb"])
        ld("sp", ident_f[:], tbs["tb_ident"], writes=["identf"])
        ld("sp", kdec_t[:], tbs["tb_kdec"], writes=["kdec"])
        ld("sp", odec_t[:], tbs["tb_odec"], writes=["odec"])
        ld("sp", coef_t[:], tbs["tb_coef"], writes=["coef"])
        ld("sp", oneh_t[:], tbs["tb_oneh"], writes=["oneh"])
        ld("sp", gT_t[:], tbs["tb_gT"], writes=["gT"])
        ld("sp", mask_t[:], tbs["tb_mask"], writes=["mask"])
        ld("sp", cmask_t[:], tbs["tb_cmask"], writes=["cmask"])
        ld("sp", nw1_t[:].rearrange("p (l k) -> p l k", l=NL), nw1_d.rearrange("l p k -> p l k"), writes=["nw1"])
        ld("sp", nw2_t[:].rearrange("p (l k) -> p l k", l=NL), nw2_d.rearrange("l p k -> p l k"), writes=["nw2"])
        ld("sp", gnw_t[:].rearrange("p (l k) -> p l k", l=NL), gnw_d.rearrange("l p k -> p l k"), writes=["gnw"])
        ld("sp", nwf_t[:], nwf_d, writes=["nwf"])
        CONST_KEYS = ["cos", "sin", "identb", "pmb", "onesb", "identf", "kdec", "odec", "coef", "oneh", "gT",
                      "mask", "cmask", "nw1", "nw2", "gnw", "nwf"]

        def ingest():
            XS = 0
            for n in range(NT):
                s = n % 2
                xt = avf(XS + s * 2 * D, 2 * D)
                ld("sp", xt, x_in[n * 128:(n + 1) * 128, :], writes=[("xs", s)])
                for k0 in range(0, KC, 4):
                    gi = nxt("g", 4)
                    ps = psG[gi]
                    fw.pe_group([trp(ps[:, j * 128:(j + 1) * 128], xt[:, (k0 + j) * 128:(k0 + j + 1) * 128], ident_f[:])
                                 for j in range(4)], reads=[("xs", s), "identf"], writes=[("psG", gi)])
                    si = nxt("sf", 3)
                    fw.op("act", actf(stg_f[si][:], ps[:], AF.Copy), reads=[("psG", gi)], writes=[("stgf", si)])
                    ld("pool", XT[k0:k0 + 4, :, n * 128:(n + 1) * 128].rearrange("k p t -> p k t"),
                       stg_f[si][:].rearrange("p (k t) -> p k t", k=4), reads=[("stgf", si)], writes=["XT"], group="g_XT")

        BUFA = 0
        BUFB = KC * T

        def hT(k, t0, n):
            return arena[:, BUFA + k * T + t0: BUFA + k * T + t0 + n]

        def mT(k, t0, n):
            return arena[:, BUFB + k * T + t0: BUFB + k * T + t0 + n]

        def norm_pass(nw_ap_fn, final=False):
            XB = BUFB
            assert XB + 2 * KC * 1024 <= ARENA
            for tb in range(NB):
                xbk = ("xb", tb % 2)
                xb = avf(XB + (tb % 2) * KC * 1024, KC * 1024).rearrange("p (k t) -> p k t", k=KC)
                ld("sp", xb, XT[:, :, tb * 512:(tb + 1) * 512].rearrange("k p t -> p k t"),
                   reads=["XT"], writes=[xbk])
                fns = []
                for k in range(KC):
                    si = nxt("sb", NSB)
                    fw.op("act", actf(stg_b[si][:], xb[:, k, :], AF.Square), reads=[xbk], writes=[("stgb", si)])
                    fw.pe_group([mm(psX[:], ones_b[:], stg_b[si][:], k == 0, k == KC - 1)],
                                reads=[("stgb", si), "onesb"], writes=["psX"])
                fw.op("act", actf(rstd_t[:], psX[:], AF.Sqrt, scale=1.0 / D, bias=EPS), reads=["psX"], writes=["rstd"])
                fw.op("dve", lambda E: E.reciprocal(out=rstd_t[:], in_=rstd_t[:]), reads=["rstd"], writes=["rstd"])
                if not final:
                    for k in range(KC):
                        fw.op("dve", stt(hT(k, tb * 512, 512), xb[:, k, :], nw_ap_fn(k), rstd_t[:], ALU.mult, ALU.mult),
                              reads=[xbk, "rstd", "nw1", "nw2"], writes=["bufA"])
                else:
                    for k in range(KC):
                        fw.op("dve", stt(xb[:, k, :], xb[:, k, :], nw_ap_fn(k), rstd_t[:], ALU.mult, ALU.mult),
                              reads=[xbk, "rstd", "nwf"], writes=[xbk])
                    for tl in range(4):
                        n = tb * 4 + tl
                        s = n % 2
                        yt = avf(BUFA + s * 2 * D, 2 * D)
                        for k0 in range(0, KC, 4):
                            gi = nxt("g", 4)
                            ps = psG[gi]
                            fw.pe_group([trp(ps[:, j * 128:(j + 1) * 128], xb[:, k0 + j, tl * 128:(tl + 1) * 128], ident_f[:])
                                         for j in range(4)], reads=[xbk, "identf"], writes=[("psG", gi)])
                            fw.op("act", actf(yt[:, k0 * 128:(k0 + 4) * 128], ps[:], AF.Copy),
                                  reads=[("psG", gi)], writes=[("yt", s)])
                        ld("pool", y_out[n * 128:(n + 1) * 128, :], yt, reads=[("yt", s)], writes=["y"], group="g_y")

        pre = {"done": None}

        def prefetch_w(wload, tag):
            wload(0, 0)
            pre["done"] = tag

        def gemm(Xfn, kc, nblk, wload, units_of_block, wctx=None, tag=None, filler=None, filler_from=0):
            wctx = wctx or WCTX
            deferred = []
            if tag is not None and pre["done"] == tag:
                pre["done"] = None
            else:
                wload(0, 0)
            for blk in range(nblk):
                slot = blk % 2
                if blk + 1 < nblk:
                    wload(blk + 1, (blk + 1) % 2)
                for u in units_of_block(blk, slot):
                    gi = nxt("g", 4)
                    ps = psG[gi][:, 0:u["n"]]
                    fw.pe_group([mm(ps, u["lhs"](k), u["rhs"](k), k == 0, k == kc - 1) for k in range(kc)],
                                reads=[(wctx["key"], slot), "bufA", "bufB"], writes=[("psG", gi)])
                    for dfn in deferred:
                        dfn()
                    deferred = []
                    d = u["epi"](ps, gi)
                    if d is not None:
                        deferred.append(d)
                    if filler is not None and blk >= filler_from:
                        next(filler, None)
            for dfn in deferred:
                dfn()
            if filler is not None:
                for _ in filler:
                    pass

        def wslice(slot, k, c0, n, wctx=None):
            wctx = wctx or WCTX
            return wctx["slots"][slot][:, k * wctx["wbc"] + c0: k * wctx["wbc"] + c0 + n]

        def wload_cols(Wl, kc_total, col_of_blk, ncols=None, dst_off=0, wctx=None):
            wctx = wctx or WCTX
            wbc = wctx["wbc"]
            ncols = ncols or wbc

            def f(blk, slot):
                col = col_of_blk(blk)
                src = Wl.rearrange("(k p) n -> p k n", p=128)
                dst = wctx["slots"][slot].rearrange("p (k n) -> p k n", n=wbc)
                step = 4
                grp = newgroup()
                for k0 in range(0, kc_total, step):
                    k1 = min(kc_total, k0 + step)
                    ld("pool", dst[:, k0:k1, dst_off:dst_off + ncols], src[:, k0:k1, col:col + ncols],
                       writes=[(wctx["key"], slot)], group=grp)
            return f

        def phase1(l):
            Wl = w_in[l]
            W1 = 512 if c.QK % 512 == 0 else 256
            W1S = KC * W1
            w1ctx = dict(slots=[arena[:, BUFB + i * W1S: BUFB + (i + 1) * W1S] for i in range(2)], wbc=W1, key="wb1")
            CS = BUFB + 2 * W1S
            assert CS + 2 * T <= ARENA
            cos_v = arena[:, CS:CS + T]
            sin_v = arena[:, CS + T:CS + 2 * T]
            ld("pool", cos_v, tbs["tb_cos"], writes=["cos"])
            ld("pool", sin_v, tbs["tb_sin"], writes=["sin"])
            nblk = c.INC // W1
            kv = [b for b in range(nblk) if c.ok <= b * W1 < c.og]
            order = kv + [b for b in range(nblk) if b not in kv]
            PA0 = CS + 2 * T
            assert PA0 + T + NT * 256 + 512 + 4 * G * 256 <= ARENA, "arena overflow (pass A inside phase 1)"

            def kind_of(col):
                if col < c.ok:
                    return "q"
                if col < c.ov:
                    return "k"
                if col < c.og:
                    return "v"
                if col < c.osu:
                    return "g"
                if col < c.osv:
                    return "su"
                if col < c.oga:
                    return "sv"
                if col < c.ogb:
                    return "ga"
                return "gb"

            def units(blk, slot):
                col = order[blk] * W1
                kind = kind_of(col)
                us = []
                if kind in ("v", "g", "sv"):
                    base = {"v": c.ov, "g": c.og, "sv": c.osv}[kind]
                    dst = {"v": ZVv, "g": ZG, "sv": ZS}[kind]
                    func = {"v": AF.Copy, "g": AF.Silu, "sv": AF.Gelu}[kind]
                    for n in range(NT):
                        def epi(ps, gi, n=n, dst=dst, func=func, c0=col - base):
                            si = nxt("sb", NSB)
                            fw.op("act", actf(stg_b[si][:, 0:W1], ps, func), reads=[("psG", gi)], writes=[("stgb", si)])
                            ld("pool", dst[:, n, c0:c0 + W1], stg_b[si][:, 0:W1], reads=[("stgb", si)], writes=["Z"], group="g_Z")
                            return None
                        us.append(dict(lhs=lambda k, n=n: hT(k, n * 128, 128),
                                       rhs=lambda k, slot=slot: wslice(slot, k, 0, W1, w1ctx), n=W1, epi=epi))
                else:
                    base = {"q": c.oq, "k": c.ok, "su": c.osu, "ga": c.oga, "gb": c.ogb}[kind]
                    for m in range(W1 // 128):
                        ch = (col - base) // 128 + m
                        for tb in range(NB):
                            if kind in ("q", "k"):
                                dst = ZQ if kind == "q" else ZK

                                def epi(ps, gi, ch=ch, tb=tb, dst=dst):
                                    si = nxt("sb", NSB)
                                    zb = stg_b[si]
                                    fw.op("act", actf(zb[:], ps, AF.Copy), reads=[("psG", gi)], writes=[("stgb", si)])

                                    def later():
                                        fw.pe_group([mm(psX[:], pm_b[:], zb[:], True, True)],
                                                    reads=[("stgb", si), "pmb"], writes=["psX"])
                                        f1 = nxt("sf", 3)
                                        fw.op("dve", tt(stg_f[f1][:], zb[:], cos_v[:, tb * 512:(tb + 1) * 512], ALU.mult),
                                              reads=[("stgb", si), "cos"], writes=[("stgf", f1)])
                                        f2 = nxt("sf", 3)
                                        fw.op("dve", tt(stg_f[f2][:], psX[:], sin_v[:, tb * 512:(tb + 1) * 512], ALU.mult),
                                              reads=["psX", "sin"], writes=[("stgf", f2)])
                                        so = nxt("sb", NSB)
                                        fw.op("dve", tt(stg_b[so][:], stg_f[f1][:], stg_f[f2][:], ALU.add),
                                              reads=[("stgf", f1), ("stgf", f2)], writes=[("stgb", so)])
                                        ld("pool", dst[ch, :, tb * 512:(tb + 1) * 512], stg_b[so][:],
                                           reads=[("stgb", so)], writes=["Z"], group="g_Z")
                                    return later
                            else:
                                dst = {"su": ZU, "ga": GA, "gb": GB}[kind]
                                func = AF.Gelu if kind == "su" else AF.Sigmoid

                                def epi(ps, gi, ch=ch, tb=tb, dst=dst, func=func):
                                    si = nxt("sb", NSB)
                                    fw.op("act", actf(stg_b[si][:], ps, func), reads=[("psG", gi)], writes=[("stgb", si)])
                                    ld("pool", dst[ch, :, tb * 512:(tb + 1) * 512], stg_b[si][:],
                                       reads=[("stgb", si)], writes=["Z"], group="g_Z")
                                    return None
                            us.append(dict(lhs=lambda k, slot=slot, m=m: wslice(slot, k, m * 128, 128, w1ctx),
                                           rhs=lambda k, tb=tb: hT(k, tb * 512, 512), n=512, epi=epi))
                return us

            gemm(hT, KC, nblk, wload_cols(Wl, KC, lambda blk: order[blk] * W1, wctx=w1ctx), units, wctx=w1ctx,
                 filler=pass_a_gen(PA0), filler_from=len(kv) + 1)

        R_SLOT = 2 * T + 2 * NT * 256
        R0 = 0
        RTMP = R0 + 2 * R_SLOT
        RTMP_SZ = 14336
        PA_END = 512 + 4 * G * 256
        O_OFF = RTMP + RTMP_SZ
        GTH = O_OFF + 2 * 2 * NT * 256
        assert GTH + 4 * T <= ARENA, "arena overflow (retention)"
        S_OFF = RTMP + PA_END
        assert S_OFF + 4 * D + 2 * D + KC * 512 * 2 <= ARENA, "arena overflow (sgu)"

        def load_layer_small(l):
            ld("pool", lnw_t[:], lnw_d[l:l + 1, :].partition_broadcast(128), writes=["lnw"])
            ld("pool", lnb_t[:], lnb_d[l:l + 1, :].partition_broadcast(128), writes=["lnb"])
            ld("pool", wsb_t[:].rearrange("p (g j) -> p g j", g=SG), ws_d[l].rearrange("g i j -> i g j"), writes=["wsb"])
            ld("sp", bsf_t[:], bs_d[l:l + 1, :], writes=["bsf"])
            for g in range(SG):
                ti = nxt("t", 2)
                fw.pe_group([trp(psT[ti][:, 0:128], wsb_t[:, g * 128:(g + 1) * 128], ident_b[:])],
                            reads=["wsb", "identb"], writes=[("psT", ti)])
                fw.op("dve", tt(wT_t[:, g * 128:(g + 1) * 128], psT[ti][:, 0:128], cmask_t[:], ALU.mult),
                      reads=[("psT", ti), "cmask"], writes=["wT"])
            fw.op("dve", lambda E: E.tensor_copy(out=bsh_t[:], in_=bsf_t[:]), reads=["bsf"], writes=["bsh"])
            fw.op("dve", tt(bsr_t[:], bsf_t[:], bsh_t[:], ALU.subtract), reads=["bsf", "bsh"], writes=["bsr"])
            fw.op("dve", lambda E: E.tensor_copy(out=bsl_t[:], in_=bsr_t[:]), reads=["bsr"], writes=["bsl"])

        def head_views(s):
            b = R0 + s * R_SLOT
            qT = av(b, T)
            kT = av(b + T, T)
            v = av(b + 2 * T, NT * 256)
            sg = av(b + 2 * T + NT * 256, NT * 256)
            return qT, kT, v, sg

        def kt_make(h, n, kT, s, pi=0):
            ti = nxt("t", 2)
            fw.pe_group([trp(psT[ti][:, 0:128], kT[:, n * 128:(n + 1) * 128], ident_b[:])],
                        reads=[("hk", s), "identb"], writes=[("psT", ti)])
            kk = pi * 2 + n % 2
            kt = av(RTMP + kk * 128, 128)
            fw.op("act", actf(kt, psT[ti][:, 0:128], AF.Copy, scale=kdec_t[:, n * H + h:n * H + h + 1]),
                  reads=[("psT", ti), "kdec"], writes=[("kt", kk)])
            return kt, ("kt", kk)

        def pass_a_gen(PA0):
            exg = newgroup()
            kT = av(PA0, T)
            v = av(PA0 + T, NT * 256)
            KT0 = PA0 + T + NT * 256
            EX0 = KT0 + 512
            acc = psS[:, 0:256]
            acck = ("psS", 0)
            for h in range(H):
                ld("sp", kT, ZK[h], reads=["Z"], writes=["pa_k"])
                ld("sp", v.rearrange("p (n e) -> p n e", n=NT), ZVv[:, :, h * 256:(h + 1) * 256], reads=["Z"], writes=["pa_v"])
                prev = None
                for n in range(NT):
                    if prev is not None:
                        pk, pn = prev
                        fw.pe_group([mm(acc, pk, v[:, pn * 256:(pn + 1) * 256], pn == 0, pn == NT - 1)],
                                    reads=[("pa_kt", pn % 2), "pa_v"], writes=[acck])
                    ti = nxt("t", 2)
                    fw.pe_group([trp(psT[ti][:, 0:128], kT[:, n * 128:(n + 1) * 128], ident_b[:])],
                                reads=["pa_k", "identb"], writes=[("psT", ti)])
                    kt = av(KT0 + (n % 2) * 128, 128)
                    fw.op("act", actf(kt, psT[ti][:, 0:128], AF.Copy, scale=kdec_t[:, n * H + h:n * H + h + 1]),
                          reads=[("psT", ti), "kdec"], writes=[("pa_kt", n % 2)])
                    prev = (kt, n)
                    yield
                pk, pn = prev
                fw.pe_group([mm(acc, pk, v[:, pn * 256:(pn + 1) * 256], pn == 0, pn == NT - 1)],
                            reads=[("pa_kt", pn % 2), "pa_v"], writes=[acck])
                ex = avf(EX0 + (h % 2) * 2 * G * 256, 2 * G * 256)
                for sl in range(G):
                    fw.op("dve", lambda E, ex=ex, sl=sl, h=h: E.tensor_scalar_mul(
                        out=ex[:, sl * 256:(sl + 1) * 256], in0=acc, scalar1=oneh_t[:, sl * H + h: sl * H + h + 1]),
                        reads=[acck, "oneh"], writes=[("ex", h % 2)])
                ld("pool", EXI.rearrange("(g h p) e -> p g h e", g=G, h=H)[:, :, h, :],
                   ex.rearrange("p (g e) -> p g e", g=G), reads=[("ex", h % 2)], writes=["EXI"], group=exg)
                yield

        def exchange():
            rgs = [list(range(b * G, (b + 1) * G)) for b in range(c.BATCH)]
            fw.coll(lambda E: E.collective_compute(
                "AllReduce", ALU.add, replica_groups=rgs,
                ins=[EXI.opt()], outs=[EXO.opt()]), reads=["EXI"], writes=["EXO"])

        def sgu(l):
            ZV_O = S_OFF
            TMP_O = ZV_O + 4 * D
            ZU_O = TMP_O + 2 * D
            OUT_O = ZU_O + KC * 512
            for tb in range(NB):
                zv = av(ZV_O, 4 * D)
                vn = zv
                zu = av(ZU_O, KC * 512).rearrange("p (k t) -> p k t", k=KC)
                out = av(OUT_O, KC * 512).rearrange("p (k t) -> p k t", k=KC)
                ld("sp", zv.rearrange("p (n d) -> p n d", n=4), ZS[:, tb * 4:(tb + 1) * 4, :], writes=[("zv", 0), ("zv", 1), ("zv", 2), ("zv", 3)])
                ld("sp", zu, ZU[:, :, tb * 512:(tb + 1) * 512].rearrange("k p t -> p k t"), writes=["zu"])
                mv = small[:, 0:8]
                nchk = max(1, D // 512)
                for tl in range(4):
                    stats = small[:, 16 + tl * 6 * nchk: 16 + (tl + 1) * 6 * nchk]
                    for cc in range(nchk):
                        w = min(512, D)
                        fw.op("dve", lambda E, tl=tl, cc=cc, stats=stats, w=w: E.bn_stats(
                            out=stats[:, cc * 6:(cc + 1) * 6], in_=zv[:, tl * D + cc * w: tl * D + (cc + 1) * w]),
                            reads=[("zv", tl)], writes=[("sgst", tl)])
                    fw.op("dve", lambda E, tl=tl, stats=stats: E.bn_aggr(out=mv[:, tl * 2:tl * 2 + 2], in_=stats.rearrange("p (c s) -> p c s", s=6)),
                          reads=[("sgst", tl)], writes=["sgmv"])
                rs = small[:, 8:12]
                mv3 = mv.rearrange("p (t two) -> p t two", two=2)
                fw.op("act", actf(rs, mv3[:, :, 1], AF.Sqrt, bias=EPS), reads=["sgmv"], writes=["sgrs"])
                fw.op("dve", lambda E: E.reciprocal(out=rs, in_=rs), reads=["sgrs"], writes=["sgrs"])
                for tl in range(4):
                    tmp = avf(TMP_O, 2 * D)
                    fw.op("dve", stt(tmp, zv[:, tl * D:(tl + 1) * D], mv[:, tl * 2:tl * 2 + 1], lnw_t[:],
                                     ALU.subtract, ALU.mult), reads=[("zv", tl), "sgmv", "lnw"], writes=["sgtmp"])
                    fw.op("dve", stt(vn[:, tl * D:(tl + 1) * D], tmp, rs[:, tl:tl + 1], lnb_t[:], ALU.mult, ALU.add),
                          reads=["sgtmp", "sgrs", "lnb"], writes=[("zv", tl)])
                for tl in range(4):
                    for k0 in range(0, KC, 4):
                        gi = nxt("g", 4)
                        ps = psG[gi]
                        fns = []
                        for j in range(4):
                            k = k0 + j
                            g = k // 2
                            o = ps[:, j * 128:(j + 1) * 128]
                            fns.append(mm(o, vn[:, tl * D + k * 128: tl * D + (k + 1) * 128], wT_t[:, g * 128:(g + 1) * 128], True, False))
                            fns.append(mm(o, ones_b[0:1, :], bsh_t[0:1, g * 128:(g + 1) * 128], False, False))
                            fns.append(mm(o, ones_b[0:1, :], bsl_t[0:1, g * 128:(g + 1) * 128], False, True))
                        fw.pe_group(fns, reads=[("zv", tl), "wT", "bsh", "bsl", "onesb"], writes=[("psG", gi)])
                        fw.op("dve", tt(out[:, k0:k0 + 4, tl * 128:(tl + 1) * 128],
                                        ps[:].rearrange("p (k t) -> p k t", k=4),
                                        zu[:, k0:k0 + 4, tl * 128:(tl + 1) * 128], ALU.mult),
                              reads=[("psG", gi), "zu"], writes=["sgout"])
                ld("pool", ST[:, :, tb * 512:(tb + 1) * 512].rearrange("k p t -> p k t"), out, reads=["sgout"], writes=["ST"], group="g_ST")

        def pass_b(l):
            SIN_HI = RTMP + 512
            SIN_LO = SIN_HI + H * 256
            EXL = SIN_LO + H * 256
            for h in range(H):
                e = h % 2
                exl = avf(EXL + e * 2 * G * 256, 2 * G * 256)
                ld("sp", exl.rearrange("p (g e) -> p g e", g=G),
                   EXO.rearrange("(g h p) e -> p g h e", g=G, h=H)[:, :, h, :], reads=["EXO"], writes=[("exl", e)])
                s0 = s0_t[:]
                fw.op("dve", lambda E, exl=exl, h=h, s0=s0: E.tensor_scalar_mul(out=s0, in0=exl[:, 0:256], scalar1=coef_t[:, h:h + 1]),
                      reads=[("exl", e), "coef"], writes=["s0"])
                for sl in range(1, G):
                    fw.op("dve", stt(s0, exl[:, sl * 256:(sl + 1) * 256], coef_t[:, sl * H + h: sl * H + h + 1], s0,
                                     ALU.mult, ALU.add), reads=[("exl", e), "coef", "s0"], writes=["s0"])
                hi = av(SIN_HI + h * 256, 256)
                lo = av(SIN_LO + h * 256, 256)
                fw.op("dve", lambda E, hi=hi, s0=s0: E.tensor_copy(out=hi, in_=s0), reads=["s0"], writes=["sinhi"])
                fw.op("dve", tt(s0, s0, hi, ALU.subtract), reads=["s0", "sinhi"], writes=["s0"])
                fw.op("dve", lambda E, lo=lo, s0=s0: E.tensor_copy(out=lo, in_=s0), reads=["s0"], writes=["sinlo"])
            SB_O = EXL + 4 * G * 256
            SC_O = SB_O + 1024
            TF_O = SC_O + 512
            U_O = TF_O + 2048
            assert U_O + 1024 <= RTMP + RTMP_SZ, "RTMP overflow"

            def head_gen(h, pi):
                s = pi
                qT, kT, v, sg = head_views(s)
                ld("sp", qT, ZQ[h], writes=[("hq", s)])
                ld("sp", kT, ZK[h], writes=[("hk", s)])
                ld("sp", v.rearrange("p (n e) -> p n e", n=NT), ZVv[:, :, h * 256:(h + 1) * 256], writes=[("hv", s)])
                ld("sp", sg.rearrange("p (n e) -> p n e", n=NT), ZG[:, :, h * 256:(h + 1) * 256], writes=[("hg", s)])
                if pi == 0:
                    acc, acck = psS[:, 0:256], ("psS", 0)
                else:
                    acc, acck = psG[3][:, 0:256], ("psG", 3)
                fw.pe_group([mm(acc, ident_b[:], av(SIN_HI + h * 256, 256), True, False),
                             mm(acc, ident_b[:], av(SIN_LO + h * 256, 256), False, False)],
                            reads=["sinhi", "sinlo", "identb"], writes=[acck])
                o_all = avf(O_OFF + pi * 2 * NT * 256, 2 * NT * 256)
                mv = small[:, pi * 32: pi * 32 + 2 * NT]
                yield
                for n in range(NT):
                    sbi = pi * 2 + n % 2
                    Sb = av(SB_O + sbi * 256, 256)
                    fw.op("act", actf(Sb, acc, AF.Copy), reads=[acck], writes=[("Sb", sbi)])
                    kt, ktk = kt_make(h, n, kT, s, pi)
                    fw.pe_group([mm(psX[:, 0:128], kT[:, n * 128:(n + 1) * 128], qT[:, n * 128:(n + 1) * 128], True, True)],
                                reads=[("hk", s), ("hq", s)], writes=["psX"])
                    sc = av(SC_O + sbi * 128, 128)
                    fw.op("dve", stt(sc, psX[:, 0:128], kdec_t[:, n * H + h:n * H + h + 1],
                                     mask_t[:, h * 128:(h + 1) * 128], ALU.mult, ALU.mult),
                          reads=["psX", "kdec", "mask"], writes=[("sc", sbi)])
                    yield
                    gi = nxt("g3", 3)
                    po = psG[gi][:, 0:256]
                    fw.pe_group([mm(po, sc, v[:, n * 256:(n + 1) * 256], True, False),
                                 mm(po, qT[:, n * 128:(n + 1) * 128], Sb, False, True)],
                                reads=[("sc", sbi), ("Sb", sbi), ("hv", s), ("hq", s)], writes=[("psG", gi)])
                    fw.pe_group([mm(acc, kt, v[:, n * 256:(n + 1) * 256], False, n == NT - 1)],
                                reads=[ktk, ("hv", s)], writes=[acck])
                    fw.op("act", actf(o_all[:, n * 256:(n + 1) * 256], po, AF.Copy, scale=odec_t[:, n * H + h:n * H + h + 1]),
                          reads=[("psG", gi), "odec"], writes=[("o", pi, n)])
                    st6 = small[:, 64 + sbi * 6: 64 + sbi * 6 + 6]
                    fw.op("dve", lambda E, n=n, st6=st6, o_all=o_all: E.bn_stats(out=st6, in_=o_all[:, n * 256:(n + 1) * 256]),
                          reads=[("o", pi, n)], writes=[("st6", sbi)])
                    fw.op("dve", lambda E, n=n, st6=st6, mv=mv: E.bn_aggr(out=mv[:, 2 * n:2 * n + 2], in_=st6),
                          reads=[("st6", sbi)], writes=[("rmv", pi)])
                    yield
                rs = small[:, 128 + pi * 32:128 + pi * 32 + NT]
                fw.op("act", actf(rs, mv.rearrange("p (t two) -> p t two", two=2)[:, :, 1], AF.Sqrt, bias=EPS),
                      reads=[("rmv", pi)], writes=[("rrs", pi)])
                fw.op("dve", lambda E, rs=rs: E.reciprocal(out=rs, in_=rs), reads=[("rrs", pi)], writes=[("rrs", pi)])
                gth = av(GTH + s * 2 * T, 2 * T)
                yield
                for n in range(NT):
                    ti2 = pi * 2 + n % 2
                    tf = avf(TF_O + ti2 * 512, 512)
                    fw.op("dve", tsc(tf, o_all[:, n * 256:(n + 1) * 256], mv[:, 2 * n:2 * n + 1], rs[:, n:n + 1],
                                     ALU.subtract, ALU.mult), reads=[("o", pi, n), ("rmv", pi), ("rrs", pi)], writes=[("tf", ti2)])
                    u = av(U_O + ti2 * 256, 256)
                    fw.op("dve", tt(u, tf, sg[:, n * 256:(n + 1) * 256], ALU.mult),
                          reads=[("tf", ti2), ("hg", s)], writes=[("u", ti2)])
                    ti = nxt("t", 2)
                    fw.pe_group([trp(psT[ti][:, 0:128], u[:, 0:128], ident_b[:]),
                                 trp(psT[ti][:, 128:256], u[:, 128:256], ident_b[:])],
                                reads=[("u", ti2), "identb"], writes=[("psT", ti)])
                    yield
                    for e2 in range(2):
                        fw.op("act", actf(gth[:, e2 * T + n * 128: e2 * T + (n + 1) * 128], psT[ti][:, e2 * 128:(e2 + 1) * 128],
                                          AF.Copy, scale=gnw_t[:, l * RC + h * 2 + e2: l * RC + h * 2 + e2 + 1]),
                              reads=[("psT", ti), "gnw"], writes=[("gth", s)])
                    yield
                ld("pool", GT[h * 2:h * 2 + 2].rearrange("k p t -> p k t"), gth.rearrange("p (k t) -> p k t", k=2),
                   reads=[("gth", s)], writes=["GT"], group="g_GT")

            for h0 in range(0, H, 2):
                gens = [head_gen(h0 + pi, pi) for pi in range(min(2, H - h0))]
                alive = list(gens)
                while alive:
                    for g_ in list(alive):
                        try:
                            next(g_)
                        except StopIteration:
                            alive.remove(g_)

        def load_bufA(src, nch):
            grp = newgroup()
            for k0 in range(0, nch, 4):
                k1 = min(nch, k0 + 4)
                ld("sp", arena[:, BUFA + k0 * T: BUFA + k1 * T].rearrange("p (k t) -> p k t", t=T),
                   src[k0:k1].rearrange("k p t -> p k t"), writes=["bufA"], group=grp)
            assert nch * T <= ARENA

        def phase3(l):
            load_bufA(GT, RC)

            def mk_units(gate, first):
                def units(blk, slot):
                    us = []
                    for m in range(WBC // 128):
                        ch = blk * (WBC // 128) + m
                        for tb in range(NB):
                            def epi(ps, gi, ch=ch, tb=tb):
                                li = nxt("lb", 2)
                                ld("sp", ld_b[li][:], gate[ch, :, tb * 512:(tb + 1) * 512], writes=[("ldb", li)])
                                if first:
                                    fw.op("dve", tt(mT(ch, tb * 512, 512), ps, ld_b[li][:], ALU.mult),
                                          reads=[("psG", gi), ("ldb", li)], writes=[("mT", ch, tb)])
                                else:
                                    fi = nxt("sf", 3)
                                    fw.op("dve", tt(stg_f[fi][:], ps, ld_b[li][:], ALU.mult),
                                          reads=[("psG", gi), ("ldb", li)], writes=[("stgf", fi)])
                                    fw.op("dve", tt(mT(ch, tb * 512, 512), mT(ch, tb * 512, 512), stg_f[fi][:], ALU.add),
                                          reads=[("stgf", fi), ("mT", ch, tb)], writes=[("mT", ch, tb)])
                                return None
                            us.append(dict(lhs=lambda k, slot=slot, m=m: wslice(slot, k, m * 128, 128),
                                           rhs=lambda k, tb=tb: hT(k, tb * 512, 512), n=512, epi=epi))
                    return us
                return units

            gemm(hT, RC, D // WBC, wload_cols(ret_proj[l], RC, lambda blk: blk * WBC), mk_units(GA, True), tag=("p3a", l))
            prefetch_w(wload_cols(sgu_proj[l], KC, lambda blk: blk * WBC), ("p3b", l))
            fw.barrier()
            load_bufA(ST, KC)
            gemm(hT, KC, D // WBC, wload_cols(sgu_proj[l], KC, lambda blk: blk * WBC), mk_units(GB, False), tag=("p3b", l))
            prefetch_w(wload_cols(w_out[l], KC, lambda blk: blk * WBC), ("p3c", l))
            fw.barrier()
            resid_gemm(mT, KC, w_out[l], tag=("p3c", l))
            prefetch_w(wl4(l), ("p4", l))

        def resid_gemm(Xfn, kc, Wl, tag=None):
            def units(blk, slot):
                us = []
                for m in range(WBC // 128):
                    ch = blk * (WBC // 128) + m
                    for tb in range(NB):
                        def epi(ps, gi, ch=ch, tb=tb):
                            li = nxt("lf", 2)
                            ld("sp", ld_f[li][:], XT[ch, :, tb * 512:(tb + 1) * 512], reads=[("XT", ch, tb)], writes=[("ldf", li)])
                            fi = nxt("sf", 3)
                            fw.op("dve", tt(stg_f[fi][:], ps, ld_f[li][:], ALU.add),
                                  reads=[("psG", gi), ("ldf", li)], writes=[("stgf", fi)])
                            ld("pool", XT[ch, :, tb * 512:(tb + 1) * 512], stg_f[fi][:], reads=[("stgf", fi)],
                               writes=[("XT", ch, tb)])
                            return None
                        us.append(dict(lhs=lambda k, slot=slot, m=m: wslice(slot, k, m * 128, 128),
                                       rhs=lambda k, tb=tb: Xfn(k, tb * 512, 512), n=512, epi=epi))
                return us
            gemm(Xfn, kc, D // WBC, wload_cols(Wl, kc, lambda blk: blk * WBC), units, tag=tag)

        def wl4(l):
            Wl = w_ffn_in[l]
            HB = WBC // 2

            return wload_cols(Wl, KC, lambda b: b * WBC)

        def phase4(l):
            HB = WBC // 2
            nblk = DFF // HB
            wl = wl4(l)

            def units(blk, slot):
                us = []
                for m in range(HB // 128):
                    ch = blk * (HB // 128) + m
                    for tb in range(NB):
                        hold = {}

                        def epi_a(ps, gi, hold=hold):
                            fi = nxt("sf", 3)
                            fw.op("act", actf(stg_f[fi][:], ps, AF.Silu), reads=[("psG", gi)], writes=[("stgf", fi)])
                            hold["fi"] = fi
                            return None

                        def epi_c(ps, gi, hold=hold, ch=ch, tb=tb):
                            fi = hold["fi"]
                            si = nxt("sb", NSB)
                            fw.op("dve", tt(stg_b[si][:], ps, stg_f[fi][:], ALU.mult),
                                  reads=[("psG", gi), ("stgf", fi)], writes=[("stgb", si)])
                            ld("pool", AT[ch, :, tb * 512:(tb + 1) * 512], stg_b[si][:], reads=[("stgb", si)], writes=["AT"], group="g_AT")
                            return None
                        us.append(dict(lhs=lambda k, slot=slot, m=m: wslice(slot, k, m * 128, 128),
                                       rhs=lambda k, tb=tb: hT(k, tb * 512, 512), n=512, epi=epi_a))
                        us.append(dict(lhs=lambda k, slot=slot, m=m: wslice(slot, k, HB + m * 128, 128),
                                       rhs=lambda k, tb=tb: hT(k, tb * 512, 512), n=512, epi=epi_c))
                return us
            gemm(hT, KC, nblk, wl, units, tag=("p4", l))
            prefetch_w(wload_cols(w_ffn_out[l][0:KH * 128, :], KH, lambda blk: blk * WBC), ("p5", l, 0))

        def phase5(l):
            Wl = w_ffn_out[l]
            for half in range(2):
                load_bufA(AT[half * KH:(half + 1) * KH], KH)
                resid_gemm(hT, KH, Wl[half * KH * 128:(half + 1) * KH * 128, :], tag=("p5", l, half))
                if half == 0:
                    prefetch_w(wload_cols(Wl[KH * 128:2 * KH * 128, :], KH, lambda blk: blk * WBC), ("p5", l, 1))

        import os as _os
        STOP = int(_os.environ.get("K_STOP", "99"))

        def program():
            ingest()
            fw.barrier()
            if STOP <= 1:
                return
            for l in range(NL):
                load_layer_small(l)
                norm_pass(lambda k, l=l: nw1_t[:, l * KC + k: l * KC + k + 1])
                fw.barrier()
                if STOP <= 2:
                    return
                phase1(l)
                fw.barrier()
                if STOP <= 3:
                    return
                exchange()
                if STOP <= 4:
                    return
                sgu(l)
                fw.barrier()
                if STOP <= 5:
                    return
                prefetch_w(wload_cols(ret_proj[l], RC, lambda blk: blk * WBC), ("p3a", l))
                pass_b(l)
                fw.barrier()
                if STOP <= 6:
                    return
                phase3(l)
                fw.barrier()
                if STOP <= 7:
                    return
                norm_pass(lambda k, l=l: nw2_t[:, l * KC + k: l * KC + k + 1])
                fw.barrier()
                if STOP <= 8:
                    return
                phase4(l)
                fw.barrier()
                if STOP <= 9:
                    return
                phase5(l)
                fw.barrier()
                if STOP <= 10:
                    return
            norm_pass(lambda k: nwf_t[:, k:k + 1], final=True)

        program()
        fw.finish()
        fw.run(block)
    return nc


def _run(cfg, x, norm_mix_w, w_in, ret_gn_w, ret_proj, sgu_ln_w, sgu_ln_b, sgu_w_s, sgu_b_s, sgu_proj, w_out,
         norm_ffn_w, w_ffn_in, w_ffn_out, final_norm_w, trace=False):
    c = cfg
    f = lambda a: np.ascontiguousarray(np.asarray(a, dtype=np.float32))
    x = f(x).reshape(c.BATCH * c.SEQ, c.D)
    shared = {
        "w_in": f(w_in), "ret_proj": f(ret_proj), "sgu_proj": f(sgu_proj), "w_out": f(w_out),
        "w_ffn_in": f(f(w_ffn_in).reshape(c.NL, c.D, 2, c.FC, 128).transpose(0, 1, 3, 2, 4).reshape(c.NL, c.D, 2 * c.DFF)),
        "w_ffn_out": f(w_ffn_out),
        "norm_mix_w": f(f(norm_mix_w).reshape(c.NL, c.KC, 128).transpose(0, 2, 1)),
        "norm_ffn_w": f(f(norm_ffn_w).reshape(c.NL, c.KC, 128).transpose(0, 2, 1)),
        "final_norm_w": f(f(final_norm_w).reshape(c.KC, 128).transpose(1, 0)),
        "ret_gn_w": f(f(ret_gn_w).reshape(c.NL, c.RC, 128).transpose(0, 2, 1)),
        "sgu_ln_w": f(sgu_ln_w), "sgu_ln_b": f(sgu_ln_b), "sgu_w_s": f(sgu_w_s),
        "sgu_b_s": f(f(sgu_b_s).reshape(c.NL, c.SG * 128)),
    }
    nc = _build(c)
    in_maps = []
    for core in range(c.NCORES):
        m = dict(shared)
        m["x"] = x[core * c.T:(core + 1) * c.T]
        m.update(_tables(c, core))
        in_maps.append(m)
    res = run_bass_kernel_spmd(nc, in_maps, core_ids=list(range(c.NCORES)), **({"trace": True} if trace else {}))
    y = np.concatenate([res.results[i]["y"] for i in range(c.NCORES)], 0)
    return y.reshape(c.BATCH, c.SEQ, c.D).astype(np.float32), res


def kernel(**inputs):
    cfg = Cfg()
    y, _ = _run(cfg, **inputs)
    return y
```

```python
import math
from contextlib import ExitStack

import numpy as np
import concourse.bass as bass
import concourse.mybir as mybir
from concourse.bass_utils import run_bass_kernel_spmd

F32 = mybir.dt.float32
BF16 = mybir.dt.bfloat16
AF = mybir.ActivationFunctionType
ALU = mybir.AluOpType
EPS = 1e-6
ENGS = ("pe", "act", "dve", "pool", "sp")


class Fw:
    def __init__(self, nc, stack, n_dma_slots=8):
        self.nc = nc
        self.q = {e: [] for e in ENGS}
        self.sem = {e: stack.enter_context(nc.semaphore("s_" + e)) for e in ENGS}
        self.cnt = {e: 0 for e in ENGS}
        self.waited = {e: {} for e in ENGS}
        self.last_w = {}
        self.readers = {}
        self.slots = {}
        self.slot_rr = {}
        for qn in ("sp", "pool"):
            self.slots[qn] = [[stack.enter_context(nc.semaphore("d_%s%d" % (qn, i))), 0]
                              for i in range(n_dma_slots)]
            self.slot_rr[qn] = 0
        self.cc_sem = stack.enter_context(nc.semaphore("cc_sem"))
        self.coll_cnt = 0

    def _wait(self, eng, ev):
        if ev is None:
            return
        sem, val, src = ev[0], ev[1], ev[2]
        if src == eng and eng == "pe":
            return
        key = id(sem)
        if self.waited[eng].get(key, 0) >= val:
            return
        self.waited[eng][key] = val
        self.q[eng].append(lambda E, sem=sem, val=val: E.wait_ge(sem, val))

    def _deps(self, eng, reads, writes, group=None):
        for r in reads:
            for ev in self.last_w.get(r, ()):
                self._wait(eng, ev)
        for w in writes:
            for ev in self.last_w.get(w, ()):
                if group is not None and len(ev) > 3 and ev[3] == group:
                    continue
                self._wait(eng, ev)
            for ev in self.readers.get(w, ()):
                self._wait(eng, ev)

    def _record(self, ev, reads, writes):
        for w in writes:
            if ev[2] is None:
                lst = [e for e in self.last_w.get(w, ()) if e[2] is None and e[0] is not ev[0]]
                lst.append(ev)
                self.last_w[w] = lst
            else:
                self.last_w[w] = [ev]
            self.readers[w] = []
        for r in reads:
            lst = self.readers.setdefault(r, [])
            lst[:] = [e for e in lst if e[0] is not ev[0]]
            lst.append(ev)

    def op(self, eng, fn, reads=(), writes=()):
        self._deps(eng, reads, writes)
        self.cnt[eng] += 1
        sem = self.sem[eng]
        ev = (sem, self.cnt[eng], eng)
        self.q[eng].append(lambda E, fn=fn, sem=sem: fn(E).then_inc(sem, 1))
        self._record(ev, reads, writes)
        return ev

    def pe_group(self, fns, reads=(), writes=()):
        self._deps("pe", reads, writes)
        for fn in fns[:-1]:
            self.q["pe"].append(fn)
        self.cnt["pe"] += 1
        sem = self.sem["pe"]
        ev = (sem, self.cnt["pe"], "pe")
        last = fns[-1]
        self.q["pe"].append(lambda E, fn=last, sem=sem: fn(E).then_inc(sem, 1))
        self._record(ev, reads, writes)
        return ev

    def coll(self, fn, reads=(), writes=()):
        self._deps("pool", reads, writes)
        self.coll_cnt += 1
        n = self.coll_cnt
        sem = self.cc_sem
        self.q["pool"].append(lambda E, fn=fn, sem=sem: fn(E).then_inc(sem, 1))
        self.q["pool"].append(lambda E, sem=sem, n=n: E.wait_ge(sem, n))
        self.cnt["pool"] += 1
        psem = self.sem["pool"]
        ev = (psem, self.cnt["pool"], "pool")
        self.q["pool"].append(lambda E, psem=psem: E.sem_inc(psem, 1))
        self._record(ev, reads, writes)
        return ev

    def dma(self, qn, fn, reads=(), writes=(), group=None):
        self._deps(qn, reads, writes, group)
        i = self.slot_rr[qn]
        self.slot_rr[qn] = (i + 1) % len(self.slots[qn])
        slot = self.slots[qn][i]
        sem = slot[0]
        if slot[1] > 0:
            self._wait(qn, (sem, 16 * slot[1], None))
        slot[1] += 1
        ev = (sem, 16 * slot[1], None, group)
        self.q[qn].append(lambda E, fn=fn, sem=sem: fn(E).then_inc(sem, 16))
        self._record(ev, reads, writes)
        return ev

    def _sp_wait_all(self):
        for q2 in self.slots:
            for sem, c in self.slots[q2]:
                if c > 0:
                    self._wait("sp", (sem, 16 * c, None))
        for e in ENGS:
            if self.cnt[e] > 0 and e != "sp":
                self._wait("sp", (self.sem[e], self.cnt[e], e))

    def barrier(self):
        self._sp_wait_all()
        self.cnt["sp"] += 1
        sem = self.sem["sp"]
        ev = (sem, self.cnt["sp"], "sp")
        self.q["sp"].append(lambda E, sem=sem: E.sem_inc(sem, 1))
        for e in ENGS:
            if e != "sp":
                self._wait(e, ev)

    def finish(self):
        self._sp_wait_all()

    def run(self, block):
        q = self.q

        @block.tensor
        def _(E):
            for f in q["pe"]:
                f(E)

        @block.scalar
        def _(E):
            for f in q["act"]:
                f(E)

        @block.vector
        def _(E):
            for f in q["dve"]:
                f(E)

        @block.gpsimd
        def _(E):
            for f in q["pool"]:
                f(E)

        @block.sync
        def _(E):
            for f in q["sp"]:
                f(E)


class Cfg:
    def __init__(self, D=2048, H=8, DFF=5632, T=2048, NL=4, NCORES=8, SEQ=8192, BATCH=2):
        self.D, self.H, self.DFF, self.T, self.NL = D, H, DFF, T, NL
        self.NCORES, self.SEQ, self.BATCH = NCORES, SEQ, BATCH
        self.G = SEQ // T
        assert self.G * BATCH == NCORES
        self.KC = D // 128
        self.QK = H * 128
        self.RV = H * 256
        self.RC = self.RV // 128
        self.SG = D // 256
        self.NT = T // 128
        self.NB = T // 512
        self.FC = DFF // 128
        self.INC = 2 * self.QK + 2 * self.RV + 4 * D
        self.oq, self.ok = 0, self.QK
        self.ov = 2 * self.QK
        self.og = self.ov + self.RV
        self.osu = self.og + self.RV
        self.osv = self.osu + D
        self.oga = self.osv + D
        self.ogb = self.oga + D


def _tables(cfg, core):
    H, T, NT, G = cfg.H, cfg.T, cfg.NT, cfg.G
    rank = core % G
    pos0 = rank * T
    half = 64
    inv = (10000.0 ** (-np.arange(half, dtype=np.float32) / np.float32(half))).astype(np.float32)
    pos = (pos0 + np.arange(T)).astype(np.float32)
    ang = (pos[None, :] * inv[:, None]).astype(np.float32)
    cos = np.cos(ang).astype(np.float32)
    sin = np.sin(ang).astype(np.float32)
    tb_cos = np.concatenate([cos, cos], 0)
    tb_sin = np.concatenate([-sin, sin], 0)
    logg = np.log1p(-(2.0 ** (-5.0 - np.arange(H, dtype=np.float64))))
    p = np.arange(128, dtype=np.float64)
    n = np.arange(NT, dtype=np.float64)
    loc = n[None, :, None] * 128 + p[:, None, None]
    kdec = np.exp(-logg[None, None, :] * loc)
    odec = np.exp(logg[None, None, :] * loc) * (128.0 ** -0.5)
    coef = np.zeros((128, G, H), np.float64)
    for s in range(G):
        if s < rank:
            coef[:, s, :] = np.exp(logg * T * (rank - s - 1))[None, :]
    oneh = np.zeros((128, G, H), np.float64)
    oneh[:, rank, :] = np.exp(logg * T)[None, :]
    gT = np.broadcast_to(np.exp(logg * T)[None, :], (128, H))
    j = np.arange(128)[:, None]
    i = np.arange(128)[None, :]
    cj, ci = j // 64, i // 64
    mask = np.zeros((128, H, 128), np.float64)
    for h in range(H):
        m = np.where(i >= j, 1.0, np.exp(logg[h] * 2.0 * (j - i)))
        m = np.where(cj > ci, 0.0, m)
        m = np.where(cj < ci, 1.0, m)
        mask[:, h, :] = m
    cmask = (cj <= ci).astype(np.float32)
    ident = np.eye(128, dtype=np.float32)
    pm = np.zeros((128, 128), np.float32)
    for d in range(128):
        pm[(d + 64) % 128, d] = 1.0
    f = lambda a: np.ascontiguousarray(np.asarray(a, dtype=np.float32))
    return {
        "tb_cos": f(tb_cos), "tb_sin": f(tb_sin),
        "tb_kdec": f(kdec.reshape(128, NT * H)), "tb_odec": f(odec.reshape(128, NT * H)),
        "tb_coef": f(coef.reshape(128, G * H)), "tb_oneh": f(oneh.reshape(128, G * H)), "tb_gT": f(gT),
        "tb_mask": f(mask.reshape(128, H * 128)), "tb_cmask": f(cmask),
        "tb_ident": f(ident), "tb_pm": f(pm), "tb_ones": np.ones((128, 128), np.float32),
    }


def _build(cfg):
    c = cfg
    D, H, T, NL, KC, NT, NB, FC, RC, SG, G, DFF = c.D, c.H, c.T, c.NL, c.KC, c.NT, c.NB, c.FC, c.RC, c.SG, c.G, c.DFF
    nc = bass.Bass("TRN2", target_bir_lowering=False)

    def din(name, shape):
        return nc.dram_tensor(name, list(shape), F32, kind="ExternalInput").ap()

    def dscr(name, shape, dt):
        return nc.dram_tensor(name, list(shape), dt, kind="Internal").ap()

    x_in = din("x", [T, D])
    w_in = din("w_in", [NL, D, c.INC])
    ret_proj = din("ret_proj", [NL, c.RV, D])
    sgu_proj = din("sgu_proj", [NL, D, D])
    w_out = din("w_out", [NL, D, D])
    w_ffn_in = din("w_ffn_in", [NL, D, 2 * DFF])
    w_ffn_out = din("w_ffn_out", [NL, DFF, D])
    nw1_d = din("norm_mix_w", [NL, 128, KC])
    nw2_d = din("norm_ffn_w", [NL, 128, KC])
    nwf_d = din("final_norm_w", [128, KC])
    gnw_d = din("ret_gn_w", [NL, 128, RC])
    lnw_d = din("sgu_ln_w", [NL, D])
    lnb_d = din("sgu_ln_b", [NL, D])
    ws_d = din("sgu_w_s", [NL, SG, 128, 128])
    bs_d = din("sgu_b_s", [NL, SG * 128])
    tbs = {}
    for nm, shp in (("tb_cos", [128, T]), ("tb_sin", [128, T]), ("tb_kdec", [128, NT * H]),
                    ("tb_odec", [128, NT * H]), ("tb_coef", [128, G * H]), ("tb_oneh", [128, G * H]),
                    ("tb_gT", [128, H]), ("tb_mask", [128, H * 128]), ("tb_cmask", [128, 128]),
                    ("tb_ident", [128, 128]), ("tb_pm", [128, 128]), ("tb_ones", [128, 128])):
        tbs[nm] = din(nm, shp)
    y_out = nc.dram_tensor("y", [T, D], F32, kind="ExternalOutput").ap()

    XT = dscr("XT", [KC, 128, T], F32)
    ZQ = dscr("ZQ", [H, 128, T], BF16)
    ZK = dscr("ZK", [H, 128, T], BF16)
    ZVv = dscr("ZVv", [128, NT, c.RV], BF16)
    ZG = dscr("ZG", [128, NT, c.RV], BF16)
    ZU = dscr("ZU", [KC, 128, T], BF16)
    ZS = dscr("ZS", [128, NT, D], BF16)
    GA = dscr("GA", [KC, 128, T], BF16)
    GB = dscr("GB", [KC, 128, T], BF16)
    GT = dscr("GT", [RC, 128, T], BF16)
    ST = dscr("ST", [KC, 128, T], BF16)
    AT = dscr("AT", [FC, 128, T], BF16)
    EXI = dscr("EXI", [G * H * 128, 256], F32)
    EXO = dscr("EXO", [G * H * 128, 256], F32)

    WBC = 256
    ARENA = 65536
    with ExitStack() as st:
        fw = Fw(nc, st)
        sb = lambda name, shape, dt: st.enter_context(nc.sbuf_tensor(name, list(shape), dt))
        arena = sb("arena", [128, ARENA], BF16)
        KH = FC // 2
        assert FC % 2 == 0
        wb = [sb("wb%d" % i, [128, max(KC, KH) * WBC], BF16) for i in range(2)]
        WCTX = dict(slots=[w[:] for w in wb], wbc=WBC, key="wb")
        kdec_t = sb("kdec_t", [128, NT * H], F32)
        odec_t = sb("odec_t", [128, NT * H], F32)
        coef_t = sb("coef_t", [128, G * H], F32)
        oneh_t = sb("oneh_t", [128, G * H], F32)
        gT_t = sb("gT_t", [128, H], F32)
        mask_t = sb("mask_t", [128, H * 128], F32)
        cmask_t = sb("cmask_t", [128, 128], F32)
        ident_b = sb("ident_b", [128, 128], BF16)
        ident_f = sb("ident_f", [128, 128], F32)
        pm_b = sb("pm_b", [128, 128], BF16)
        ones_b = sb("ones_b", [128, 128], BF16)
        nw1_t = sb("nw1_t", [128, NL * KC], F32)
        nw2_t = sb("nw2_t", [128, NL * KC], F32)
        nwf_t = sb("nwf_t", [128, KC], F32)
        gnw_t = sb("gnw_t", [128, NL * RC], F32)
        lnw_t = sb("lnw_t", [128, D], BF16)
        lnb_t = sb("lnb_t", [128, D], BF16)
        wsb_t = sb("wsb_t", [128, SG * 128], BF16)
        wT_t = sb("wT_t", [128, SG * 128], BF16)
        bsf_t = sb("bsf_t", [1, SG * 128], F32)
        bsh_t = sb("bsh_t", [1, SG * 128], BF16)
        bsr_t = sb("bsr_t", [1, SG * 128], F32)
        bsl_t = sb("bsl_t", [1, SG * 128], BF16)
        NSB = 4
        stg_b = [sb("stgb%d" % i, [128, 512], BF16) for i in range(NSB)]
        stg_f = [sb("stgf%d" % i, [128, 512], F32) for i in range(3)]
        ld_b = [sb("ldb%d" % i, [128, 512], BF16) for i in range(2)]
        ld_f = [sb("ldf%d" % i, [128, 512], F32) for i in range(2)]
        rstd_t = sb("rstd_t", [128, 512], F32)
        small = sb("small", [128, 256], F32)
        s0_t = sb("s0_t", [128, 256], F32)
        psG = [st.enter_context(nc.psum_tensor("psG%d" % i, [128, 512], F32)) for i in range(4)]
        psT = [st.enter_context(nc.psum_tensor("psT%d" % i, [128, 1024], BF16)) for i in range(2)]
        psX = st.enter_context(nc.psum_tensor("psX", [128, 512], F32))
        psS = st.enter_context(nc.psum_tensor("psS", [128, 512], F32))
        block = st.enter_context(nc.Block())

        rr = {"g": 0, "g3": 0, "sb": 0, "sf": 0, "lb": 0, "lf": 0, "t": 0}

        def nxt(kind, n):
            i = rr[kind]
            rr[kind] = (i + 1) % n
            return i

        def av(off, n):
            return arena[:, off:off + n]

        def avf(off, n):
            return arena[:, off:off + n].bitcast(F32)

        def mm(ps, lhsT, rhs, start, stop):
            return lambda E: E.matmul(ps, lhsT=lhsT, rhs=rhs, start=start, stop=stop)

        def trp(out, in_, ident):
            return lambda E: E.transpose(out=out, in_=in_, identity=ident)

        def actf(out, in_, func, scale=1.0, bias=0.0):
            return lambda E: E.activation(out=out, in_=in_, func=func, bias=bias, scale=scale)

        def tt(out, in0, in1, op):
            return lambda E: E.tensor_tensor(out=out, in0=in0, in1=in1, op=op)

        def stt(out, in0, scalar, in1, op0, op1):
            return lambda E: E.scalar_tensor_tensor(out=out, in0=in0, scalar=scalar, in1=in1, op0=op0, op1=op1)

        def tsc(out, in0, s1, s2, op0, op1):
            return lambda E: E.tensor_scalar(out=out, in0=in0, scalar1=s1, scalar2=s2, op0=op0, op1=op1)

        gid = [0]

        def newgroup():
            gid[0] += 1
            return gid[0]

        def ld(qn, out, in_, reads=(), writes=(), group=None):
            return fw.dma(qn, lambda E: E.dma_start(out=out, in_=in_), reads=reads, writes=writes, group=group)

        ld("pool", ident_b[:], tbs["tb_ident"], writes=["identb"])
        ld("pool", pm_b[:], tbs["tb_pm"], writes=["pmb"])
        ld("pool", ones_b[:], tbs["tb_ones"], writes=["onesb"])
        ld("sp", ident_f[:], tbs["tb_ident"], writes=["identf"])
        ld("sp", kdec_t[:], tbs["tb_kdec"], writes=["kdec"])
        ld("sp", odec_t[:], tbs["tb_odec"], writes=["odec"])
        ld("sp", coef_t[:], tbs["tb_coef"], writes=["coef"])
        ld("sp", oneh_t[:], tbs["tb_oneh"], writes=["oneh"])
        ld("sp", gT_t[:], tbs["tb_gT"], writes=["gT"])
        ld("sp", mask_t[:], tbs["tb_mask"], writes=["mask"])
        ld("sp", cmask_t[:], tbs["tb_cmask"], writes=["cmask"])
        ld("sp", nw1_t[:].rearrange("p (l k) -> p l k", l=NL), nw1_d.rearrange("l p k -> p l k"), writes=["nw1"])
        ld("sp", nw2_t[:].rearrange("p (l k) -> p l k", l=NL), nw2_d.rearrange("l p k -> p l k"), writes=["nw2"])
        ld("sp", gnw_t[:].rearrange("p (l k) -> p l k", l=NL), gnw_d.rearrange("l p k -> p l k"), writes=["gnw"])
        ld("sp", nwf_t[:], nwf_d, writes=["nwf"])
        CONST_KEYS = ["cos", "sin", "identb", "pmb", "onesb", "identf", "kdec", "odec", "coef", "oneh", "gT",
                      "mask", "cmask", "nw1", "nw2", "gnw", "nwf"]

        def ingest():
            XS = 0
            for n in range(NT):
                s = n % 2
                xt = avf(XS + s * 2 * D, 2 * D)
                ld("sp", xt, x_in[n * 128:(n + 1) * 128, :], writes=[("xs", s)])
                for k0 in range(0, KC, 4):
                    gi = nxt("g", 4)
                    ps = psG[gi]
                    fw.pe_group([trp(ps[:, j * 128:(j + 1) * 128], xt[:, (k0 + j) * 128:(k0 + j + 1) * 128], ident_f[:])
                                 for j in range(4)], reads=[("xs", s), "identf"], writes=[("psG", gi)])
                    si = nxt("sf", 3)
                    fw.op("act", actf(stg_f[si][:], ps[:], AF.Copy), reads=[("psG", gi)], writes=[("stgf", si)])
                    ld("pool", XT[k0:k0 + 4, :, n * 128:(n + 1) * 128].rearrange("k p t -> p k t"),
                       stg_f[si][:].rearrange("p (k t) -> p k t", k=4), reads=[("stgf", si)], writes=["XT"], group="g_XT")

        BUFA = 0
        BUFB = KC * T

        def hT(k, t0, n):
            return arena[:, BUFA + k * T + t0: BUFA + k * T + t0 + n]

        def mT(k, t0, n):
            return arena[:, BUFB + k * T + t0: BUFB + k * T + t0 + n]

        def norm_pass(nw_ap_fn, final=False):
            XB = BUFB
            assert XB + 2 * KC * 1024 <= ARENA
            for tb in range(NB):
                xbk = ("xb", tb % 2)
                xb = avf(XB + (tb % 2) * KC * 1024, KC * 1024).rearrange("p (k t) -> p k t", k=KC)
                ld("sp", xb, XT[:, :, tb * 512:(tb + 1) * 512].rearrange("k p t -> p k t"),
                   reads=["XT"], writes=[xbk])
                fns = []
                for k in range(KC):
                    si = nxt("sb", NSB)
                    fw.op("act", actf(stg_b[si][:], xb[:, k, :], AF.Square), reads=[xbk], writes=[("stgb", si)])
                    fw.pe_group([mm(psX[:], ones_b[:], stg_b[si][:], k == 0, k == KC - 1)],
                                reads=[("stgb", si), "onesb"], writes=["psX"])
                fw.op("act", actf(rstd_t[:], psX[:], AF.Sqrt, scale=1.0 / D, bias=EPS), reads=["psX"], writes=["rstd"])
                fw.op("dve", lambda E: E.reciprocal(out=rstd_t[:], in_=rstd_t[:]), reads=["rstd"], writes=["rstd"])
                if not final:
                    for k in range(KC):
                        fw.op("dve", stt(hT(k, tb * 512, 512), xb[:, k, :], nw_ap_fn(k), rstd_t[:], ALU.mult, ALU.mult),
                              reads=[xbk, "rstd", "nw1", "nw2"], writes=["bufA"])
                else:
                    for k in range(KC):
                        fw.op("dve", stt(xb[:, k, :], xb[:, k, :], nw_ap_fn(k), rstd_t[:], ALU.mult, ALU.mult),
                              reads=[xbk, "rstd", "nwf"], writes=[xbk])
                    for tl in range(4):
                        n = tb * 4 + tl
                        s = n % 2
                        yt = avf(BUFA + s * 2 * D, 2 * D)
                        for k0 in range(0, KC, 4):
                            gi = nxt("g", 4)
                            ps = psG[gi]
                            fw.pe_group([trp(ps[:, j * 128:(j + 1) * 128], xb[:, k0 + j, tl * 128:(tl + 1) * 128], ident_f[:])
                                         for j in range(4)], reads=[xbk, "identf"], writes=[("psG", gi)])
                            fw.op("act", actf(yt[:, k0 * 128:(k0 + 4) * 128], ps[:], AF.Copy),
                                  reads=[("psG", gi)], writes=[("yt", s)])
                        ld("pool", y_out[n * 128:(n + 1) * 128, :], yt, reads=[("yt", s)], writes=["y"], group="g_y")

        pre = {"done": None}

        def prefetch_w(wload, tag):
            wload(0, 0)
            pre["done"] = tag

        def gemm(Xfn, kc, nblk, wload, units_of_block, wctx=None, tag=None, filler=None, filler_from=0):
            wctx = wctx or WCTX
            deferred = []
            if tag is not None and pre["done"] == tag:
                pre["done"] = None
            else:
                wload(0, 0)
            for blk in range(nblk):
                slot = blk % 2
                if blk + 1 < nblk:
                    wload(blk + 1, (blk + 1) % 2)
                for u in units_of_block(blk, slot):
                    gi = nxt("g", 4)
                    ps = psG[gi][:, 0:u["n"]]
                    fw.pe_group([mm(ps, u["lhs"](k), u["rhs"](k), k == 0, k == kc - 1) for k in range(kc)],
                                reads=[(wctx["key"], slot), "bufA", "bufB"], writes=[("psG", gi)])
                    for dfn in deferred:
                        dfn()
                    deferred = []
                    d = u["epi"](ps, gi)
                    if d is not None:
                        deferred.append(d)
                    if filler is not None and blk >= filler_from:
                        next(filler, None)
            for dfn in deferred:
                dfn()
            if filler is not None:
                for _ in filler:
                    pass

        def wslice(slot, k, c0, n, wctx=None):
            wctx = wctx or WCTX
            return wctx["slots"][slot][:, k * wctx["wbc"] + c0: k * wctx["wbc"] + c0 + n]

        def wload_cols(Wl, kc_total, col_of_blk, ncols=None, dst_off=0, wctx=None):
            wctx = wctx or WCTX
            wbc = wctx["wbc"]
            ncols = ncols or wbc

            def f(blk, slot):
                col = col_of_blk(blk)
                src = Wl.rearrange("(k p) n -> p k n", p=128)
                dst = wctx["slots"][slot].rearrange("p (k n) -> p k n", n=wbc)
                step = 4
                grp = newgroup()
                for k0 in range(0, kc_total, step):
                    k1 = min(kc_total, k0 + step)
                    ld("pool", dst[:, k0:k1, dst_off:dst_off + ncols], src[:, k0:k1, col:col + ncols],
                       writes=[(wctx["key"], slot)], group=grp)
            return f

        def phase1(l):
            Wl = w_in[l]
            W1 = 512 if c.QK % 512 == 0 else 256
            W1S = KC * W1
            w1ctx = dict(slots=[arena[:, BUFB + i * W1S: BUFB + (i + 1) * W1S] for i in range(2)], wbc=W1, key="wb1")
            CS = BUFB + 2 * W1S
            assert CS + 2 * T <= ARENA
            cos_v = arena[:, CS:CS + T]
            sin_v = arena[:, CS + T:CS + 2 * T]
            ld("pool", cos_v, tbs["tb_cos"], writes=["cos"])
            ld("pool", sin_v, tbs["tb_sin"], writes=["sin"])
            nblk = c.INC // W1
            kv = [b for b in range(nblk) if c.ok <= b * W1 < c.og]
            order = kv + [b for b in range(nblk) if b not in kv]
            PA0 = CS + 2 * T
            assert PA0 + T + NT * 256 + 512 + 4 * G * 256 <= ARENA, "arena overflow (pass A inside phase 1)"

            def kind_of(col):
                if col < c.ok:
                    return "q"
                if col < c.ov:
                    return "k"
                if col < c.og:
                    return "v"
                if col < c.osu:
                    return "g"
                if col < c.osv:
                    return "su"
                if col < c.oga:
                    return "sv"
                if col < c.ogb:
                    return "ga"
                return "gb"

            def units(blk, slot):
                col = order[blk] * W1
                kind = kind_of(col)
                us = []
                if kind in ("v", "g", "sv"):
                    base = {"v": c.ov, "g": c.og, "sv": c.osv}[kind]
                    dst = {"v": ZVv, "g": ZG, "sv": ZS}[kind]
                    func = {"v": AF.Copy, "g": AF.Silu, "sv": AF.Gelu}[kind]
                    for n in range(NT):
                        def epi(ps, gi, n=n, dst=dst, func=func, c0=col - base):
                            si = nxt("sb", NSB)
                            fw.op("act", actf(stg_b[si][:, 0:W1], ps, func), reads=[("psG", gi)], writes=[("stgb", si)])
                            ld("pool", dst[:, n, c0:c0 + W1], stg_b[si][:, 0:W1], reads=[("stgb", si)], writes=["Z"], group="g_Z")
                            return None
                        us.append(dict(lhs=lambda k, n=n: hT(k, n * 128, 128),
                                       rhs=lambda k, slot=slot: wslice(slot, k, 0, W1, w1ctx), n=W1, epi=epi))
                else:
                    base = {"q": c.oq, "k": c.ok, "su": c.osu, "ga": c.oga, "gb": c.ogb}[kind]
                    for m in range(W1 // 128):
                        ch = (col - base) // 128 + m
                        for tb in range(NB):
                            if kind in ("q", "k"):
                                dst = ZQ if kind == "q" else ZK

                                def epi(ps, gi, ch=ch, tb=tb, dst=dst):
                                    si = nxt("sb", NSB)
                                    zb = stg_b[si]
                                    fw.op("act", actf(zb[:], ps, AF.Copy), reads=[("psG", gi)], writes=[("stgb", si)])

                                    def later():
                                        fw.pe_group([mm(psX[:], pm_b[:], zb[:], True, True)],
                                                    reads=[("stgb", si), "pmb"], writes=["psX"])
                                        f1 = nxt("sf", 3)
                                        fw.op("dve", tt(stg_f[f1][:], zb[:], cos_v[:, tb * 512:(tb + 1) * 512], ALU.mult),
                                              reads=[("stgb", si), "cos"], writes=[("stgf", f1)])
                                        f2 = nxt("sf", 3)
                                        fw.op("dve", tt(stg_f[f2][:], psX[:], sin_v[:, tb * 512:(tb + 1) * 512], ALU.mult),
                                              reads=["psX", "sin"], writes=[("stgf", f2)])
                                        so = nxt("sb", NSB)
                                        fw.op("dve", tt(stg_b[so][:], stg_f[f1][:], stg_f[f2][:], ALU.add),
                                              reads=[("stgf", f1), ("stgf", f2)], writes=[("stgb", so)])
                                        ld("pool", dst[ch, :, tb * 512:(tb + 1) * 512], stg_b[so][:],
                                           reads=[("stgb", so)], writes=["Z"], group="g_Z")
                                    return later
                            else:
                                dst = {"su": ZU, "ga": GA, "gb": GB}[kind]
                                func = AF.Gelu if kind == "su" else AF.Sigmoid

                                def epi(ps, gi, ch=ch, tb=tb, dst=dst, func=func):
                                    si = nxt("sb", NSB)
                                    fw.op("act", actf(stg_b[si][:], ps, func), reads=[("psG", gi)], writes=[("stgb", si)])
                                    ld("pool", dst[ch, :, tb * 512:(tb + 1) * 512], stg_b[si][:],
                                       reads=[("stgb", si)], writes=["Z"], group="g_Z")
                                    return None
                            us.append(dict(lhs=lambda k, slot=slot, m=m: wslice(slot, k, m * 128, 128, w1ctx),
                                           rhs=lambda k, tb=tb: hT(k, tb * 512, 512), n=512, epi=epi))
                return us

            gemm(hT, KC, nblk, wload_cols(Wl, KC, lambda blk: order[blk] * W1, wctx=w1ctx), units, wctx=w1ctx,
                 filler=pass_a_gen(PA0), filler_from=len(kv) + 1)

        R_SLOT = 2 * T + 2 * NT * 256
        R0 = 0
        RTMP = R0 + 2 * R_SLOT
        RTMP_SZ = 14336
        PA_END = 512 + 4 * G * 256
        O_OFF = RTMP + RTMP_SZ
        GTH = O_OFF + 2 * 2 * NT * 256
        assert GTH + 4 * T <= ARENA, "arena overflow (retention)"
        S_OFF = RTMP + PA_END
        assert S_OFF + 4 * D + 2 * D + KC * 512 * 2 <= ARENA, "arena overflow (sgu)"

        def load_layer_small(l):
            ld("pool", lnw_t[:], lnw_d[l:l + 1, :].partition_broadcast(128), writes=["lnw"])
            ld("pool", lnb_t[:], lnb_d[l:l + 1, :].partition_broadcast(128), writes=["lnb"])
            ld("pool", wsb_t[:].rearrange("p (g j) -> p g j", g=SG), ws_d[l].rearrange("g i j -> i g j"), writes=["wsb"])
            ld("sp", bsf_t[:], bs_d[l:l + 1, :], writes=["bsf"])
            for g in range(SG):
                ti = nxt("t", 2)
                fw.pe_group([trp(psT[ti][:, 0:128], wsb_t[:, g * 128:(g + 1) * 128], ident_b[:])],
                            reads=["wsb", "identb"], writes=[("psT", ti)])
                fw.op("dve", tt(wT_t[:, g * 128:(g + 1) * 128], psT[ti][:, 0:128], cmask_t[:], ALU.mult),
                      reads=[("psT", ti), "cmask"], writes=["wT"])
            fw.op("dve", lambda E: E.tensor_copy(out=bsh_t[:], in_=bsf_t[:]), reads=["bsf"], writes=["bsh"])
            fw.op("dve", tt(bsr_t[:], bsf_t[:], bsh_t[:], ALU.subtract), reads=["bsf", "bsh"], writes=["bsr"])
            fw.op("dve", lambda E: E.tensor_copy(out=bsl_t[:], in_=bsr_t[:]), reads=["bsr"], writes=["bsl"])

        def head_views(s):
            b = R0 + s * R_SLOT
            qT = av(b, T)
            kT = av(b + T, T)
            v = av(b + 2 * T, NT * 256)
            sg = av(b + 2 * T + NT * 256, NT * 256)
            return qT, kT, v, sg

        def kt_make(h, n, kT, s, pi=0):
            ti = nxt("t", 2)
            fw.pe_group([trp(psT[ti][:, 0:128], kT[:, n * 128:(n + 1) * 128], ident_b[:])],
                        reads=[("hk", s), "identb"], writes=[("psT", ti)])
            kk = pi * 2 + n % 2
            kt = av(RTMP + kk * 128, 128)
            fw.op("act", actf(kt, psT[ti][:, 0:128], AF.Copy, scale=kdec_t[:, n * H + h:n * H + h + 1]),
                  reads=[("psT", ti), "kdec"], writes=[("kt", kk)])
            return kt, ("kt", kk)

        def pass_a_gen(PA0):
            exg = newgroup()
            kT = av(PA0, T)
            v = av(PA0 + T, NT * 256)
            KT0 = PA0 + T + NT * 256
            EX0 = KT0 + 512
            acc = psS[:, 0:256]
            acck = ("psS", 0)
            for h in range(H):
                ld("sp", kT, ZK[h], reads=["Z"], writes=["pa_k"])
                ld("sp", v.rearrange("p (n e) -> p n e", n=NT), ZVv[:, :, h * 256:(h + 1) * 256], reads=["Z"], writes=["pa_v"])
                prev = None
                for n in range(NT):
                    if prev is not None:
                        pk, pn = prev
                        fw.pe_group([mm(acc, pk, v[:, pn * 256:(pn + 1) * 256], pn == 0, pn == NT - 1)],
                                    reads=[("pa_kt", pn % 2), "pa_v"], writes=[acck])
                    ti = nxt("t", 2)
                    fw.pe_group([trp(psT[ti][:, 0:128], kT[:, n * 128:(n + 1) * 128], ident_b[:])],
                                reads=["pa_k", "identb"], writes=[("psT", ti)])
                    kt = av(KT0 + (n % 2) * 128, 128)
                    fw.op("act", actf(kt, psT[ti][:, 0:128], AF.Copy, scale=kdec_t[:, n * H + h:n * H + h + 1]),
                          reads=[("psT", ti), "kdec"], writes=[("pa_kt", n % 2)])
                    prev = (kt, n)
                    yield
                pk, pn = prev
                fw.pe_group([mm(acc, pk, v[:, pn * 256:(pn + 1) * 256], pn == 0, pn == NT - 1)],
                            reads=[("pa_kt", pn % 2), "pa_v"], writes=[acck])
                ex = avf(EX0 + (h % 2) * 2 * G * 256, 2 * G * 256)
                for sl in range(G):
                    fw.op("dve", lambda E, ex=ex, sl=sl, h=h: E.tensor_scalar_mul(
                        out=ex[:, sl * 256:(sl + 1) * 256], in0=acc, scalar1=oneh_t[:, sl * H + h: sl * H + h + 1]),
                        reads=[acck, "oneh"], writes=[("ex", h % 2)])
                ld("pool", EXI.rearrange("(g h p) e -> p g h e", g=G, h=H)[:, :, h, :],
                   ex.rearrange("p (g e) -> p g e", g=G), reads=[("ex", h % 2)], writes=["EXI"], group=exg)
                yield

        def exchange():
            rgs = [list(range(b * G, (b + 1) * G)) for b in range(c.BATCH)]
            fw.coll(lambda E: E.collective_compute(
                "AllReduce", ALU.add, replica_groups=rgs,
                ins=[EXI.opt()], outs=[EXO.opt()]), reads=["EXI"], writes=["EXO"])

        def sgu(l):
            TBS = 2
            TW = TBS * 128
            ZV_O = S_OFF
            TMP_O = ZV_O + 2 * TBS * D
            ZU_O = TMP_O + 2 * D
            OUT_O = ZU_O + 2 * KC * TW
            assert OUT_O + 2 * KC * TW <= ARENA, "arena overflow (sgu)"
            nb = NT // TBS
            nchk = max(1, D // 512)
            assert 32 + 2 * TBS * 6 * nchk <= 256

            def bufs(b):
                p = b % 2
                zv = av(ZV_O + p * TBS * D, TBS * D)
                zu = av(ZU_O + p * KC * TW, KC * TW).rearrange("p (k t) -> p k t", k=KC)
                out = av(OUT_O + p * KC * TW, KC * TW).rearrange("p (k t) -> p k t", k=KC)
                return p, zv, zu, out

            def loads(b):
                p, zv, zu, out = bufs(b)
                ld("sp", zv.rearrange("p (n d) -> p n d", n=TBS), ZS[:, b * TBS:(b + 1) * TBS, :],
                   writes=[("zv", p, tl) for tl in range(TBS)])
                ld("sp", zu, ZU[:, :, b * TW:(b + 1) * TW].rearrange("k p t -> p k t"), writes=[("zu", p)])

            loads(0)
            for b in range(nb):
                if b + 1 < nb:
                    loads(b + 1)
                p, zv, zu, out = bufs(b)
                vn = zv
                mv = small[:, p * 8: p * 8 + 2 * TBS]
                rs = small[:, 16 + p * 4: 16 + p * 4 + TBS]
                for tl in range(TBS):
                    so = 32 + (p * TBS + tl) * 6 * nchk
                    stats = small[:, so: so + 6 * nchk]
                    for cc in range(nchk):
                        w = min(512, D)
                        fw.op("dve", lambda E, tl=tl, cc=cc, stats=stats, w=w, zv=zv: E.bn_stats(
                            out=stats[:, cc * 6:(cc + 1) * 6], in_=zv[:, tl * D + cc * w: tl * D + (cc + 1) * w]),
                            reads=[("zv", p, tl)], writes=[("sgst", p, tl)])
                    fw.op("dve", lambda E, tl=tl, stats=stats, mv=mv: E.bn_aggr(
                        out=mv[:, tl * 2:tl * 2 + 2], in_=stats.rearrange("p (c s) -> p c s", s=6)),
                        reads=[("sgst", p, tl)], writes=[("sgmv", p)])
                mv3 = mv.rearrange("p (t two) -> p t two", two=2)
                fw.op("act", actf(rs, mv3[:, :, 1], AF.Sqrt, bias=EPS), reads=[("sgmv", p)], writes=[("sgrs", p)])
                fw.op("dve", lambda E, rs=rs: E.reciprocal(out=rs, in_=rs), reads=[("sgrs", p)], writes=[("sgrs", p)])
                for tl in range(TBS):
                    tmp = avf(TMP_O, 2 * D)
                    fw.op("dve", stt(tmp, zv[:, tl * D:(tl + 1) * D], mv[:, tl * 2:tl * 2 + 1], lnw_t[:],
                                     ALU.subtract, ALU.mult), reads=[("zv", p, tl), ("sgmv", p), "lnw"], writes=["sgtmp"])
                    fw.op("dve", stt(vn[:, tl * D:(tl + 1) * D], tmp, rs[:, tl:tl + 1], lnb_t[:], ALU.mult, ALU.add),
                          reads=["sgtmp", ("sgrs", p), "lnb"], writes=[("zv", p, tl)])
                for tl in range(TBS):
                    for k0 in range(0, KC, 4):
                        gi = nxt("g", 4)
                        ps = psG[gi]
                        fns = []
                        for j in range(4):
                            k = k0 + j
                            g = k // 2
                            o = ps[:, j * 128:(j + 1) * 128]
                            fns.append(mm(o, vn[:, tl * D + k * 128: tl * D + (k + 1) * 128], wT_t[:, g * 128:(g + 1) * 128], True, False))
                            fns.append(mm(o, ones_b[0:1, :], bsh_t[0:1, g * 128:(g + 1) * 128], False, False))
                            fns.append(mm(o, ones_b[0:1, :], bsl_t[0:1, g * 128:(g + 1) * 128], False, True))
                        fw.pe_group(fns, reads=[("zv", p, tl), "wT", "bsh", "bsl", "onesb"], writes=[("psG", gi)])
                        fw.op("dve", tt(out[:, k0:k0 + 4, tl * 128:(tl + 1) * 128],
                                        ps[:].rearrange("p (k t) -> p k t", k=4),
                                        zu[:, k0:k0 + 4, tl * 128:(tl + 1) * 128], ALU.mult),
                              reads=[("psG", gi), ("zu", p)], writes=[("sgout", p)])
                ld("pool", ST[:, :, b * TW:(b + 1) * TW].rearrange("k p t -> p k t"), out, reads=[("sgout", p)],
                   writes=["ST"], group="g_ST")

        def pass_b(l):
            SIN_HI = RTMP + 512
            SIN_LO = SIN_HI + H * 256
            EXL = SIN_LO + H * 256
            for h in range(H):
                e = h % 2
                exl = avf(EXL + e * 2 * G * 256, 2 * G * 256)
                ld("sp", exl.rearrange("p (g e) -> p g e", g=G),
                   EXO.rearrange("(g h p) e -> p g h e", g=G, h=H)[:, :, h, :], reads=["EXO"], writes=[("exl", e)])
                s0 = s0_t[:]
                fw.op("dve", lambda E, exl=exl, h=h, s0=s0: E.tensor_scalar_mul(out=s0, in0=exl[:, 0:256], scalar1=coef_t[:, h:h + 1]),
                      reads=[("exl", e), "coef"], writes=["s0"])
                for sl in range(1, G):
                    fw.op("dve", stt(s0, exl[:, sl * 256:(sl + 1) * 256], coef_t[:, sl * H + h: sl * H + h + 1], s0,
                                     ALU.mult, ALU.add), reads=[("exl", e), "coef", "s0"], writes=["s0"])
                hi = av(SIN_HI + h * 256, 256)
                lo = av(SIN_LO + h * 256, 256)
                fw.op("dve", lambda E, hi=hi, s0=s0: E.tensor_copy(out=hi, in_=s0), reads=["s0"], writes=["sinhi"])
                fw.op("dve", tt(s0, s0, hi, ALU.subtract), reads=["s0", "sinhi"], writes=["s0"])
                fw.op("dve", lambda E, lo=lo, s0=s0: E.tensor_copy(out=lo, in_=s0), reads=["s0"], writes=["sinlo"])
            SB_O = EXL + 4 * G * 256
            SC_O = SB_O + 1024
            TF_O = SC_O + 512
            U_O = TF_O + 2048
            assert U_O + 1024 <= RTMP + RTMP_SZ, "RTMP overflow"

            def head_gen(h, pi):
                s = pi
                qT, kT, v, sg = head_views(s)
                ld("sp", qT, ZQ[h], writes=[("hq", s)])
                ld("sp", kT, ZK[h], writes=[("hk", s)])
                ld("sp", v.rearrange("p (n e) -> p n e", n=NT), ZVv[:, :, h * 256:(h + 1) * 256], writes=[("hv", s)])
                ld("sp", sg.rearrange("p (n e) -> p n e", n=NT), ZG[:, :, h * 256:(h + 1) * 256], writes=[("hg", s)])
                if pi == 0:
                    acc, acck = psS[:, 0:256], ("psS", 0)
                else:
                    acc, acck = psG[3][:, 0:256], ("psG", 3)
                fw.pe_group([mm(acc, ident_b[:], av(SIN_HI + h * 256, 256), True, False),
                             mm(acc, ident_b[:], av(SIN_LO + h * 256, 256), False, False)],
                            reads=["sinhi", "sinlo", "identb"], writes=[acck])
                o_all = avf(O_OFF + pi * 2 * NT * 256, 2 * NT * 256)
                mv = small[:, pi * 32: pi * 32 + 2 * NT]
                yield
                for n in range(NT):
                    sbi = pi * 2 + n % 2
                    Sb = av(SB_O + sbi * 256, 256)
                    fw.op("act", actf(Sb, acc, AF.Copy), reads=[acck], writes=[("Sb", sbi)])
                    kt, ktk = kt_make(h, n, kT, s, pi)
                    fw.pe_group([mm(psX[:, 0:128], kT[:, n * 128:(n + 1) * 128], qT[:, n * 128:(n + 1) * 128], True, True)],
                                reads=[("hk", s), ("hq", s)], writes=["psX"])
                    sc = av(SC_O + sbi * 128, 128)
                    fw.op("dve", stt(sc, psX[:, 0:128], kdec_t[:, n * H + h:n * H + h + 1],
                                     mask_t[:, h * 128:(h + 1) * 128], ALU.mult, ALU.mult),
                          reads=["psX", "kdec", "mask"], writes=[("sc", sbi)])
                    yield
                    gi = nxt("g3", 3)
                    po = psG[gi][:, 0:256]
                    fw.pe_group([mm(po, sc, v[:, n * 256:(n + 1) * 256], True, False),
                                 mm(po, qT[:, n * 128:(n + 1) * 128], Sb, False, True)],
                                reads=[("sc", sbi), ("Sb", sbi), ("hv", s), ("hq", s)], writes=[("psG", gi)])
                    fw.pe_group([mm(acc, kt, v[:, n * 256:(n + 1) * 256], False, n == NT - 1)],
                                reads=[ktk, ("hv", s)], writes=[acck])
                    fw.op("act", actf(o_all[:, n * 256:(n + 1) * 256], po, AF.Copy, scale=odec_t[:, n * H + h:n * H + h + 1]),
                          reads=[("psG", gi), "odec"], writes=[("o", pi, n)])
                    st6 = small[:, 64 + sbi * 6: 64 + sbi * 6 + 6]
                    fw.op("dve", lambda E, n=n, st6=st6, o_all=o_all: E.bn_stats(out=st6, in_=o_all[:, n * 256:(n + 1) * 256]),
                          reads=[("o", pi, n)], writes=[("st6", sbi)])
                    fw.op("dve", lambda E, n=n, st6=st6, mv=mv: E.bn_aggr(out=mv[:, 2 * n:2 * n + 2], in_=st6),
                          reads=[("st6", sbi)], writes=[("rmv", pi)])
                    yield
                rs = small[:, 128 + pi * 32:128 + pi * 32 + NT]
                fw.op("act", actf(rs, mv.rearrange("p (t two) -> p t two", two=2)[:, :, 1], AF.Sqrt, bias=EPS),
                      reads=[("rmv", pi)], writes=[("rrs", pi)])
                fw.op("dve", lambda E, rs=rs: E.reciprocal(out=rs, in_=rs), reads=[("rrs", pi)], writes=[("rrs", pi)])
                gth = av(GTH + s * 2 * T, 2 * T)
                yield
                for n in range(NT):
                    ti2 = pi * 2 + n % 2
                    tf = avf(TF_O + ti2 * 512, 512)
                    fw.op("dve", tsc(tf, o_all[:, n * 256:(n + 1) * 256], mv[:, 2 * n:2 * n + 1], rs[:, n:n + 1],
                                     ALU.subtract, ALU.mult), reads=[("o", pi, n), ("rmv", pi), ("rrs", pi)], writes=[("tf", ti2)])
                    u = av(U_O + ti2 * 256, 256)
                    fw.op("dve", tt(u, tf, sg[:, n * 256:(n + 1) * 256], ALU.mult),
                          reads=[("tf", ti2), ("hg", s)], writes=[("u", ti2)])
                    ti = nxt("t", 2)
                    fw.pe_group([trp(psT[ti][:, 0:128], u[:, 0:128], ident_b[:]),
                                 trp(psT[ti][:, 128:256], u[:, 128:256], ident_b[:])],
                                reads=[("u", ti2), "identb"], writes=[("psT", ti)])
                    yield
                    for e2 in range(2):
                        fw.op("act", actf(gth[:, e2 * T + n * 128: e2 * T + (n + 1) * 128], psT[ti][:, e2 * 128:(e2 + 1) * 128],
                                          AF.Copy, scale=gnw_t[:, l * RC + h * 2 + e2: l * RC + h * 2 + e2 + 1]),
                              reads=[("psT", ti), "gnw"], writes=[("gth", s)])
                    yield
                ld("pool", GT[h * 2:h * 2 + 2].rearrange("k p t -> p k t"), gth.rearrange("p (k t) -> p k t", k=2),
                   reads=[("gth", s)], writes=["GT"], group="g_GT")

            for h0 in range(0, H, 2):
                gens = [head_gen(h0 + pi, pi) for pi in range(min(2, H - h0))]
                alive = list(gens)
                while alive:
                    for g_ in list(alive):
                        try:
                            next(g_)
                        except StopIteration:
                            alive.remove(g_)

        def load_bufA(src, nch):
            grp = newgroup()
            for k0 in range(0, nch, 4):
                k1 = min(nch, k0 + 4)
                ld("sp", arena[:, BUFA + k0 * T: BUFA + k1 * T].rearrange("p (k t) -> p k t", t=T),
                   src[k0:k1].rearrange("k p t -> p k t"), writes=["bufA"], group=grp)
            assert nch * T <= ARENA

        def phase3(l):
            load_bufA(GT, RC)

            def mk_units(gate, first):
                def units(blk, slot):
                    us = []
                    for m in range(WBC // 128):
                        ch = blk * (WBC // 128) + m
                        for tb in range(NB):
                            def epi(ps, gi, ch=ch, tb=tb):
                                li = nxt("lb", 2)
                                ld("sp", ld_b[li][:], gate[ch, :, tb * 512:(tb + 1) * 512], writes=[("ldb", li)])
                                if first:
                                    fw.op("dve", tt(mT(ch, tb * 512, 512), ps, ld_b[li][:], ALU.mult),
                                          reads=[("psG", gi), ("ldb", li)], writes=[("mT", ch, tb)])
                                else:
                                    fi = nxt("sf", 3)
                                    fw.op("dve", tt(stg_f[fi][:], ps, ld_b[li][:], ALU.mult),
                                          reads=[("psG", gi), ("ldb", li)], writes=[("stgf", fi)])
                                    fw.op("dve", tt(mT(ch, tb * 512, 512), mT(ch, tb * 512, 512), stg_f[fi][:], ALU.add),
                                          reads=[("stgf", fi), ("mT", ch, tb)], writes=[("mT", ch, tb)])
                                return None
                            us.append(dict(lhs=lambda k, slot=slot, m=m: wslice(slot, k, m * 128, 128),
                                           rhs=lambda k, tb=tb: hT(k, tb * 512, 512), n=512, epi=epi))
                    return us
                return units

            gemm(hT, RC, D // WBC, wload_cols(ret_proj[l], RC, lambda blk: blk * WBC), mk_units(GA, True), tag=("p3a", l))
            prefetch_w(wload_cols(sgu_proj[l], KC, lambda blk: blk * WBC), ("p3b", l))
            fw.barrier()
            load_bufA(ST, KC)
            gemm(hT, KC, D // WBC, wload_cols(sgu_proj[l], KC, lambda blk: blk * WBC), mk_units(GB, False), tag=("p3b", l))
            prefetch_w(wload_cols(w_out[l], KC, lambda blk: blk * WBC), ("p3c", l))
            fw.barrier()
            resid_gemm(mT, KC, w_out[l], tag=("p3c", l))
            prefetch_w(wl4(l), ("p4", l))

        def resid_gemm(Xfn, kc, Wl, tag=None):
            def units(blk, slot):
                us = []
                for m in range(WBC // 128):
                    ch = blk * (WBC // 128) + m
                    for tb in range(NB):
                        def epi(ps, gi, ch=ch, tb=tb):
                            li = nxt("lf", 2)
                            ld("sp", ld_f[li][:], XT[ch, :, tb * 512:(tb + 1) * 512], reads=[("XT", ch, tb)], writes=[("ldf", li)])
                            fi = nxt("sf", 3)
                            fw.op("dve", tt(stg_f[fi][:], ps, ld_f[li][:], ALU.add),
                                  reads=[("psG", gi), ("ldf", li)], writes=[("stgf", fi)])
                            ld("pool", XT[ch, :, tb * 512:(tb + 1) * 512], stg_f[fi][:], reads=[("stgf", fi)],
                               writes=[("XT", ch, tb)])
                            return None
                        us.append(dict(lhs=lambda k, slot=slot, m=m: wslice(slot, k, m * 128, 128),
                                       rhs=lambda k, tb=tb: Xfn(k, tb * 512, 512), n=512, epi=epi))
                return us
            gemm(Xfn, kc, D // WBC, wload_cols(Wl, kc, lambda blk: blk * WBC), units, tag=tag)

        def wl4(l):
            Wl = w_ffn_in[l]
            HB = WBC // 2

            return wload_cols(Wl, KC, lambda b: b * WBC)

        def phase4(l):
            HB = WBC // 2
            nblk = DFF // HB
            wl = wl4(l)

            def units(blk, slot):
                us = []
                for m in range(HB // 128):
                    ch = blk * (HB // 128) + m
                    for tb in range(NB):
                        hold = {}

                        def epi_a(ps, gi, hold=hold):
                            fi = nxt("sf", 3)
                            fw.op("act", actf(stg_f[fi][:], ps, AF.Silu), reads=[("psG", gi)], writes=[("stgf", fi)])
                            hold["fi"] = fi
                            return None

                        def epi_c(ps, gi, hold=hold, ch=ch, tb=tb):
                            fi = hold["fi"]
                            si = nxt("sb", NSB)
                            fw.op("dve", tt(stg_b[si][:], ps, stg_f[fi][:], ALU.mult),
                                  reads=[("psG", gi), ("stgf", fi)], writes=[("stgb", si)])
                            ld("pool", AT[ch, :, tb * 512:(tb + 1) * 512], stg_b[si][:], reads=[("stgb", si)], writes=["AT"], group="g_AT")
                            return None
                        us.append(dict(lhs=lambda k, slot=slot, m=m: wslice(slot, k, m * 128, 128),
                                       rhs=lambda k, tb=tb: hT(k, tb * 512, 512), n=512, epi=epi_a))
                        us.append(dict(lhs=lambda k, slot=slot, m=m: wslice(slot, k, HB + m * 128, 128),
                                       rhs=lambda k, tb=tb: hT(k, tb * 512, 512), n=512, epi=epi_c))
                return us
            gemm(hT, KC, nblk, wl, units, tag=("p4", l))
            prefetch_w(wload_cols(w_ffn_out[l][0:KH * 128, :], KH, lambda blk: blk * WBC), ("p5", l, 0))

        def phase5(l):
            Wl = w_ffn_out[l]
            for half in range(2):
                load_bufA(AT[half * KH:(half + 1) * KH], KH)
                resid_gemm(hT, KH, Wl[half * KH * 128:(half + 1) * KH * 128, :], tag=("p5", l, half))
                if half == 0:
                    prefetch_w(wload_cols(Wl[KH * 128:2 * KH * 128, :], KH, lambda blk: blk * WBC), ("p5", l, 1))

        import os as _os
        STOP = int(_os.environ.get("K_STOP", "99"))

        def program():
            ingest()
            fw.barrier()
            if STOP <= 1:
                return
            for l in range(NL):
                load_layer_small(l)
                norm_pass(lambda k, l=l: nw1_t[:, l * KC + k: l * KC + k + 1])
                fw.barrier()
                if STOP <= 2:
                    return
                phase1(l)
                fw.barrier()
                if STOP <= 3:
                    return
                exchange()
                if STOP <= 4:
                    return
                sgu(l)
                fw.barrier()
                if STOP <= 5:
                    return
                prefetch_w(wload_cols(ret_proj[l], RC, lambda blk: blk * WBC), ("p3a", l))
                pass_b(l)
                fw.barrier()
                if STOP <= 6:
                    return
                phase3(l)
                fw.barrier()
                if STOP <= 7:
                    return
                norm_pass(lambda k, l=l: nw2_t[:, l * KC + k: l * KC + k + 1])
                fw.barrier()
                if STOP <= 8:
                    return
                phase4(l)
                fw.barrier()
                if STOP <= 9:
                    return
                phase5(l)
                fw.barrier()
                if STOP <= 10:
                    return
            norm_pass(lambda k: nwf_t[:, k:k + 1], final=True)

        program()
        fw.finish()
        fw.run(block)
    return nc


def _run(cfg, x, norm_mix_w, w_in, ret_gn_w, ret_proj, sgu_ln_w, sgu_ln_b, sgu_w_s, sgu_b_s, sgu_proj, w_out,
         norm_ffn_w, w_ffn_in, w_ffn_out, final_norm_w, trace=False):
    c = cfg
    f = lambda a: np.ascontiguousarray(np.asarray(a, dtype=np.float32))
    x = f(x).reshape(c.BATCH * c.SEQ, c.D)
    shared = {
        "w_in": f(w_in), "ret_proj": f(ret_proj), "sgu_proj": f(sgu_proj), "w_out": f(w_out),
        "w_ffn_in": f(f(w_ffn_in).reshape(c.NL, c.D, 2, c.FC, 128).transpose(0, 1, 3, 2, 4).reshape(c.NL, c.D, 2 * c.DFF)),
        "w_ffn_out": f(w_ffn_out),
        "norm_mix_w": f(f(norm_mix_w).reshape(c.NL, c.KC, 128).transpose(0, 2, 1)),
        "norm_ffn_w": f(f(norm_ffn_w).reshape(c.NL, c.KC, 128).transpose(0, 2, 1)),
        "final_norm_w": f(f(final_norm_w).reshape(c.KC, 128).transpose(1, 0)),
        "ret_gn_w": f(f(ret_gn_w).reshape(c.NL, c.RC, 128).transpose(0, 2, 1)),
        "sgu_ln_w": f(sgu_ln_w), "sgu_ln_b": f(sgu_ln_b), "sgu_w_s": f(sgu_w_s),
        "sgu_b_s": f(f(sgu_b_s).reshape(c.NL, c.SG * 128)),
    }
    nc = _build(c)
    in_maps = []
    for core in range(c.NCORES):
        m = dict(shared)
        m["x"] = x[core * c.T:(core + 1) * c.T]
        m.update(_tables(c, core))
        in_maps.append(m)
    res = run_bass_kernel_spmd(nc, in_maps, core_ids=list(range(c.NCORES)), **({"trace": True} if trace else {}))
    y = np.concatenate([res.results[i]["y"] for i in range(c.NCORES)], 0)
    return y.reshape(c.BATCH, c.SEQ, c.D).astype(np.float32), res


def kernel(**inputs):
    cfg = Cfg()
    y, _ = _run(cfg, **inputs)
    return y
```

```python
import math
from contextlib import ExitStack

import numpy as np
import concourse.bass as bass
import concourse.mybir as mybir
from concourse.bass_utils import run_bass_kernel_spmd

F32 = mybir.dt.float32
BF16 = mybir.dt.bfloat16
AF = mybir.ActivationFunctionType
ALU = mybir.AluOpType
EPS = 1e-6
ENGS = ("pe", "act", "dve", "pool", "sp")


class Fw:
    def __init__(self, nc, stack, n_dma_slots=8):
        self.nc = nc
        self.q = {e: [] for e in ENGS}
        self.sem = {e: stack.enter_context(nc.semaphore("s_" + e)) for e in ENGS}
        self.cnt = {e: 0 for e in ENGS}
        self.waited = {e: {} for e in ENGS}
        self.last_w = {}
        self.readers = {}
        self.slots = {}
        self.slot_rr = {}
        for qn in ("sp", "pool"):
            self.slots[qn] = [[stack.enter_context(nc.semaphore("d_%s%d" % (qn, i))), 0]
                              for i in range(n_dma_slots)]
            self.slot_rr[qn] = 0
        self.cc_sem = stack.enter_context(nc.semaphore("cc_sem"))
        self.coll_cnt = 0

    def _wait(self, eng, ev):
        if ev is None:
            return
        sem, val, src = ev[0], ev[1], ev[2]
        if src == eng and eng == "pe":
            return
        key = id(sem)
        if self.waited[eng].get(key, 0) >= val:
            return
        self.waited[eng][key] = val
        self.q[eng].append(lambda E, sem=sem, val=val: E.wait_ge(sem, val))

    def _deps(self, eng, reads, writes, group=None):
        for r in reads:
            for ev in self.last_w.get(r, ()):
                self._wait(eng, ev)
        for w in writes:
            for ev in self.last_w.get(w, ()):
                if group is not None and len(ev) > 3 and ev[3] == group:
                    continue
                self._wait(eng, ev)
            for ev in self.readers.get(w, ()):
                self._wait(eng, ev)

    def _record(self, ev, reads, writes):
        for w in writes:
            if ev[2] is None:
                lst = [e for e in self.last_w.get(w, ()) if e[2] is None and e[0] is not ev[0]]
                lst.append(ev)
                self.last_w[w] = lst
            else:
                self.last_w[w] = [ev]
            self.readers[w] = []
        for r in reads:
            lst = self.readers.setdefault(r, [])
            lst[:] = [e for e in lst if e[0] is not ev[0]]
            lst.append(ev)

    def op(self, eng, fn, reads=(), writes=()):
        self._deps(eng, reads, writes)
        self.cnt[eng] += 1
        sem = self.sem[eng]
        ev = (sem, self.cnt[eng], eng)
        self.q[eng].append(lambda E, fn=fn, sem=sem: fn(E).then_inc(sem, 1))
        self._record(ev, reads, writes)
        return ev

    def pe_group(self, fns, reads=(), writes=()):
        self._deps("pe", reads, writes)
        for fn in fns[:-1]:
            self.q["pe"].append(fn)
        self.cnt["pe"] += 1
        sem = self.sem["pe"]
        ev = (sem, self.cnt["pe"], "pe")
        last = fns[-1]
        self.q["pe"].append(lambda E, fn=last, sem=sem: fn(E).then_inc(sem, 1))
        self._record(ev, reads, writes)
        return ev

    def coll(self, fn, reads=(), writes=()):
        self._deps("pool", reads, writes)
        self.coll_cnt += 1
        n = self.coll_cnt
        sem = self.cc_sem
        self.q["pool"].append(lambda E, fn=fn, sem=sem: fn(E).then_inc(sem, 1))
        self.q["pool"].append(lambda E, sem=sem, n=n: E.wait_ge(sem, n))
        self.cnt["pool"] += 1
        psem = self.sem["pool"]
        ev = (psem, self.cnt["pool"], "pool")
        self.q["pool"].append(lambda E, psem=psem: E.sem_inc(psem, 1))
        self._record(ev, reads, writes)
        return ev

    def dma(self, qn, fn, reads=(), writes=(), group=None):
        self._deps(qn, reads, writes, group)
        i = self.slot_rr[qn]
        self.slot_rr[qn] = (i + 1) % len(self.slots[qn])
        slot = self.slots[qn][i]
        sem = slot[0]
        if slot[1] > 0:
            self._wait(qn, (sem, 16 * slot[1], None))
        slot[1] += 1
        ev = (sem, 16 * slot[1], None, group)
        self.q[qn].append(lambda E, fn=fn, sem=sem: fn(E).then_inc(sem, 16))
        self._record(ev, reads, writes)
        return ev

    def _sp_wait_all(self):
        for q2 in self.slots:
            for sem, c in self.slots[q2]:
                if c > 0:
                    self._wait("sp", (sem, 16 * c, None))
        for e in ENGS:
            if self.cnt[e] > 0 and e != "sp":
                self._wait("sp", (self.sem[e], self.cnt[e], e))

    def barrier(self):
        self._sp_wait_all()
        self.cnt["sp"] += 1
        sem = self.sem["sp"]
        ev = (sem, self.cnt["sp"], "sp")
        self.q["sp"].append(lambda E, sem=sem: E.sem_inc(sem, 1))
        for e in ENGS:
            if e != "sp":
                self._wait(e, ev)

    def finish(self):
        self._sp_wait_all()

    def run(self, block):
        q = self.q

        @block.tensor
        def _(E):
            for f in q["pe"]:
                f(E)

        @block.scalar
        def _(E):
            for f in q["act"]:
                f(E)

        @block.vector
        def _(E):
            for f in q["dve"]:
                f(E)

        @block.gpsimd
        def _(E):
            for f in q["pool"]:
                f(E)

        @block.sync
        def _(E):
            for f in q["sp"]:
                f(E)


class Cfg:
    def __init__(self, D=2048, H=8, DFF=5632, T=2048, NL=4, NCORES=8, SEQ=8192, BATCH=2):
        self.D, self.H, self.DFF, self.T, self.NL = D, H, DFF, T, NL
        self.NCORES, self.SEQ, self.BATCH = NCORES, SEQ, BATCH
        self.G = SEQ // T
        assert self.G * BATCH == NCORES
        self.KC = D // 128
        self.QK = H * 128
        self.RV = H * 256
        self.RC = self.RV // 128
        self.SG = D // 256
        self.NT = T // 128
        self.NB = T // 512
        self.FC = DFF // 128
        self.INC = 2 * self.QK + 2 * self.RV + 4 * D
        self.oq, self.ok = 0, self.QK
        self.ov = 2 * self.QK
        self.og = self.ov + self.RV
        self.osu = self.og + self.RV
        self.osv = self.osu + D
        self.oga = self.osv + D
        self.ogb = self.oga + D


def _tables(cfg, core):
    H, T, NT, G = cfg.H, cfg.T, cfg.NT, cfg.G
    rank = core % G
    pos0 = rank * T
    half = 64
    inv = (10000.0 ** (-np.arange(half, dtype=np.float32) / np.float32(half))).astype(np.float32)
    pos = (pos0 + np.arange(T)).astype(np.float32)
    ang = (pos[None, :] * inv[:, None]).astype(np.float32)
    cos = np.cos(ang).astype(np.float32)
    sin = np.sin(ang).astype(np.float32)
    tb_cos = np.concatenate([cos, cos], 0)
    tb_sin = np.concatenate([-sin, sin], 0)
    logg = np.log1p(-(2.0 ** (-5.0 - np.arange(H, dtype=np.float64))))
    p = np.arange(128, dtype=np.float64)
    n = np.arange(NT, dtype=np.float64)
    loc = n[None, :, None] * 128 + p[:, None, None]
    kdec = np.exp(-logg[None, None, :] * loc)
    odec = np.exp(logg[None, None, :] * loc) * (128.0 ** -0.5)
    coef = np.zeros((128, G, H), np.float64)
    for s in range(G):
        if s < rank:
            coef[:, s, :] = np.exp(logg * T * (rank - s - 1))[None, :]
    oneh = np.zeros((128, G, H), np.float64)
    oneh[:, rank, :] = np.exp(logg * T)[None, :]
    gT = np.broadcast_to(np.exp(logg * T)[None, :], (128, H))
    j = np.arange(128)[:, None]
    i = np.arange(128)[None, :]
    cj, ci = j // 64, i // 64
    mask = np.zeros((128, H, 128), np.float64)
    for h in range(H):
        m = np.where(i >= j, 1.0, np.exp(logg[h] * 2.0 * (j - i)))
        m = np.where(cj > ci, 0.0, m)
        m = np.where(cj < ci, 1.0, m)
        mask[:, h, :] = m
    cmask = (cj <= ci).astype(np.float32)
    ident = np.eye(128, dtype=np.float32)
    pm = np.zeros((128, 128), np.float32)
    for d in range(128):
        pm[(d + 64) % 128, d] = 1.0
    f = lambda a: np.ascontiguousarray(np.asarray(a, dtype=np.float32))
    return {
        "tb_cos": f(tb_cos), "tb_sin": f(tb_sin),
        "tb_kdec": f(kdec.reshape(128, NT * H)), "tb_odec": f(odec.reshape(128, NT * H)),
        "tb_coef": f(coef.reshape(128, G * H)), "tb_oneh": f(oneh.reshape(128, G * H)), "tb_gT": f(gT),
        "tb_mask": f(mask.reshape(128, H * 128)), "tb_cmask": f(cmask),
        "tb_ident": f(ident), "tb_pm": f(pm), "tb_ones": np.ones((128, 128), np.float32),
    }


def _build(cfg):
    c = cfg
    D, H, T, NL, KC, NT, NB, FC, RC, SG, G, DFF = c.D, c.H, c.T, c.NL, c.KC, c.NT, c.NB, c.FC, c.RC, c.SG, c.G, c.DFF
    nc = bass.Bass("TRN2", target_bir_lowering=False)

    def din(name, shape):
        return nc.dram_tensor(name, list(shape), F32, kind="ExternalInput").ap()

    def dscr(name, shape, dt):
        return nc.dram_tensor(name, list(shape), dt, kind="Internal").ap()

    x_in = din("x", [T, D])
    w_in = din("w_in", [NL, D, c.INC])
    ret_proj = din("ret_proj", [NL, c.RV, D])
    sgu_proj = din("sgu_proj", [NL, D, D])
    w_out = din("w_out", [NL, D, D])
    w_ffn_in = din("w_ffn_in", [NL, D, 2 * DFF])
    w_ffn_out = din("w_ffn_out", [NL, DFF, D])
    nw1_d = din("norm_mix_w", [NL, 128, KC])
    nw2_d = din("norm_ffn_w", [NL, 128, KC])
    nwf_d = din("final_norm_w", [128, KC])
    gnw_d = din("ret_gn_w", [NL, 128, RC])
    lnw_d = din("sgu_ln_w", [NL, D])
    lnb_d = din("sgu_ln_b", [NL, D])
    ws_d = din("sgu_w_s", [NL, SG, 128, 128])
    bs_d = din("sgu_b_s", [NL, SG * 128])
    tbs = {}
    for nm, shp in (("tb_cos", [128, T]), ("tb_sin", [128, T]), ("tb_kdec", [128, NT * H]),
                    ("tb_odec", [128, NT * H]), ("tb_coef", [128, G * H]), ("tb_oneh", [128, G * H]),
                    ("tb_gT", [128, H]), ("tb_mask", [128, H * 128]), ("tb_cmask", [128, 128]),
                    ("tb_ident", [128, 128]), ("tb_pm", [128, 128]), ("tb_ones", [128, 128])):
        tbs[nm] = din(nm, shp)
    y_out = nc.dram_tensor("y", [T, D], F32, kind="ExternalOutput").ap()

    XT = dscr("XT", [KC, 128, T], F32)
    ZQ = dscr("ZQ", [H, 128, T], BF16)
    ZK = dscr("ZK", [H, 128, T], BF16)
    ZVv = dscr("ZVv", [128, NT, c.RV], BF16)
    ZG = dscr("ZG", [128, NT, c.RV], BF16)
    ZU = dscr("ZU", [KC, 128, T], BF16)
    ZS = dscr("ZS", [128, NT, D], BF16)
    GA = dscr("GA", [KC, 128, T], BF16)
    GB = dscr("GB", [KC, 128, T], BF16)
    GT = dscr("GT", [RC, 128, T], BF16)
    ST = dscr("ST", [KC, 128, T], BF16)
    AT = dscr("AT", [FC, 128, T], BF16)
    EXI = dscr("EXI", [G * H * 128, 256], F32)
    EXO = dscr("EXO", [G * H * 128, 256], F32)

    WBC = 256
    ARENA = 65536
    with ExitStack() as st:
        fw = Fw(nc, st)
        sb = lambda name, shape, dt: st.enter_context(nc.sbuf_tensor(name, list(shape), dt))
        arena = sb("arena", [128, ARENA], BF16)
        KH = FC // 2
        assert FC % 2 == 0
        wb = [sb("wb%d" % i, [128, max(KC, KH) * WBC], BF16) for i in range(2)]
        WCTX = dict(slots=[w[:] for w in wb], wbc=WBC, key="wb")
        kdec_t = sb("kdec_t", [128, NT * H], F32)
        odec_t = sb("odec_t", [128, NT * H], F32)
        coef_t = sb("coef_t", [128, G * H], F32)
        oneh_t = sb("oneh_t", [128, G * H], F32)
        gT_t = sb("gT_t", [128, H], F32)
        mask_t = sb("mask_t", [128, H * 128], F32)
        cmask_t = sb("cmask_t", [128, 128], F32)
        ident_b = sb("ident_b", [128, 128], BF16)
        ident_f = sb("ident_f", [128, 128], F32)
        pm_b = sb("pm_b", [128, 128], BF16)
        ones_b = sb("ones_b", [128, 128], BF16)
        nw1_t = sb("nw1_t", [128, NL * KC], F32)
        nw2_t = sb("nw2_t", [128, NL * KC], F32)
        nwf_t = sb("nwf_t", [128, KC], F32)
        gnw_t = sb("gnw_t", [128, NL * RC], F32)
        lnw_t = sb("lnw_t", [128, D], BF16)
        lnb_t = sb("lnb_t", [128, D], BF16)
        wsb_t = sb("wsb_t", [128, SG * 128], BF16)
        wT_t = sb("wT_t", [128, SG * 128], BF16)
        bsf_t = sb("bsf_t", [1, SG * 128], F32)
        bsh_t = sb("bsh_t", [1, SG * 128], BF16)
        bsr_t = sb("bsr_t", [1, SG * 128], F32)
        bsl_t = sb("bsl_t", [1, SG * 128], BF16)
        NSB = 4
        stg_b = [sb("stgb%d" % i, [128, 512], BF16) for i in range(NSB)]
        stg_f = [sb("stgf%d" % i, [128, 512], F32) for i in range(3)]
        ld_b = [sb("ldb%d" % i, [128, 512], BF16) for i in range(2)]
        ld_f = [sb("ldf%d" % i, [128, 512], F32) for i in range(2)]
        rstd_t = sb("rstd_t", [128, 512], F32)
        small = sb("small", [128, 256], F32)
        s0_t = sb("s0_t", [128, 256], F32)
        psG = [st.enter_context(nc.psum_tensor("psG%d" % i, [128, 512], F32)) for i in range(4)]
        psT = [st.enter_context(nc.psum_tensor("psT%d" % i, [128, 1024], BF16)) for i in range(2)]
        psX = st.enter_context(nc.psum_tensor("psX", [128, 512], F32))
        psS = st.enter_context(nc.psum_tensor("psS", [128, 512], F32))
        block = st.enter_context(nc.Block())

        rr = {"g": 0, "g3": 0, "sb": 0, "sf": 0, "lb": 0, "lf": 0, "t": 0}

        def nxt(kind, n):
            i = rr[kind]
            rr[kind] = (i + 1) % n
            return i

        def av(off, n):
            return arena[:, off:off + n]

        def avf(off, n):
            return arena[:, off:off + n].bitcast(F32)

        def mm(ps, lhsT, rhs, start, stop):
            return lambda E: E.matmul(ps, lhsT=lhsT, rhs=rhs, start=start, stop=stop)

        def trp(out, in_, ident):
            return lambda E: E.transpose(out=out, in_=in_, identity=ident)

        def actf(out, in_, func, scale=1.0, bias=0.0):
            return lambda E: E.activation(out=out, in_=in_, func=func, bias=bias, scale=scale)

        def tt(out, in0, in1, op):
            return lambda E: E.tensor_tensor(out=out, in0=in0, in1=in1, op=op)

        def stt(out, in0, scalar, in1, op0, op1):
            return lambda E: E.scalar_tensor_tensor(out=out, in0=in0, scalar=scalar, in1=in1, op0=op0, op1=op1)

        def tsc(out, in0, s1, s2, op0, op1):
            return lambda E: E.tensor_scalar(out=out, in0=in0, scalar1=s1, scalar2=s2, op0=op0, op1=op1)

        gid = [0]

        def newgroup():
            gid[0] += 1
            return gid[0]

        def ld(qn, out, in_, reads=(), writes=(), group=None):
            return fw.dma(qn, lambda E: E.dma_start(out=out, in_=in_), reads=reads, writes=writes, group=group)

        ld("pool", ident_b[:], tbs["tb_ident"], writes=["identb"])
        ld("pool", pm_b[:], tbs["tb_pm"], writes=["pmb"])
        ld("pool", ones_b[:], tbs["tb_ones"], writes=["onesb"])
        ld("sp", ident_f[:], tbs["tb_ident"], writes=["identf"])
        ld("sp", kdec_t[:], tbs["tb_kdec"], writes=["kdec"])
        ld("sp", odec_t[:], tbs["tb_odec"], writes=["odec"])
        ld("sp", coef_t[:], tbs["tb_coef"], writes=["coef"])
        ld("sp", oneh_t[:], tbs["tb_oneh"], writes=["oneh"])
        ld("sp", gT_t[:], tbs["tb_gT"], writes=["gT"])
        ld("sp", mask_t[:], tbs["tb_mask"], writes=["mask"])
        ld("sp", cmask_t[:], tbs["tb_cmask"], writes=["cmask"])
        ld("sp", nw1_t[:].rearrange("p (l k) -> p l k", l=NL), nw1_d.rearrange("l p k -> p l k"), writes=["nw1"])
        ld("sp", nw2_t[:].rearrange("p (l k) -> p l k", l=NL), nw2_d.rearrange("l p k -> p l k"), writes=["nw2"])
        ld("sp", gnw_t[:].rearrange("p (l k) -> p l k", l=NL), gnw_d.rearrange("l p k -> p l k"), writes=["gnw"])
        ld("sp", nwf_t[:], nwf_d, writes=["nwf"])
        CONST_KEYS = ["cos", "sin", "identb", "pmb", "onesb", "identf", "kdec", "odec", "coef", "oneh", "gT",
                      "mask", "cmask", "nw1", "nw2", "gnw", "nwf"]

        def ingest():
            XS = 0
            for n in range(NT):
                s = n % 2
                xt = avf(XS + s * 2 * D, 2 * D)
                ld("sp", xt, x_in[n * 128:(n + 1) * 128, :], writes=[("xs", s)])
                for k0 in range(0, KC, 4):
                    gi = nxt("g", 4)
                    ps = psG[gi]
                    fw.pe_group([trp(ps[:, j * 128:(j + 1) * 128], xt[:, (k0 + j) * 128:(k0 + j + 1) * 128], ident_f[:])
                                 for j in range(4)], reads=[("xs", s), "identf"], writes=[("psG", gi)])
                    si = nxt("sf", 3)
                    fw.op("act", actf(stg_f[si][:], ps[:], AF.Copy), reads=[("psG", gi)], writes=[("stgf", si)])
                    ld("pool", XT[k0:k0 + 4, :, n * 128:(n + 1) * 128].rearrange("k p t -> p k t"),
                       stg_f[si][:].rearrange("p (k t) -> p k t", k=4), reads=[("stgf", si)], writes=["XT"], group="g_XT")

        BUFA = 0
        BUFB = KC * T

        def hT(k, t0, n):
            return arena[:, BUFA + k * T + t0: BUFA + k * T + t0 + n]

        def mT(k, t0, n):
            return arena[:, BUFB + k * T + t0: BUFB + k * T + t0 + n]

        def norm_pass(nw_ap_fn, final=False):
            XB = BUFB
            assert XB + 2 * KC * 1024 <= ARENA
            for tb in range(NB):
                xbk = ("xb", tb % 2)
                xb = avf(XB + (tb % 2) * KC * 1024, KC * 1024).rearrange("p (k t) -> p k t", k=KC)
                ld("sp", xb, XT[:, :, tb * 512:(tb + 1) * 512].rearrange("k p t -> p k t"),
                   reads=["XT"], writes=[xbk])
                fns = []
                for k in range(KC):
                    si = nxt("sb", NSB)
                    fw.op("act", actf(stg_b[si][:], xb[:, k, :], AF.Square), reads=[xbk], writes=[("stgb", si)])
                    fw.pe_group([mm(psX[:], ones_b[:], stg_b[si][:], k == 0, k == KC - 1)],
                                reads=[("stgb", si), "onesb"], writes=["psX"])
                fw.op("act", actf(rstd_t[:], psX[:], AF.Sqrt, scale=1.0 / D, bias=EPS), reads=["psX"], writes=["rstd"])
                fw.op("dve", lambda E: E.reciprocal(out=rstd_t[:], in_=rstd_t[:]), reads=["rstd"], writes=["rstd"])
                if not final:
                    for k in range(KC):
                        fw.op("dve", stt(hT(k, tb * 512, 512), xb[:, k, :], nw_ap_fn(k), rstd_t[:], ALU.mult, ALU.mult),
                              reads=[xbk, "rstd", "nw1", "nw2"], writes=["bufA"])
                else:
                    for k in range(KC):
                        fw.op("dve", stt(xb[:, k, :], xb[:, k, :], nw_ap_fn(k), rstd_t[:], ALU.mult, ALU.mult),
                              reads=[xbk, "rstd", "nwf"], writes=[xbk])
                    for tl in range(4):
                        n = tb * 4 + tl
                        s = n % 2
                        yt = avf(BUFA + s * 2 * D, 2 * D)
                        for k0 in range(0, KC, 4):
                            gi = nxt("g", 4)
                            ps = psG[gi]
                            fw.pe_group([trp(ps[:, j * 128:(j + 1) * 128], xb[:, k0 + j, tl * 128:(tl + 1) * 128], ident_f[:])
                                         for j in range(4)], reads=[xbk, "identf"], writes=[("psG", gi)])
                            fw.op("act", actf(yt[:, k0 * 128:(k0 + 4) * 128], ps[:], AF.Copy),
                                  reads=[("psG", gi)], writes=[("yt", s)])
                        ld("pool", y_out[n * 128:(n + 1) * 128, :], yt, reads=[("yt", s)], writes=["y"], group="g_y")

        pre = {"done": None}

        def prefetch_w(wload, tag):
            wload(0, 0)
            pre["done"] = tag

        def gemm(Xfn, kc, nblk, wload, units_of_block, wctx=None, tag=None, filler=None, filler_from=0):
            wctx = wctx or WCTX
            deferred = []
            if tag is not None and pre["done"] == tag:
                pre["done"] = None
            else:
                wload(0, 0)
            for blk in range(nblk):
                slot = blk % 2
                if blk + 1 < nblk:
                    wload(blk + 1, (blk + 1) % 2)
                for u in units_of_block(blk, slot):
                    gi = nxt("g", 4)
                    ps = psG[gi][:, 0:u["n"]]
                    fw.pe_group([mm(ps, u["lhs"](k), u["rhs"](k), k == 0, k == kc - 1) for k in range(kc)],
                                reads=[(wctx["key"], slot), "bufA", "bufB"], writes=[("psG", gi)])
                    for dfn in deferred:
                        dfn()
                    deferred = []
                    d = u["epi"](ps, gi)
                    if d is not None:
                        deferred.append(d)
                    if filler is not None and blk >= filler_from:
                        next(filler, None)
            for dfn in deferred:
                dfn()
            if filler is not None:
                for _ in filler:
                    pass

        def wslice(slot, k, c0, n, wctx=None):
            wctx = wctx or WCTX
            return wctx["slots"][slot][:, k * wctx["wbc"] + c0: k * wctx["wbc"] + c0 + n]

        def wload_cols(Wl, kc_total, col_of_blk, ncols=None, dst_off=0, wctx=None):
            wctx = wctx or WCTX
            wbc = wctx["wbc"]
            ncols = ncols or wbc

            def f(blk, slot):
                col = col_of_blk(blk)
                src = Wl.rearrange("(k p) n -> p k n", p=128)
                dst = wctx["slots"][slot].rearrange("p (k n) -> p k n", n=wbc)
                step = 4
                grp = newgroup()
                for k0 in range(0, kc_total, step):
                    k1 = min(kc_total, k0 + step)
                    ld("pool", dst[:, k0:k1, dst_off:dst_off + ncols], src[:, k0:k1, col:col + ncols],
                       writes=[(wctx["key"], slot)], group=grp)
            return f

        def phase1(l):
            Wl = w_in[l]
            W1 = 512 if c.QK % 512 == 0 else 256
            W1S = KC * W1
            w1ctx = dict(slots=[arena[:, BUFB + i * W1S: BUFB + (i + 1) * W1S] for i in range(2)], wbc=W1, key="wb1")
            CS = BUFB + 2 * W1S
            assert CS + 2 * T <= ARENA
            cos_v = arena[:, CS:CS + T]
            sin_v = arena[:, CS + T:CS + 2 * T]
            ld("pool", cos_v, tbs["tb_cos"], writes=["cos"])
            ld("pool", sin_v, tbs["tb_sin"], writes=["sin"])
            nblk = c.INC // W1
            kv = [b for b in range(nblk) if c.ok <= b * W1 < c.og]
            order = kv + [b for b in range(nblk) if b not in kv]
            PA0 = CS + 2 * T
            assert PA0 + T + NT * 256 + 512 + 4 * G * 256 <= ARENA, "arena overflow (pass A inside phase 1)"

            def kind_of(col):
                if col < c.ok:
                    return "q"
                if col < c.ov:
                    return "k"
                if col < c.og:
                    return "v"
                if col < c.osu:
                    return "g"
                if col < c.osv:
                    return "su"
                if col < c.oga:
                    return "sv"
                if col < c.ogb:
                    return "ga"
                return "gb"

            def units(blk, slot):
                col = order[blk] * W1
                kind = kind_of(col)
                us = []
                if kind in ("v", "g", "sv"):
                    base = {"v": c.ov, "g": c.og, "sv": c.osv}[kind]
                    dst = {"v": ZVv, "g": ZG, "sv": ZS}[kind]
                    func = {"v": AF.Copy, "g": AF.Silu, "sv": AF.Gelu}[kind]
                    for n in range(NT):
                        def epi(ps, gi, n=n, dst=dst, func=func, c0=col - base):
                            si = nxt("sb", NSB)
                            fw.op("act", actf(stg_b[si][:, 0:W1], ps, func), reads=[("psG", gi)], writes=[("stgb", si)])
                            ld("pool", dst[:, n, c0:c0 + W1], stg_b[si][:, 0:W1], reads=[("stgb", si)], writes=["Z"], group="g_Z")
                            return None
                        us.append(dict(lhs=lambda k, n=n: hT(k, n * 128, 128),
                                       rhs=lambda k, slot=slot: wslice(slot, k, 0, W1, w1ctx), n=W1, epi=epi))
                else:
                    base = {"q": c.oq, "k": c.ok, "su": c.osu, "ga": c.oga, "gb": c.ogb}[kind]
                    for m in range(W1 // 128):
                        ch = (col - base) // 128 + m
                        for tb in range(NB):
                            if kind in ("q", "k"):
                                dst = ZQ if kind == "q" else ZK

                                def epi(ps, gi, ch=ch, tb=tb, dst=dst):
                                    si = nxt("sb", NSB)
                                    zb = stg_b[si]
                                    fw.op("act", actf(zb[:], ps, AF.Copy), reads=[("psG", gi)], writes=[("stgb", si)])

                                    def later():
                                        fw.pe_group([mm(psX[:], pm_b[:], zb[:], True, True)],
                                                    reads=[("stgb", si), "pmb"], writes=["psX"])
                                        f1 = nxt("sf", 3)
                                        fw.op("dve", tt(stg_f[f1][:], zb[:], cos_v[:, tb * 512:(tb + 1) * 512], ALU.mult),
                                              reads=[("stgb", si), "cos"], writes=[("stgf", f1)])
                                        f2 = nxt("sf", 3)
                                        fw.op("dve", tt(stg_f[f2][:], psX[:], sin_v[:, tb * 512:(tb + 1) * 512], ALU.mult),
                                              reads=["psX", "sin"], writes=[("stgf", f2)])
                                        so = nxt("sb", NSB)
                                        fw.op("dve", tt(stg_b[so][:], stg_f[f1][:], stg_f[f2][:], ALU.add),
                                              reads=[("stgf", f1), ("stgf", f2)], writes=[("stgb", so)])
                                        ld("pool", dst[ch, :, tb * 512:(tb + 1) * 512], stg_b[so][:],
                                           reads=[("stgb", so)], writes=["Z"], group="g_Z")
                                    return later
                            else:
                                dst = {"su": ZU, "ga": GA, "gb": GB}[kind]
                                func = AF.Gelu if kind == "su" else AF.Sigmoid

                                def epi(ps, gi, ch=ch, tb=tb, dst=dst, func=func):
                                    si = nxt("sb", NSB)
                                    fw.op("act", actf(stg_b[si][:], ps, func), reads=[("psG", gi)], writes=[("stgb", si)])
                                    ld("pool", dst[ch, :, tb * 512:(tb + 1) * 512], stg_b[si][:],
                                       reads=[("stgb", si)], writes=["Z"], group="g_Z")
                                    return None
                            us.append(dict(lhs=lambda k, slot=slot, m=m: wslice(slot, k, m * 128, 128, w1ctx),
                                           rhs=lambda k, tb=tb: hT(k, tb * 512, 512), n=512, epi=epi))
                return us

            gemm(hT, KC, nblk, wload_cols(Wl, KC, lambda blk: order[blk] * W1, wctx=w1ctx), units, wctx=w1ctx,
                 filler=pass_a_gen(PA0), filler_from=len(kv) + 1)

        R_SLOT = 2 * T + 2 * NT * 256
        R0 = 0
        RTMP = R0 + 2 * R_SLOT
        RTMP_SZ = 14336
        PA_END = 512 + 4 * G * 256
        O_OFF = RTMP + RTMP_SZ
        GTH = O_OFF + 2 * 2 * NT * 256
        assert GTH + 4 * T <= ARENA, "arena overflow (retention)"
        S_OFF = RTMP + PA_END
        assert S_OFF + 4 * D + 2 * D + KC * 512 * 2 <= ARENA, "arena overflow (sgu)"

        def load_layer_small(l):
            ld("pool", lnw_t[:], lnw_d[l:l + 1, :].partition_broadcast(128), writes=["lnw"])
            ld("pool", lnb_t[:], lnb_d[l:l + 1, :].partition_broadcast(128), writes=["lnb"])
            ld("pool", wsb_t[:].rearrange("p (g j) -> p g j", g=SG), ws_d[l].rearrange("g i j -> i g j"), writes=["wsb"])
            ld("sp", bsf_t[:], bs_d[l:l + 1, :], writes=["bsf"])
            for g in range(SG):
                ti = nxt("t", 2)
                fw.pe_group([trp(psT[ti][:, 0:128], wsb_t[:, g * 128:(g + 1) * 128], ident_b[:])],
                            reads=["wsb", "identb"], writes=[("psT", ti)])
                fw.op("dve", tt(wT_t[:, g * 128:(g + 1) * 128], psT[ti][:, 0:128], cmask_t[:], ALU.mult),
                      reads=[("psT", ti), "cmask"], writes=["wT"])
            fw.op("dve", lambda E: E.tensor_copy(out=bsh_t[:], in_=bsf_t[:]), reads=["bsf"], writes=["bsh"])
            fw.op("dve", tt(bsr_t[:], bsf_t[:], bsh_t[:], ALU.subtract), reads=["bsf", "bsh"], writes=["bsr"])
            fw.op("dve", lambda E: E.tensor_copy(out=bsl_t[:], in_=bsr_t[:]), reads=["bsr"], writes=["bsl"])

        def head_views(s):
            b = R0 + s * R_SLOT
            qT = av(b, T)
            kT = av(b + T, T)
            v = av(b + 2 * T, NT * 256)
            sg = av(b + 2 * T + NT * 256, NT * 256)
            return qT, kT, v, sg

        def kt_make(h, n, kT, s, pi=0):
            ti = nxt("t", 2)
            fw.pe_group([trp(psT[ti][:, 0:128], kT[:, n * 128:(n + 1) * 128], ident_b[:])],
                        reads=[("hk", s), "identb"], writes=[("psT", ti)])
            kk = pi * 2 + n % 2
            kt = av(RTMP + kk * 128, 128)
            fw.op("act", actf(kt, psT[ti][:, 0:128], AF.Copy, scale=kdec_t[:, n * H + h:n * H + h + 1]),
                  reads=[("psT", ti), "kdec"], writes=[("kt", kk)])
            return kt, ("kt", kk)

        def pass_a_gen(PA0):
            exg = newgroup()
            kT = av(PA0, T)
            v = av(PA0 + T, NT * 256)
            KT0 = PA0 + T + NT * 256
            EX0 = KT0 + 512
            acc = psS[:, 0:256]
            acck = ("psS", 0)
            for h in range(H):
                ld("sp", kT, ZK[h], reads=["Z"], writes=["pa_k"])
                ld("sp", v.rearrange("p (n e) -> p n e", n=NT), ZVv[:, :, h * 256:(h + 1) * 256], reads=["Z"], writes=["pa_v"])
                prev = None
                for n in range(NT):
                    if prev is not None:
                        pk, pn = prev
                        fw.pe_group([mm(acc, pk, v[:, pn * 256:(pn + 1) * 256], pn == 0, pn == NT - 1)],
                                    reads=[("pa_kt", pn % 2), "pa_v"], writes=[acck])
                    ti = nxt("t", 2)
                    fw.pe_group([trp(psT[ti][:, 0:128], kT[:, n * 128:(n + 1) * 128], ident_b[:])],
                                reads=["pa_k", "identb"], writes=[("psT", ti)])
                    kt = av(KT0 + (n % 2) * 128, 128)
                    fw.op("act", actf(kt, psT[ti][:, 0:128], AF.Copy, scale=kdec_t[:, n * H + h:n * H + h + 1]),
                          reads=[("psT", ti), "kdec"], writes=[("pa_kt", n % 2)])
                    prev = (kt, n)
                    yield
                pk, pn = prev
                fw.pe_group([mm(acc, pk, v[:, pn * 256:(pn + 1) * 256], pn == 0, pn == NT - 1)],
                            reads=[("pa_kt", pn % 2), "pa_v"], writes=[acck])
                ex = avf(EX0 + (h % 2) * 2 * G * 256, 2 * G * 256)
                for sl in range(G):
                    fw.op("dve", lambda E, ex=ex, sl=sl, h=h: E.tensor_scalar_mul(
                        out=ex[:, sl * 256:(sl + 1) * 256], in0=acc, scalar1=oneh_t[:, sl * H + h: sl * H + h + 1]),
                        reads=[acck, "oneh"], writes=[("ex", h % 2)])
                ld("pool", EXI.rearrange("(g h p) e -> p g h e", g=G, h=H)[:, :, h, :],
                   ex.rearrange("p (g e) -> p g e", g=G), reads=[("ex", h % 2)], writes=["EXI"], group=exg)
                yield

        def exchange():
            rgs = [list(range(b * G, (b + 1) * G)) for b in range(c.BATCH)]
            fw.coll(lambda E: E.collective_compute(
                "AllReduce", ALU.add, replica_groups=rgs,
                ins=[EXI.opt()], outs=[EXO.opt()]), reads=["EXI"], writes=["EXO"])

        def sgu(l):
            TBS = 2
            TW = TBS * 128
            ZV_O = S_OFF
            TMP_O = ZV_O + 2 * TBS * D
            ZU_O = TMP_O + 2 * D
            OUT_O = ZU_O + 2 * KC * TW
            assert OUT_O + 2 * KC * TW <= ARENA, "arena overflow (sgu)"
            nb = NT // TBS
            nchk = max(1, D // 512)
            assert 32 + 2 * TBS * 6 * nchk <= 256

            def bufs(b):
                p = b % 2
                zv = av(ZV_O + p * TBS * D, TBS * D)
                zu = av(ZU_O + p * KC * TW, KC * TW).rearrange("p (k t) -> p k t", k=KC)
                out = av(OUT_O + p * KC * TW, KC * TW).rearrange("p (k t) -> p k t", k=KC)
                return p, zv, zu, out

            def loads(b):
                p, zv, zu, out = bufs(b)
                ld("sp", zv.rearrange("p (n d) -> p n d", n=TBS), ZS[:, b * TBS:(b + 1) * TBS, :],
                   writes=[("zv", p, tl) for tl in range(TBS)])
                ld("sp", zu, ZU[:, :, b * TW:(b + 1) * TW].rearrange("k p t -> p k t"), writes=[("zu", p)])

            loads(0)
            for b in range(nb):
                if b + 1 < nb:
                    loads(b + 1)
                p, zv, zu, out = bufs(b)
                vn = zv
                mv = small[:, p * 8: p * 8 + 2 * TBS]
                rs = small[:, 16 + p * 4: 16 + p * 4 + TBS]
                for tl in range(TBS):
                    so = 32 + (p * TBS + tl) * 6 * nchk
                    stats = small[:, so: so + 6 * nchk]
                    for cc in range(nchk):
                        w = min(512, D)
                        fw.op("dve", lambda E, tl=tl, cc=cc, stats=stats, w=w, zv=zv: E.bn_stats(
                            out=stats[:, cc * 6:(cc + 1) * 6], in_=zv[:, tl * D + cc * w: tl * D + (cc + 1) * w]),
                            reads=[("zv", p, tl)], writes=[("sgst", p, tl)])
                    fw.op("dve", lambda E, tl=tl, stats=stats, mv=mv: E.bn_aggr(
                        out=mv[:, tl * 2:tl * 2 + 2], in_=stats.rearrange("p (c s) -> p c s", s=6)),
                        reads=[("sgst", p, tl)], writes=[("sgmv", p)])
                mv3 = mv.rearrange("p (t two) -> p t two", two=2)
                fw.op("act", actf(rs, mv3[:, :, 1], AF.Sqrt, bias=EPS), reads=[("sgmv", p)], writes=[("sgrs", p)])
                fw.op("dve", lambda E, rs=rs: E.reciprocal(out=rs, in_=rs), reads=[("sgrs", p)], writes=[("sgrs", p)])
                for tl in range(TBS):
                    tmp = avf(TMP_O, 2 * D)
                    fw.op("dve", stt(tmp, zv[:, tl * D:(tl + 1) * D], mv[:, tl * 2:tl * 2 + 1], lnw_t[:],
                                     ALU.subtract, ALU.mult), reads=[("zv", p, tl), ("sgmv", p), "lnw"], writes=["sgtmp"])
                    fw.op("dve", stt(vn[:, tl * D:(tl + 1) * D], tmp, rs[:, tl:tl + 1], lnb_t[:], ALU.mult, ALU.add),
                          reads=["sgtmp", ("sgrs", p), "lnb"], writes=[("zv", p, tl)])
                for tl in range(TBS):
                    for k0 in range(0, KC, 4):
                        gi = nxt("g", 4)
                        ps = psG[gi]
                        fns = []
                        for j in range(4):
                            k = k0 + j
                            g = k // 2
                            o = ps[:, j * 128:(j + 1) * 128]
                            fns.append(mm(o, vn[:, tl * D + k * 128: tl * D + (k + 1) * 128], wT_t[:, g * 128:(g + 1) * 128], True, False))
                            fns.append(mm(o, ones_b[0:1, :], bsh_t[0:1, g * 128:(g + 1) * 128], False, False))
                            fns.append(mm(o, ones_b[0:1, :], bsl_t[0:1, g * 128:(g + 1) * 128], False, True))
                        fw.pe_group(fns, reads=[("zv", p, tl), "wT", "bsh", "bsl", "onesb"], writes=[("psG", gi)])
                        fw.op("dve", tt(out[:, k0:k0 + 4, tl * 128:(tl + 1) * 128],
                                        ps[:].rearrange("p (k t) -> p k t", k=4),
                                        zu[:, k0:k0 + 4, tl * 128:(tl + 1) * 128], ALU.mult),
                              reads=[("psG", gi), ("zu", p)], writes=[("sgout", p)])
                ld("sp", ST[:, :, b * TW:(b + 1) * TW].rearrange("k p t -> p k t"), out, reads=[("sgout", p)],
                   writes=["ST"], group="g_ST")

        def pass_b(l):
            SIN_HI = RTMP + 512
            SIN_LO = SIN_HI + H * 256
            EXL = SIN_LO + H * 256
            for h in range(H):
                e = h % 2
                exl = avf(EXL + e * 2 * G * 256, 2 * G * 256)
                ld("sp", exl.rearrange("p (g e) -> p g e", g=G),
                   EXO.rearrange("(g h p) e -> p g h e", g=G, h=H)[:, :, h, :], reads=["EXO"], writes=[("exl", e)])
                s0 = s0_t[:]
                fw.op("dve", lambda E, exl=exl, h=h, s0=s0: E.tensor_scalar_mul(out=s0, in0=exl[:, 0:256], scalar1=coef_t[:, h:h + 1]),
                      reads=[("exl", e), "coef"], writes=["s0"])
                for sl in range(1, G):
                    fw.op("dve", stt(s0, exl[:, sl * 256:(sl + 1) * 256], coef_t[:, sl * H + h: sl * H + h + 1], s0,
                                     ALU.mult, ALU.add), reads=[("exl", e), "coef", "s0"], writes=["s0"])
                hi = av(SIN_HI + h * 256, 256)
                lo = av(SIN_LO + h * 256, 256)
                fw.op("dve", lambda E, hi=hi, s0=s0: E.tensor_copy(out=hi, in_=s0), reads=["s0"], writes=["sinhi"])
                fw.op("dve", tt(s0, s0, hi, ALU.subtract), reads=["s0", "sinhi"], writes=["s0"])
                fw.op("dve", lambda E, lo=lo, s0=s0: E.tensor_copy(out=lo, in_=s0), reads=["s0"], writes=["sinlo"])
            SB_O = EXL + 4 * G * 256
            SC_O = SB_O + 1024
            TF_O = SC_O + 512
            U_O = TF_O + 2048
            assert U_O + 1024 <= RTMP + RTMP_SZ, "RTMP overflow"

            def head_gen(h, pi):
                s = pi
                qT, kT, v, sg = head_views(s)
                ld("sp", qT, ZQ[h], writes=[("hq", s)])
                ld("sp", kT, ZK[h], writes=[("hk", s)])
                ld("sp", v.rearrange("p (n e) -> p n e", n=NT), ZVv[:, :, h * 256:(h + 1) * 256], writes=[("hv", s)])
                ld("sp", sg.rearrange("p (n e) -> p n e", n=NT), ZG[:, :, h * 256:(h + 1) * 256], writes=[("hg", s)])
                if pi == 0:
                    acc, acck = psS[:, 0:256], ("psS", 0)
                else:
                    acc, acck = psG[3][:, 0:256], ("psG", 3)
                fw.pe_group([mm(acc, ident_b[:], av(SIN_HI + h * 256, 256), True, False),
                             mm(acc, ident_b[:], av(SIN_LO + h * 256, 256), False, False)],
                            reads=["sinhi", "sinlo", "identb"], writes=[acck])
                o_all = avf(O_OFF + pi * 2 * NT * 256, 2 * NT * 256)
                mv = small[:, pi * 32: pi * 32 + 2 * NT]
                yield
                for n in range(NT):
                    sbi = pi * 2 + n % 2
                    Sb = av(SB_O + sbi * 256, 256)
                    fw.op("act", actf(Sb, acc, AF.Copy), reads=[acck], writes=[("Sb", sbi)])
                    kt, ktk = kt_make(h, n, kT, s, pi)
                    fw.pe_group([mm(psX[:, 0:128], kT[:, n * 128:(n + 1) * 128], qT[:, n * 128:(n + 1) * 128], True, True)],
                                reads=[("hk", s), ("hq", s)], writes=["psX"])
                    sc = av(SC_O + sbi * 128, 128)
                    fw.op("dve", stt(sc, psX[:, 0:128], kdec_t[:, n * H + h:n * H + h + 1],
                                     mask_t[:, h * 128:(h + 1) * 128], ALU.mult, ALU.mult),
                          reads=["psX", "kdec", "mask"], writes=[("sc", sbi)])
                    yield
                    gi = nxt("g3", 3)
                    po = psG[gi][:, 0:256]
                    fw.pe_group([mm(po, sc, v[:, n * 256:(n + 1) * 256], True, False),
                                 mm(po, qT[:, n * 128:(n + 1) * 128], Sb, False, True)],
                                reads=[("sc", sbi), ("Sb", sbi), ("hv", s), ("hq", s)], writes=[("psG", gi)])
                    fw.pe_group([mm(acc, kt, v[:, n * 256:(n + 1) * 256], False, n == NT - 1)],
                                reads=[ktk, ("hv", s)], writes=[acck])
                    fw.op("act", actf(o_all[:, n * 256:(n + 1) * 256], po, AF.Copy, scale=odec_t[:, n * H + h:n * H + h + 1]),
                          reads=[("psG", gi), "odec"], writes=[("o", pi, n)])
                    st6 = small[:, 64 + sbi * 6: 64 + sbi * 6 + 6]
                    fw.op("dve", lambda E, n=n, st6=st6, o_all=o_all: E.bn_stats(out=st6, in_=o_all[:, n * 256:(n + 1) * 256]),
                          reads=[("o", pi, n)], writes=[("st6", sbi)])
                    fw.op("dve", lambda E, n=n, st6=st6, mv=mv: E.bn_aggr(out=mv[:, 2 * n:2 * n + 2], in_=st6),
                          reads=[("st6", sbi)], writes=[("rmv", pi)])
                    yield
                rs = small[:, 128 + pi * 32:128 + pi * 32 + NT]
                fw.op("act", actf(rs, mv.rearrange("p (t two) -> p t two", two=2)[:, :, 1], AF.Sqrt, bias=EPS),
                      reads=[("rmv", pi)], writes=[("rrs", pi)])
                fw.op("dve", lambda E, rs=rs: E.reciprocal(out=rs, in_=rs), reads=[("rrs", pi)], writes=[("rrs", pi)])
                gth = av(GTH + s * 2 * T, 2 * T)
                yield
                for n in range(NT):
                    ti2 = pi * 2 + n % 2
                    tf = avf(TF_O + ti2 * 512, 512)
                    fw.op("dve", tsc(tf, o_all[:, n * 256:(n + 1) * 256], mv[:, 2 * n:2 * n + 1], rs[:, n:n + 1],
                                     ALU.subtract, ALU.mult), reads=[("o", pi, n), ("rmv", pi), ("rrs", pi)], writes=[("tf", ti2)])
                    u = av(U_O + ti2 * 256, 256)
                    fw.op("dve", tt(u, tf, sg[:, n * 256:(n + 1) * 256], ALU.mult),
                          reads=[("tf", ti2), ("hg", s)], writes=[("u", ti2)])
                    ti = nxt("t", 2)
                    fw.pe_group([trp(psT[ti][:, 0:128], u[:, 0:128], ident_b[:]),
                                 trp(psT[ti][:, 128:256], u[:, 128:256], ident_b[:])],
                                reads=[("u", ti2), "identb"], writes=[("psT", ti)])
                    yield
                    for e2 in range(2):
                        fw.op("act", actf(gth[:, e2 * T + n * 128: e2 * T + (n + 1) * 128], psT[ti][:, e2 * 128:(e2 + 1) * 128],
                                          AF.Copy, scale=gnw_t[:, l * RC + h * 2 + e2: l * RC + h * 2 + e2 + 1]),
                              reads=[("psT", ti), "gnw"], writes=[("gth", s)])
                    yield
                ld("pool", GT[h * 2:h * 2 + 2].rearrange("k p t -> p k t"), gth.rearrange("p (k t) -> p k t", k=2),
                   reads=[("gth", s)], writes=["GT"], group="g_GT")

            for h0 in range(0, H, 2):
                gens = [head_gen(h0 + pi, pi) for pi in range(min(2, H - h0))]
                alive = list(gens)
                while alive:
                    for g_ in list(alive):
                        try:
                            next(g_)
                        except StopIteration:
                            alive.remove(g_)

        def load_bufA(src, nch):
            grp = newgroup()
            for k0 in range(0, nch, 4):
                k1 = min(nch, k0 + 4)
                ld("sp", arena[:, BUFA + k0 * T: BUFA + k1 * T].rearrange("p (k t) -> p k t", t=T),
                   src[k0:k1].rearrange("k p t -> p k t"), writes=["bufA"], group=grp)
            assert nch * T <= ARENA

        def phase3(l):
            load_bufA(GT, RC)

            def mk_units(gate, first):
                def units(blk, slot):
                    us = []
                    for m in range(WBC // 128):
                        ch = blk * (WBC // 128) + m
                        for tb in range(NB):
                            def epi(ps, gi, ch=ch, tb=tb):
                                li = nxt("lb", 2)
                                ld("sp", ld_b[li][:], gate[ch, :, tb * 512:(tb + 1) * 512], writes=[("ldb", li)])
                                if first:
                                    fw.op("dve", tt(mT(ch, tb * 512, 512), ps, ld_b[li][:], ALU.mult),
                                          reads=[("psG", gi), ("ldb", li)], writes=[("mT", ch, tb)])
                                else:
                                    fi = nxt("sf", 3)
                                    fw.op("dve", tt(stg_f[fi][:], ps, ld_b[li][:], ALU.mult),
                                          reads=[("psG", gi), ("ldb", li)], writes=[("stgf", fi)])
                                    fw.op("dve", tt(mT(ch, tb * 512, 512), mT(ch, tb * 512, 512), stg_f[fi][:], ALU.add),
                                          reads=[("stgf", fi), ("mT", ch, tb)], writes=[("mT", ch, tb)])
                                return None
                            us.append(dict(lhs=lambda k, slot=slot, m=m: wslice(slot, k, m * 128, 128),
                                           rhs=lambda k, tb=tb: hT(k, tb * 512, 512), n=512, epi=epi))
                    return us
                return units

            gemm(hT, RC, D // WBC, wload_cols(ret_proj[l], RC, lambda blk: blk * WBC), mk_units(GA, True), tag=("p3a", l))
            prefetch_w(wload_cols(sgu_proj[l], KC, lambda blk: blk * WBC), ("p3b", l))
            fw.barrier()
            load_bufA(ST, KC)
            gemm(hT, KC, D // WBC, wload_cols(sgu_proj[l], KC, lambda blk: blk * WBC), mk_units(GB, False), tag=("p3b", l))
            prefetch_w(wload_cols(w_out[l], KC, lambda blk: blk * WBC), ("p3c", l))
            fw.barrier()
            resid_gemm(mT, KC, w_out[l], tag=("p3c", l))
            prefetch_w(wl4(l), ("p4", l))

        def resid_gemm(Xfn, kc, Wl, tag=None):
            def units(blk, slot):
                us = []
                for m in range(WBC // 128):
                    ch = blk * (WBC // 128) + m
                    for tb in range(NB):
                        def epi(ps, gi, ch=ch, tb=tb):
                            li = nxt("lf", 2)
                            ld("sp", ld_f[li][:], XT[ch, :, tb * 512:(tb + 1) * 512], reads=[("XT", ch, tb)], writes=[("ldf", li)])
                            fi = nxt("sf", 3)
                            fw.op("dve", tt(stg_f[fi][:], ps, ld_f[li][:], ALU.add),
                                  reads=[("psG", gi), ("ldf", li)], writes=[("stgf", fi)])
                            ld("pool", XT[ch, :, tb * 512:(tb + 1) * 512], stg_f[fi][:], reads=[("stgf", fi)],
                               writes=[("XT", ch, tb)])
                            return None
                        us.append(dict(lhs=lambda k, slot=slot, m=m: wslice(slot, k, m * 128, 128),
                                       rhs=lambda k, tb=tb: Xfn(k, tb * 512, 512), n=512, epi=epi))
                return us
            gemm(Xfn, kc, D // WBC, wload_cols(Wl, kc, lambda blk: blk * WBC), units, tag=tag)

        def wl4(l):
            Wl = w_ffn_in[l]
            HB = WBC // 2

            return wload_cols(Wl, KC, lambda b: b * WBC)

        def phase4(l):
            HB = WBC // 2
            nblk = DFF // HB
            wl = wl4(l)

            def units(blk, slot):
                us = []
                for m in range(HB // 128):
                    ch = blk * (HB // 128) + m
                    for tb in range(NB):
                        hold = {}

                        def epi_a(ps, gi, hold=hold):
                            fi = nxt("sf", 3)
                            fw.op("act", actf(stg_f[fi][:], ps, AF.Silu), reads=[("psG", gi)], writes=[("stgf", fi)])
                            hold["fi"] = fi
                            return None

                        def epi_c(ps, gi, hold=hold, ch=ch, tb=tb):
                            fi = hold["fi"]
                            si = nxt("sb", NSB)
                            fw.op("dve", tt(stg_b[si][:], ps, stg_f[fi][:], ALU.mult),
                                  reads=[("psG", gi), ("stgf", fi)], writes=[("stgb", si)])
                            ld("pool", AT[ch, :, tb * 512:(tb + 1) * 512], stg_b[si][:], reads=[("stgb", si)], writes=["AT"], group="g_AT")
                            return None
                        us.append(dict(lhs=lambda k, slot=slot, m=m: wslice(slot, k, m * 128, 128),
                                       rhs=lambda k, tb=tb: hT(k, tb * 512, 512), n=512, epi=epi_a))
                        us.append(dict(lhs=lambda k, slot=slot, m=m: wslice(slot, k, HB + m * 128, 128),
                                       rhs=lambda k, tb=tb: hT(k, tb * 512, 512), n=512, epi=epi_c))
                return us
            gemm(hT, KC, nblk, wl, units, tag=("p4", l))
            prefetch_w(wload_cols(w_ffn_out[l][0:KH * 128, :], KH, lambda blk: blk * WBC), ("p5", l, 0))

        def phase5(l):
            Wl = w_ffn_out[l]
            for half in range(2):
                load_bufA(AT[half * KH:(half + 1) * KH], KH)
                resid_gemm(hT, KH, Wl[half * KH * 128:(half + 1) * KH * 128, :], tag=("p5", l, half))
                if half == 0:
                    prefetch_w(wload_cols(Wl[KH * 128:2 * KH * 128, :], KH, lambda blk: blk * WBC), ("p5", l, 1))

        import os as _os
        STOP = int(_os.environ.get("K_STOP", "99"))

        def program():
            ingest()
            fw.barrier()
            if STOP <= 1:
                return
            for l in range(NL):
                load_layer_small(l)
                norm_pass(lambda k, l=l: nw1_t[:, l * KC + k: l * KC + k + 1])
                fw.barrier()
                if STOP <= 2:
                    return
                phase1(l)
                fw.barrier()
                if STOP <= 3:
                    return
                exchange()
                if STOP <= 4:
                    return
                sgu(l)
                fw.barrier()
                if STOP <= 5:
                    return
                prefetch_w(wload_cols(ret_proj[l], RC, lambda blk: blk * WBC), ("p3a", l))
                pass_b(l)
                fw.barrier()
                if STOP <= 6:
                    return
                phase3(l)
                fw.barrier()
                if STOP <= 7:
                    return
                norm_pass(lambda k, l=l: nw2_t[:, l * KC + k: l * KC + k + 1])
                fw.barrier()
                if STOP <= 8:
                    return
                phase4(l)
                fw.barrier()
                if STOP <= 9:
                    return
                phase5(l)
                fw.barrier()
                if STOP <= 10:
                    return
            norm_pass(lambda k: nwf_t[:, k:k + 1], final=True)

        program()
        fw.finish()
        fw.run(block)
    return nc


def _run(cfg, x, norm_mix_w, w_in, ret_gn_w, ret_proj, sgu_ln_w, sgu_ln_b, sgu_w_s, sgu_b_s, sgu_proj, w_out,
         norm_ffn_w, w_ffn_in, w_ffn_out, final_norm_w, trace=False):
    c = cfg
    f = lambda a: np.ascontiguousarray(np.asarray(a, dtype=np.float32))
    x = f(x).reshape(c.BATCH * c.SEQ, c.D)
    shared = {
        "w_in": f(w_in), "ret_proj": f(ret_proj), "sgu_proj": f(sgu_proj), "w_out": f(w_out),
        "w_ffn_in": f(f(w_ffn_in).reshape(c.NL, c.D, 2, c.FC, 128).transpose(0, 1, 3, 2, 4).reshape(c.NL, c.D, 2 * c.DFF)),
        "w_ffn_out": f(w_ffn_out),
        "norm_mix_w": f(f(norm_mix_w).reshape(c.NL, c.KC, 128).transpose(0, 2, 1)),
        "norm_ffn_w": f(f(norm_ffn_w).reshape(c.NL, c.KC, 128).transpose(0, 2, 1)),
        "final_norm_w": f(f(final_norm_w).reshape(c.KC, 128).transpose(1, 0)),
        "ret_gn_w": f(f(ret_gn_w).reshape(c.NL, c.RC, 128).transpose(0, 2, 1)),
        "sgu_ln_w": f(sgu_ln_w), "sgu_ln_b": f(sgu_ln_b), "sgu_w_s": f(sgu_w_s),
        "sgu_b_s": f(f(sgu_b_s).reshape(c.NL, c.SG * 128)),
    }
    nc = _build(c)
    in_maps = []
    for core in range(c.NCORES):
        m = dict(shared)
        m["x"] = x[core * c.T:(core + 1) * c.T]
        m.update(_tables(c, core))
        in_maps.append(m)
    res = run_bass_kernel_spmd(nc, in_maps, core_ids=list(range(c.NCORES)), **({"trace": True} if trace else {}))
    y = np.concatenate([res.results[i]["y"] for i in range(c.NCORES)], 0)
    return y.reshape(c.BATCH, c.SEQ, c.D).astype(np.float32), res


def kernel(**inputs):
    cfg = Cfg()
    y, _ = _run(cfg, **inputs)
    return y
```

```python
import math
from contextlib import ExitStack

import numpy as np
import concourse.bass as bass
import concourse.mybir as mybir
from concourse.bass_utils import run_bass_kernel_spmd

F32 = mybir.dt.float32
BF16 = mybir.dt.bfloat16
AF = mybir.ActivationFunctionType
ALU = mybir.AluOpType
EPS = 1e-6
ENGS = ("pe", "act", "dve", "pool", "sp")


class Fw:
    def __init__(self, nc, stack, n_dma_slots=8):
        self.nc = nc
        self.q = {e: [] for e in ENGS}
        self.sem = {e: stack.enter_context(nc.semaphore("s_" + e)) for e in ENGS}
        self.cnt = {e: 0 for e in ENGS}
        self.waited = {e: {} for e in ENGS}
        self.last_w = {}
        self.readers = {}
        self.slots = {}
        self.slot_rr = {}
        for qn in ("sp", "pool"):
            self.slots[qn] = [[stack.enter_context(nc.semaphore("d_%s%d" % (qn, i))), 0]
                              for i in range(n_dma_slots)]
            self.slot_rr[qn] = 0
        self.cc_sem = stack.enter_context(nc.semaphore("cc_sem"))
        self.coll_cnt = 0

    def _wait(self, eng, ev):
        if ev is None:
            return
        sem, val, src = ev[0], ev[1], ev[2]
        if src == eng and eng == "pe":
            return
        key = id(sem)
        if self.waited[eng].get(key, 0) >= val:
            return
        self.waited[eng][key] = val
        self.q[eng].append(lambda E, sem=sem, val=val: E.wait_ge(sem, val))

    def _deps(self, eng, reads, writes, group=None):
        for r in reads:
            for ev in self.last_w.get(r, ()):
                self._wait(eng, ev)
        for w in writes:
            for ev in self.last_w.get(w, ()):
                if group is not None and len(ev) > 3 and ev[3] == group:
                    continue
                self._wait(eng, ev)
            for ev in self.readers.get(w, ()):
                self._wait(eng, ev)

    def _record(self, ev, reads, writes):
        for w in writes:
            if ev[2] is None:
                lst = [e for e in self.last_w.get(w, ()) if e[2] is None and e[0] is not ev[0]]
                lst.append(ev)
                self.last_w[w] = lst
            else:
                self.last_w[w] = [ev]
            self.readers[w] = []
        for r in reads:
            lst = self.readers.setdefault(r, [])
            lst[:] = [e for e in lst if e[0] is not ev[0]]
            lst.append(ev)

    def op(self, eng, fn, reads=(), writes=()):
        self._deps(eng, reads, writes)
        self.cnt[eng] += 1
        sem = self.sem[eng]
        ev = (sem, self.cnt[eng], eng)
        self.q[eng].append(lambda E, fn=fn, sem=sem: fn(E).then_inc(sem, 1))
        self._record(ev, reads, writes)
        return ev

    def pe_group(self, fns, reads=(), writes=()):
        self._deps("pe", reads, writes)
        for fn in fns[:-1]:
            self.q["pe"].append(fn)
        self.cnt["pe"] += 1
        sem = self.sem["pe"]
        ev = (sem, self.cnt["pe"], "pe")
        last = fns[-1]
        self.q["pe"].append(lambda E, fn=last, sem=sem: fn(E).then_inc(sem, 1))
        self._record(ev, reads, writes)
        return ev

    def coll(self, fn, reads=(), writes=()):
        self._deps("pool", reads, writes)
        self.coll_cnt += 1
        n = self.coll_cnt
        sem = self.cc_sem
        self.q["pool"].append(lambda E, fn=fn, sem=sem: fn(E).then_inc(sem, 1))
        self.q["pool"].append(lambda E, sem=sem, n=n: E.wait_ge(sem, n))
        self.cnt["pool"] += 1
        psem = self.sem["pool"]
        ev = (psem, self.cnt["pool"], "pool")
        self.q["pool"].append(lambda E, psem=psem: E.sem_inc(psem, 1))
        self._record(ev, reads, writes)
        return ev

    def dma(self, qn, fn, reads=(), writes=(), group=None):
        self._deps(qn, reads, writes, group)
        i = self.slot_rr[qn]
        self.slot_rr[qn] = (i + 1) % len(self.slots[qn])
        slot = self.slots[qn][i]
        sem = slot[0]
        if slot[1] > 0:
            self._wait(qn, (sem, 16 * slot[1], None))
        slot[1] += 1
        ev = (sem, 16 * slot[1], None, group)
        self.q[qn].append(lambda E, fn=fn, sem=sem: fn(E).then_inc(sem, 16))
        self._record(ev, reads, writes)
        return ev

    def _sp_wait_all(self):
        for q2 in self.slots:
            for sem, c in self.slots[q2]:
                if c > 0:
                    self._wait("sp", (sem, 16 * c, None))
        for e in ENGS:
            if self.cnt[e] > 0 and e != "sp":
                self._wait("sp", (self.sem[e], self.cnt[e], e))

    def barrier(self):
        self._sp_wait_all()
        self.cnt["sp"] += 1
        sem = self.sem["sp"]
        ev = (sem, self.cnt["sp"], "sp")
        self.q["sp"].append(lambda E, sem=sem: E.sem_inc(sem, 1))
        for e in ENGS:
            if e != "sp":
                self._wait(e, ev)

    def finish(self):
        self._sp_wait_all()

    def run(self, block):
        q = self.q

        @block.tensor
        def _(E):
            for f in q["pe"]:
                f(E)

        @block.scalar
        def _(E):
            for f in q["act"]:
                f(E)

        @block.vector
        def _(E):
            for f in q["dve"]:
                f(E)

        @block.gpsimd
        def _(E):
            for f in q["pool"]:
                f(E)

        @block.sync
        def _(E):
            for f in q["sp"]:
                f(E)


class Cfg:
    def __init__(self, D=2048, H=8, DFF=5632, T=2048, NL=4, NCORES=8, SEQ=8192, BATCH=2):
        self.D, self.H, self.DFF, self.T, self.NL = D, H, DFF, T, NL
        self.NCORES, self.SEQ, self.BATCH = NCORES, SEQ, BATCH
        self.G = SEQ // T
        assert self.G * BATCH == NCORES
        self.KC = D // 128
        self.QK = H * 128
        self.RV = H * 256
        self.RC = self.RV // 128
        self.SG = D // 256
        self.NT = T // 128
        self.NB = T // 512
        self.FC = DFF // 128
        self.INC = 2 * self.QK + 2 * self.RV + 4 * D
        self.oq, self.ok = 0, self.QK
        self.ov = 2 * self.QK
        self.og = self.ov + self.RV
        self.osu = self.og + self.RV
        self.osv = self.osu + D
        self.oga = self.osv + D
        self.ogb = self.oga + D


def _tables(cfg, core):
    H, T, NT, G = cfg.H, cfg.T, cfg.NT, cfg.G
    rank = core % G
    pos0 = rank * T
    half = 64
    inv = (10000.0 ** (-np.arange(half, dtype=np.float32) / np.float32(half))).astype(np.float32)
    pos = (pos0 + np.arange(T)).astype(np.float32)
    ang = (pos[None, :] * inv[:, None]).astype(np.float32)
    cos = np.cos(ang).astype(np.float32)
    sin = np.sin(ang).astype(np.float32)
    tb_cos = np.concatenate([cos, cos], 0)
    tb_sin = np.concatenate([-sin, sin], 0)
    logg = np.log1p(-(2.0 ** (-5.0 - np.arange(H, dtype=np.float64))))
    p = np.arange(128, dtype=np.float64)
    n = np.arange(NT, dtype=np.float64)
    loc = n[None, :, None] * 128 + p[:, None, None]
    kdec = np.exp(-logg[None, None, :] * loc)
    odec = np.exp(logg[None, None, :] * loc) * (128.0 ** -0.5)
    coef = np.zeros((128, G, H), np.float64)
    for s in range(G):
        if s < rank:
            coef[:, s, :] = np.exp(logg * T * (rank - s - 1))[None, :]
    oneh = np.zeros((128, G, H), np.float64)
    oneh[:, rank, :] = np.exp(logg * T)[None, :]
    gT = np.broadcast_to(np.exp(logg * T)[None, :], (128, H))
    j = np.arange(128)[:, None]
    i = np.arange(128)[None, :]
    cj, ci = j // 64, i // 64
    mask = np.zeros((128, H, 128), np.float64)
    for h in range(H):
        m = np.where(i >= j, 1.0, np.exp(logg[h] * 2.0 * (j - i)))
        m = np.where(cj > ci, 0.0, m)
        m = np.where(cj < ci, 1.0, m)
        mask[:, h, :] = m
    cmask = (cj <= ci).astype(np.float32)
    ident = np.eye(128, dtype=np.float32)
    pm = np.zeros((128, 128), np.float32)
    for d in range(128):
        pm[(d + 64) % 128, d] = 1.0
    f = lambda a: np.ascontiguousarray(np.asarray(a, dtype=np.float32))
    return {
        "tb_cos": f(tb_cos), "tb_sin": f(tb_sin),
        "tb_kdec": f(kdec.reshape(128, NT * H)), "tb_odec": f(odec.reshape(128, NT * H)),
        "tb_coef": f(coef.reshape(128, G * H)), "tb_oneh": f(oneh.reshape(128, G * H)), "tb_gT": f(gT),
        "tb_mask": f(mask.reshape(128, H * 128)), "tb_cmask": f(cmask),
        "tb_ident": f(ident), "tb_pm": f(pm), "tb_ones": np.ones((128, 128), np.float32),
    }


def _build(cfg):
    c = cfg
    D, H, T, NL, KC, NT, NB, FC, RC, SG, G, DFF = c.D, c.H, c.T, c.NL, c.KC, c.NT, c.NB, c.FC, c.RC, c.SG, c.G, c.DFF
    nc = bass.Bass("TRN2", target_bir_lowering=False)

    def din(name, shape):
        return nc.dram_tensor(name, list(shape), F32, kind="ExternalInput").ap()

    def dscr(name, shape, dt):
        return nc.dram_tensor(name, list(shape), dt, kind="Internal").ap()

    x_in = din("x", [T, D])
    w_in = din("w_in", [NL, D, c.INC])
    ret_proj = din("ret_proj", [NL, c.RV, D])
    sgu_proj = din("sgu_proj", [NL, D, D])
    w_out = din("w_out", [NL, D, D])
    w_ffn_in = din("w_ffn_in", [NL, D, 2 * DFF])
    w_ffn_out = din("w_ffn_out", [NL, DFF, D])
    nw1_d = din("norm_mix_w", [NL, 128, KC])
    nw2_d = din("norm_ffn_w", [NL, 128, KC])
    nwf_d = din("final_norm_w", [128, KC])
    gnw_d = din("ret_gn_w", [NL, 128, RC])
    lnw_d = din("sgu_ln_w", [NL, D])
    lnb_d = din("sgu_ln_b", [NL, D])
    ws_d = din("sgu_w_s", [NL, SG, 128, 128])
    bs_d = din("sgu_b_s", [NL, SG * 128])
    tbs = {}
    for nm, shp in (("tb_cos", [128, T]), ("tb_sin", [128, T]), ("tb_kdec", [128, NT * H]),
                    ("tb_odec", [128, NT * H]), ("tb_coef", [128, G * H]), ("tb_oneh", [128, G * H]),
                    ("tb_gT", [128, H]), ("tb_mask", [128, H * 128]), ("tb_cmask", [128, 128]),
                    ("tb_ident", [128, 128]), ("tb_pm", [128, 128]), ("tb_ones", [128, 128])):
        tbs[nm] = din(nm, shp)
    y_out = nc.dram_tensor("y", [T, D], F32, kind="ExternalOutput").ap()

    XT = dscr("XT", [KC, 128, T], F32)
    ZQ = dscr("ZQ", [H, 128, T], BF16)
    ZK = dscr("ZK", [H, 128, T], BF16)
    ZVv = dscr("ZVv", [128, NT, c.RV], BF16)
    ZG = dscr("ZG", [128, NT, c.RV], BF16)
    ZU = dscr("ZU", [KC, 128, T], BF16)
    ZS = dscr("ZS", [128, NT, D], BF16)
    GA = dscr("GA", [KC, 128, T], BF16)
    GB = dscr("GB", [KC, 128, T], BF16)
    GT = dscr("GT", [RC, 128, T], BF16)
    ST = dscr("ST", [KC, 128, T], BF16)
    AT = dscr("AT", [FC, 128, T], BF16)
    EXI = dscr("EXI", [G * H * 128, 256], F32)
    EXO = dscr("EXO", [G * H * 128, 256], F32)

    WBC = 256
    ARENA = 65536
    with ExitStack() as st:
        fw = Fw(nc, st)
        sb = lambda name, shape, dt: st.enter_context(nc.sbuf_tensor(name, list(shape), dt))
        arena = sb("arena", [128, ARENA], BF16)
        KH = FC // 2
        assert FC % 2 == 0
        wb = [sb("wb%d" % i, [128, max(KC, KH) * WBC], BF16) for i in range(2)]
        WCTX = dict(slots=[w[:] for w in wb], wbc=WBC, key="wb")
        kdec_t = sb("kdec_t", [128, NT * H], F32)
        odec_t = sb("odec_t", [128, NT * H], F32)
        coef_t = sb("coef_t", [128, G * H], F32)
        oneh_t = sb("oneh_t", [128, G * H], F32)
        gT_t = sb("gT_t", [128, H], F32)
        mask_t = sb("mask_t", [128, H * 128], F32)
        cmask_t = sb("cmask_t", [128, 128], F32)
        ident_b = sb("ident_b", [128, 128], BF16)
        ident_f = sb("ident_f", [128, 128], F32)
        pm_b = sb("pm_b", [128, 128], BF16)
        ones_b = sb("ones_b", [128, 128], BF16)
        nw1_t = sb("nw1_t", [128, NL * KC], F32)
        nw2_t = sb("nw2_t", [128, NL * KC], F32)
        nwf_t = sb("nwf_t", [128, KC], F32)
        gnw_t = sb("gnw_t", [128, NL * RC], F32)
        lnw_t = sb("lnw_t", [128, D], BF16)
        lnb_t = sb("lnb_t", [128, D], BF16)
        wsb_t = sb("wsb_t", [128, SG * 128], BF16)
        wT_t = sb("wT_t", [128, SG * 128], BF16)
        bsf_t = sb("bsf_t", [1, SG * 128], F32)
        bsh_t = sb("bsh_t", [1, SG * 128], BF16)
        bsr_t = sb("bsr_t", [1, SG * 128], F32)
        bsl_t = sb("bsl_t", [1, SG * 128], BF16)
        NSB = 4
        stg_b = [sb("stgb%d" % i, [128, 512], BF16) for i in range(NSB)]
        stg_f = [sb("stgf%d" % i, [128, 512], F32) for i in range(3)]
        ld_b = [sb("ldb%d" % i, [128, 512], BF16) for i in range(2)]
        ld_f = [sb("ldf%d" % i, [128, 512], F32) for i in range(2)]
        rstd_t = sb("rstd_t", [128, 512], F32)
        small = sb("small", [128, 256], F32)
        s0_t = sb("s0_t", [128, 256], F32)
        psG = [st.enter_context(nc.psum_tensor("psG%d" % i, [128, 512], F32)) for i in range(4)]
        psT = [st.enter_context(nc.psum_tensor("psT%d" % i, [128, 1024], BF16)) for i in range(2)]
        psX = st.enter_context(nc.psum_tensor("psX", [128, 512], F32))
        psS = st.enter_context(nc.psum_tensor("psS", [128, 512], F32))
        block = st.enter_context(nc.Block())

        rr = {"g": 0, "g3": 0, "sb": 0, "sf": 0, "lb": 0, "lf": 0, "t": 0}

        def nxt(kind, n):
            i = rr[kind]
            rr[kind] = (i + 1) % n
            return i

        def av(off, n):
            return arena[:, off:off + n]

        def avf(off, n):
            return arena[:, off:off + n].bitcast(F32)

        def mm(ps, lhsT, rhs, start, stop):
            return lambda E: E.matmul(ps, lhsT=lhsT, rhs=rhs, start=start, stop=stop)

        def trp(out, in_, ident):
            return lambda E: E.transpose(out=out, in_=in_, identity=ident)

        def actf(out, in_, func, scale=1.0, bias=0.0):
            return lambda E: E.activation(out=out, in_=in_, func=func, bias=bias, scale=scale)

        def tt(out, in0, in1, op):
            return lambda E: E.tensor_tensor(out=out, in0=in0, in1=in1, op=op)

        def stt(out, in0, scalar, in1, op0, op1):
            return lambda E: E.scalar_tensor_tensor(out=out, in0=in0, scalar=scalar, in1=in1, op0=op0, op1=op1)

        def tsc(out, in0, s1, s2, op0, op1):
            return lambda E: E.tensor_scalar(out=out, in0=in0, scalar1=s1, scalar2=s2, op0=op0, op1=op1)

        gid = [0]

        def newgroup():
            gid[0] += 1
            return gid[0]

        def ld(qn, out, in_, reads=(), writes=(), group=None):
            return fw.dma(qn, lambda E: E.dma_start(out=out, in_=in_), reads=reads, writes=writes, group=group)

        ld("pool", ident_b[:], tbs["tb_ident"], writes=["identb"])
        ld("pool", pm_b[:], tbs["tb_pm"], writes=["pmb"])
        ld("pool", ones_b[:], tbs["tb_ones"], writes=["onesb"])
        ld("sp", ident_f[:], tbs["tb_ident"], writes=["identf"])
        ld("sp", kdec_t[:], tbs["tb_kdec"], writes=["kdec"])
        ld("sp", odec_t[:], tbs["tb_odec"], writes=["odec"])
        ld("sp", coef_t[:], tbs["tb_coef"], writes=["coef"])
        ld("sp", oneh_t[:], tbs["tb_oneh"], writes=["oneh"])
        ld("sp", gT_t[:], tbs["tb_gT"], writes=["gT"])
        ld("sp", mask_t[:], tbs["tb_mask"], writes=["mask"])
        ld("sp", cmask_t[:], tbs["tb_cmask"], writes=["cmask"])
        ld("sp", nw1_t[:].rearrange("p (l k) -> p l k", l=NL), nw1_d.rearrange("l p k -> p l k"), writes=["nw1"])
        ld("sp", nw2_t[:].rearrange("p (l k) -> p l k", l=NL), nw2_d.rearrange("l p k -> p l k"), writes=["nw2"])
        ld("sp", gnw_t[:].rearrange("p (l k) -> p l k", l=NL), gnw_d.rearrange("l p k -> p l k"), writes=["gnw"])
        ld("sp", nwf_t[:], nwf_d, writes=["nwf"])
        CONST_KEYS = ["cos", "sin", "identb", "pmb", "onesb", "identf", "kdec", "odec", "coef", "oneh", "gT",
                      "mask", "cmask", "nw1", "nw2", "gnw", "nwf"]

        def ingest():
            XS = 0
            for n in range(NT):
                s = n % 2
                xt = avf(XS + s * 2 * D, 2 * D)
                ld("sp", xt, x_in[n * 128:(n + 1) * 128, :], writes=[("xs", s)])
                for k0 in range(0, KC, 4):
                    gi = nxt("g", 4)
                    ps = psG[gi]
                    fw.pe_group([trp(ps[:, j * 128:(j + 1) * 128], xt[:, (k0 + j) * 128:(k0 + j + 1) * 128], ident_f[:])
                                 for j in range(4)], reads=[("xs", s), "identf"], writes=[("psG", gi)])
                    si = nxt("sf", 3)
                    fw.op("act", actf(stg_f[si][:], ps[:], AF.Copy), reads=[("psG", gi)], writes=[("stgf", si)])
                    ld("sp", XT[k0:k0 + 4, :, n * 128:(n + 1) * 128].rearrange("k p t -> p k t"),
                       stg_f[si][:].rearrange("p (k t) -> p k t", k=4), reads=[("stgf", si)], writes=["XT"], group="g_XT")

        BUFA = 0
        BUFB = KC * T

        def hT(k, t0, n):
            return arena[:, BUFA + k * T + t0: BUFA + k * T + t0 + n]

        def mT(k, t0, n):
            return arena[:, BUFB + k * T + t0: BUFB + k * T + t0 + n]

        def norm_pass(nw_ap_fn, final=False):
            XB = BUFB
            assert XB + 2 * KC * 1024 <= ARENA
            for tb in range(NB):
                xbk = ("xb", tb % 2)
                xb = avf(XB + (tb % 2) * KC * 1024, KC * 1024).rearrange("p (k t) -> p k t", k=KC)
                ld("sp", xb, XT[:, :, tb * 512:(tb + 1) * 512].rearrange("k p t -> p k t"),
                   reads=["XT"], writes=[xbk])
                fns = []
                for k in range(KC):
                    si = nxt("sb", NSB)
                    fw.op("act", actf(stg_b[si][:], xb[:, k, :], AF.Square), reads=[xbk], writes=[("stgb", si)])
                    fw.pe_group([mm(psX[:], ones_b[:], stg_b[si][:], k == 0, k == KC - 1)],
                                reads=[("stgb", si), "onesb"], writes=["psX"])
                fw.op("act", actf(rstd_t[:], psX[:], AF.Sqrt, scale=1.0 / D, bias=EPS), reads=["psX"], writes=["rstd"])
                fw.op("dve", lambda E: E.reciprocal(out=rstd_t[:], in_=rstd_t[:]), reads=["rstd"], writes=["rstd"])
                if not final:
                    for k in range(KC):
                        fw.op("dve", stt(hT(k, tb * 512, 512), xb[:, k, :], nw_ap_fn(k), rstd_t[:], ALU.mult, ALU.mult),
                              reads=[xbk, "rstd", "nw1", "nw2"], writes=["bufA"])
                else:
                    for k in range(KC):
                        fw.op("dve", stt(xb[:, k, :], xb[:, k, :], nw_ap_fn(k), rstd_t[:], ALU.mult, ALU.mult),
                              reads=[xbk, "rstd", "nwf"], writes=[xbk])
                    for tl in range(4):
                        n = tb * 4 + tl
                        s = n % 2
                        yt = avf(BUFA + s * 2 * D, 2 * D)
                        for k0 in range(0, KC, 4):
                            gi = nxt("g", 4)
                            ps = psG[gi]
                            fw.pe_group([trp(ps[:, j * 128:(j + 1) * 128], xb[:, k0 + j, tl * 128:(tl + 1) * 128], ident_f[:])
                                         for j in range(4)], reads=[xbk, "identf"], writes=[("psG", gi)])
                            fw.op("act", actf(yt[:, k0 * 128:(k0 + 4) * 128], ps[:], AF.Copy),
                                  reads=[("psG", gi)], writes=[("yt", s)])
                        ld("pool", y_out[n * 128:(n + 1) * 128, :], yt, reads=[("yt", s)], writes=["y"], group="g_y")

        pre = {"done": None}

        def prefetch_w(wload, tag):
            wload(0, 0)
            pre["done"] = tag

        def gemm(Xfn, kc, nblk, wload, units_of_block, wctx=None, tag=None, filler=None, filler_from=0):
            wctx = wctx or WCTX
            deferred = []
            if tag is not None and pre["done"] == tag:
                pre["done"] = None
            else:
                wload(0, 0)
            for blk in range(nblk):
                slot = blk % 2
                if blk + 1 < nblk:
                    wload(blk + 1, (blk + 1) % 2)
                for u in units_of_block(blk, slot):
                    gi = nxt("g", 4)
                    ps = psG[gi][:, 0:u["n"]]
                    fw.pe_group([mm(ps, u["lhs"](k), u["rhs"](k), k == 0, k == kc - 1) for k in range(kc)],
                                reads=[(wctx["key"], slot), "bufA", "bufB"], writes=[("psG", gi)])
                    for dfn in deferred:
                        dfn()
                    deferred = []
                    d = u["epi"](ps, gi)
                    if d is not None:
                        deferred.append(d)
                    if filler is not None and blk >= filler_from:
                        next(filler, None)
            for dfn in deferred:
                dfn()
            if filler is not None:
                for _ in filler:
                    pass

        def wslice(slot, k, c0, n, wctx=None):
            wctx = wctx or WCTX
            return wctx["slots"][slot][:, k * wctx["wbc"] + c0: k * wctx["wbc"] + c0 + n]

        def wload_cols(Wl, kc_total, col_of_blk, ncols=None, dst_off=0, wctx=None):
            wctx = wctx or WCTX
            wbc = wctx["wbc"]
            ncols = ncols or wbc

            def f(blk, slot):
                col = col_of_blk(blk)
                src = Wl.rearrange("(k p) n -> p k n", p=128)
                dst = wctx["slots"][slot].rearrange("p (k n) -> p k n", n=wbc)
                step = 4
                grp = newgroup()
                for k0 in range(0, kc_total, step):
                    k1 = min(kc_total, k0 + step)
                    ld("pool", dst[:, k0:k1, dst_off:dst_off + ncols], src[:, k0:k1, col:col + ncols],
                       writes=[(wctx["key"], slot)], group=grp)
            return f

        def phase1(l):
            Wl = w_in[l]
            W1 = 512 if c.QK % 512 == 0 else 256
            W1S = KC * W1
            w1ctx = dict(slots=[arena[:, BUFB + i * W1S: BUFB + (i + 1) * W1S] for i in range(2)], wbc=W1, key="wb1")
            CS = BUFB + 2 * W1S
            assert CS + 2 * T <= ARENA
            cos_v = arena[:, CS:CS + T]
            sin_v = arena[:, CS + T:CS + 2 * T]
            ld("pool", cos_v, tbs["tb_cos"], writes=["cos"])
            ld("pool", sin_v, tbs["tb_sin"], writes=["sin"])
            nblk = c.INC // W1
            kv = [b for b in range(nblk) if c.ok <= b * W1 < c.og]
            order = kv + [b for b in range(nblk) if b not in kv]
            PA0 = CS + 2 * T
            assert PA0 + T + NT * 256 + 512 + 4 * G * 256 <= ARENA, "arena overflow (pass A inside phase 1)"

            def kind_of(col):
                if col < c.ok:
                    return "q"
                if col < c.ov:
                    return "k"
                if col < c.og:
                    return "v"
                if col < c.osu:
                    return "g"
                if col < c.osv:
                    return "su"
                if col < c.oga:
                    return "sv"
                if col < c.ogb:
                    return "ga"
                return "gb"

            def units(blk, slot):
                col = order[blk] * W1
                kind = kind_of(col)
                us = []
                if kind in ("v", "g", "sv"):
                    base = {"v": c.ov, "g": c.og, "sv": c.osv}[kind]
                    dst = {"v": ZVv, "g": ZG, "sv": ZS}[kind]
                    func = {"v": AF.Copy, "g": AF.Silu, "sv": AF.Gelu}[kind]
                    for n in range(NT):
                        def epi(ps, gi, n=n, dst=dst, func=func, c0=col - base):
                            si = nxt("sb", NSB)
                            fw.op("act", actf(stg_b[si][:, 0:W1], ps, func), reads=[("psG", gi)], writes=[("stgb", si)])
                            ld("pool", dst[:, n, c0:c0 + W1], stg_b[si][:, 0:W1], reads=[("stgb", si)], writes=["Z"], group="g_Z")
                            return None
                        us.append(dict(lhs=lambda k, n=n: hT(k, n * 128, 128),
                                       rhs=lambda k, slot=slot: wslice(slot, k, 0, W1, w1ctx), n=W1, epi=epi))
                else:
                    base = {"q": c.oq, "k": c.ok, "su": c.osu, "ga": c.oga, "gb": c.ogb}[kind]
                    for m in range(W1 // 128):
                        ch = (col - base) // 128 + m
                        for tb in range(NB):
                            if kind in ("q", "k"):
                                dst = ZQ if kind == "q" else ZK

                                def epi(ps, gi, ch=ch, tb=tb, dst=dst):
                                    si = nxt("sb", NSB)
                                    zb = stg_b[si]
                                    fw.op("act", actf(zb[:], ps, AF.Copy), reads=[("psG", gi)], writes=[("stgb", si)])

                                    def later():
                                        fw.pe_group([mm(psX[:], pm_b[:], zb[:], True, True)],
                                                    reads=[("stgb", si), "pmb"], writes=["psX"])
                                        f1 = nxt("sf", 3)
                                        fw.op("dve", tt(stg_f[f1][:], zb[:], cos_v[:, tb * 512:(tb + 1) * 512], ALU.mult),
                                              reads=[("stgb", si), "cos"], writes=[("stgf", f1)])
                                        f2 = nxt("sf", 3)
                                        fw.op("dve", tt(stg_f[f2][:], psX[:], sin_v[:, tb * 512:(tb + 1) * 512], ALU.mult),
                                              reads=["psX", "sin"], writes=[("stgf", f2)])
                                        so = nxt("sb", NSB)
                                        fw.op("dve", tt(stg_b[so][:], stg_f[f1][:], stg_f[f2][:], ALU.add),
                                              reads=[("stgf", f1), ("stgf", f2)], writes=[("stgb", so)])
                                        ld("pool", dst[ch, :, tb * 512:(tb + 1) * 512], stg_b[so][:],
                                           reads=[("stgb", so)], writes=["Z"], group="g_Z")
                                    return later
                            else:
                                dst = {"su": ZU, "ga": GA, "gb": GB}[kind]
                                func = AF.Gelu if kind == "su" else AF.Sigmoid

                                def epi(ps, gi, ch=ch, tb=tb, dst=dst, func=func):
                                    si = nxt("sb", NSB)
                                    fw.op("act", actf(stg_b[si][:], ps, func), reads=[("psG", gi)], writes=[("stgb", si)])
                                    ld("pool", dst[ch, :, tb * 512:(tb + 1) * 512], stg_b[si][:],
                                       reads=[("stgb", si)], writes=["Z"], group="g_Z")
                                    return None
                            us.append(dict(lhs=lambda k, slot=slot, m=m: wslice(slot, k, m * 128, 128, w1ctx),
                                           rhs=lambda k, tb=tb: hT(k, tb * 512, 512), n=512, epi=epi))
                return us

            gemm(hT, KC, nblk, wload_cols(Wl, KC, lambda blk: order[blk] * W1, wctx=w1ctx), units, wctx=w1ctx,
                 filler=pass_a_gen(PA0), filler_from=len(kv) + 1)

        R_SLOT = 2 * T + 2 * NT * 256
        R0 = 0
        RTMP = R0 + 2 * R_SLOT
        RTMP_SZ = 14336
        PA_END = 512 + 4 * G * 256
        O_OFF = RTMP + RTMP_SZ
        GTH = O_OFF + 2 * 2 * NT * 256
        assert GTH + 4 * T <= ARENA, "arena overflow (retention)"
        S_OFF = RTMP + PA_END
        assert S_OFF + 4 * D + 2 * D + KC * 512 * 2 <= ARENA, "arena overflow (sgu)"

        def load_layer_small(l):
            ld("pool", lnw_t[:], lnw_d[l:l + 1, :].partition_broadcast(128), writes=["lnw"])
            ld("pool", lnb_t[:], lnb_d[l:l + 1, :].partition_broadcast(128), writes=["lnb"])
            ld("pool", wsb_t[:].rearrange("p (g j) -> p g j", g=SG), ws_d[l].rearrange("g i j -> i g j"), writes=["wsb"])
            ld("sp", bsf_t[:], bs_d[l:l + 1, :], writes=["bsf"])
            for g in range(SG):
                ti = nxt("t", 2)
                fw.pe_group([trp(psT[ti][:, 0:128], wsb_t[:, g * 128:(g + 1) * 128], ident_b[:])],
                            reads=["wsb", "identb"], writes=[("psT", ti)])
                fw.op("dve", tt(wT_t[:, g * 128:(g + 1) * 128], psT[ti][:, 0:128], cmask_t[:], ALU.mult),
                      reads=[("psT", ti), "cmask"], writes=["wT"])
            fw.op("dve", lambda E: E.tensor_copy(out=bsh_t[:], in_=bsf_t[:]), reads=["bsf"], writes=["bsh"])
            fw.op("dve", tt(bsr_t[:], bsf_t[:], bsh_t[:], ALU.subtract), reads=["bsf", "bsh"], writes=["bsr"])
            fw.op("dve", lambda E: E.tensor_copy(out=bsl_t[:], in_=bsr_t[:]), reads=["bsr"], writes=["bsl"])

        def head_views(s):
            b = R0 + s * R_SLOT
            qT = av(b, T)
            kT = av(b + T, T)
            v = av(b + 2 * T, NT * 256)
            sg = av(b + 2 * T + NT * 256, NT * 256)
            return qT, kT, v, sg

        def kt_make(h, n, kT, s, pi=0):
            ti = nxt("t", 2)
            fw.pe_group([trp(psT[ti][:, 0:128], kT[:, n * 128:(n + 1) * 128], ident_b[:])],
                        reads=[("hk", s), "identb"], writes=[("psT", ti)])
            kk = pi * 2 + n % 2
            kt = av(RTMP + kk * 128, 128)
            fw.op("act", actf(kt, psT[ti][:, 0:128], AF.Copy, scale=kdec_t[:, n * H + h:n * H + h + 1]),
                  reads=[("psT", ti), "kdec"], writes=[("kt", kk)])
            return kt, ("kt", kk)

        def pass_a_gen(PA0):
            exg = newgroup()
            kT = av(PA0, T)
            v = av(PA0 + T, NT * 256)
            KT0 = PA0 + T + NT * 256
            EX0 = KT0 + 512
            acc = psS[:, 0:256]
            acck = ("psS", 0)
            for h in range(H):
                ld("sp", kT, ZK[h], reads=["Z"], writes=["pa_k"])
                ld("sp", v.rearrange("p (n e) -> p n e", n=NT), ZVv[:, :, h * 256:(h + 1) * 256], reads=["Z"], writes=["pa_v"])
                prev = None
                for n in range(NT):
                    if prev is not None:
                        pk, pn = prev
                        fw.pe_group([mm(acc, pk, v[:, pn * 256:(pn + 1) * 256], pn == 0, pn == NT - 1)],
                                    reads=[("pa_kt", pn % 2), "pa_v"], writes=[acck])
                    ti = nxt("t", 2)
                    fw.pe_group([trp(psT[ti][:, 0:128], kT[:, n * 128:(n + 1) * 128], ident_b[:])],
                                reads=["pa_k", "identb"], writes=[("psT", ti)])
                    kt = av(KT0 + (n % 2) * 128, 128)
                    fw.op("act", actf(kt, psT[ti][:, 0:128], AF.Copy, scale=kdec_t[:, n * H + h:n * H + h + 1]),
                          reads=[("psT", ti), "kdec"], writes=[("pa_kt", n % 2)])
                    prev = (kt, n)
                    yield
                pk, pn = prev
                fw.pe_group([mm(acc, pk, v[:, pn * 256:(pn + 1) * 256], pn == 0, pn == NT - 1)],
                            reads=[("pa_kt", pn % 2), "pa_v"], writes=[acck])
                ex = avf(EX0 + (h % 2) * 2 * G * 256, 2 * G * 256)
                for sl in range(G):
                    fw.op("dve", lambda E, ex=ex, sl=sl, h=h: E.tensor_scalar_mul(
                        out=ex[:, sl * 256:(sl + 1) * 256], in0=acc, scalar1=oneh_t[:, sl * H + h: sl * H + h + 1]),
                        reads=[acck, "oneh"], writes=[("ex", h % 2)])
                ld("pool", EXI.rearrange("(g h p) e -> p g h e", g=G, h=H)[:, :, h, :],
                   ex.rearrange("p (g e) -> p g e", g=G), reads=[("ex", h % 2)], writes=["EXI"], group=exg)
                yield

        def exchange():
            rgs = [list(range(b * G, (b + 1) * G)) for b in range(c.BATCH)]
            fw.coll(lambda E: E.collective_compute(
                "AllReduce", ALU.add, replica_groups=rgs,
                ins=[EXI.opt()], outs=[EXO.opt()]), reads=["EXI"], writes=["EXO"])

        def sgu(l):
            TBS = 2
            TW = TBS * 128
            ZV_O = S_OFF
            TMP_O = ZV_O + 2 * TBS * D
            ZU_O = TMP_O + 2 * D
            OUT_O = ZU_O + 2 * KC * TW
            assert OUT_O + 2 * KC * TW <= ARENA, "arena overflow (sgu)"
            nb = NT // TBS
            nchk = max(1, D // 512)
            assert 32 + 2 * TBS * 6 * nchk <= 256

            def bufs(b):
                p = b % 2
                zv = av(ZV_O + p * TBS * D, TBS * D)
                zu = av(ZU_O + p * KC * TW, KC * TW).rearrange("p (k t) -> p k t", k=KC)
                out = av(OUT_O + p * KC * TW, KC * TW).rearrange("p (k t) -> p k t", k=KC)
                return p, zv, zu, out

            def loads(b):
                p, zv, zu, out = bufs(b)
                ld("sp", zv.rearrange("p (n d) -> p n d", n=TBS), ZS[:, b * TBS:(b + 1) * TBS, :],
                   writes=[("zv", p, tl) for tl in range(TBS)])
                ld("sp", zu, ZU[:, :, b * TW:(b + 1) * TW].rearrange("k p t -> p k t"), writes=[("zu", p)])

            loads(0)
            for b in range(nb):
                if b + 1 < nb:
                    loads(b + 1)
                p, zv, zu, out = bufs(b)
                vn = zv
                mv = small[:, p * 8: p * 8 + 2 * TBS]
                rs = small[:, 16 + p * 4: 16 + p * 4 + TBS]
                for tl in range(TBS):
                    so = 32 + (p * TBS + tl) * 6 * nchk
                    stats = small[:, so: so + 6 * nchk]
                    for cc in range(nchk):
                        w = min(512, D)
                        fw.op("dve", lambda E, tl=tl, cc=cc, stats=stats, w=w, zv=zv: E.bn_stats(
                            out=stats[:, cc * 6:(cc + 1) * 6], in_=zv[:, tl * D + cc * w: tl * D + (cc + 1) * w]),
                            reads=[("zv", p, tl)], writes=[("sgst", p, tl)])
                    fw.op("dve", lambda E, tl=tl, stats=stats, mv=mv: E.bn_aggr(
                        out=mv[:, tl * 2:tl * 2 + 2], in_=stats.rearrange("p (c s) -> p c s", s=6)),
                        reads=[("sgst", p, tl)], writes=[("sgmv", p)])
                mv3 = mv.rearrange("p (t two) -> p t two", two=2)
                fw.op("act", actf(rs, mv3[:, :, 1], AF.Sqrt, bias=EPS), reads=[("sgmv", p)], writes=[("sgrs", p)])
                fw.op("dve", lambda E, rs=rs: E.reciprocal(out=rs, in_=rs), reads=[("sgrs", p)], writes=[("sgrs", p)])
                for tl in range(TBS):
                    tmp = avf(TMP_O, 2 * D)
                    fw.op("dve", stt(tmp, zv[:, tl * D:(tl + 1) * D], mv[:, tl * 2:tl * 2 + 1], lnw_t[:],
                                     ALU.subtract, ALU.mult), reads=[("zv", p, tl), ("sgmv", p), "lnw"], writes=["sgtmp"])
                    fw.op("dve", stt(vn[:, tl * D:(tl + 1) * D], tmp, rs[:, tl:tl + 1], lnb_t[:], ALU.mult, ALU.add),
                          reads=["sgtmp", ("sgrs", p), "lnb"], writes=[("zv", p, tl)])
                for tl in range(TBS):
                    for k0 in range(0, KC, 4):
                        gi = nxt("g", 4)
                        ps = psG[gi]
                        fns = []
                        for j in range(4):
                            k = k0 + j
                            g = k // 2
                            o = ps[:, j * 128:(j + 1) * 128]
                            fns.append(mm(o, vn[:, tl * D + k * 128: tl * D + (k + 1) * 128], wT_t[:, g * 128:(g + 1) * 128], True, False))
                            fns.append(mm(o, ones_b[0:1, :], bsh_t[0:1, g * 128:(g + 1) * 128], False, False))
                            fns.append(mm(o, ones_b[0:1, :], bsl_t[0:1, g * 128:(g + 1) * 128], False, True))
                        fw.pe_group(fns, reads=[("zv", p, tl), "wT", "bsh", "bsl", "onesb"], writes=[("psG", gi)])
                        fw.op("dve", tt(out[:, k0:k0 + 4, tl * 128:(tl + 1) * 128],
                                        ps[:].rearrange("p (k t) -> p k t", k=4),
                                        zu[:, k0:k0 + 4, tl * 128:(tl + 1) * 128], ALU.mult),
                              reads=[("psG", gi), ("zu", p)], writes=[("sgout", p)])
                ld("sp", ST[:, :, b * TW:(b + 1) * TW].rearrange("k p t -> p k t"), out, reads=[("sgout", p)],
                   writes=["ST"], group="g_ST")

        def pass_b(l):
            SIN_HI = RTMP + 512
            SIN_LO = SIN_HI + H * 256
            EXL = SIN_LO + H * 256
            for h in range(H):
                e = h % 2
                exl = avf(EXL + e * 2 * G * 256, 2 * G * 256)
                ld("sp", exl.rearrange("p (g e) -> p g e", g=G),
                   EXO.rearrange("(g h p) e -> p g h e", g=G, h=H)[:, :, h, :], reads=["EXO"], writes=[("exl", e)])
                s0 = s0_t[:]
                fw.op("dve", lambda E, exl=exl, h=h, s0=s0: E.tensor_scalar_mul(out=s0, in0=exl[:, 0:256], scalar1=coef_t[:, h:h + 1]),
                      reads=[("exl", e), "coef"], writes=["s0"])
                for sl in range(1, G):
                    fw.op("dve", stt(s0, exl[:, sl * 256:(sl + 1) * 256], coef_t[:, sl * H + h: sl * H + h + 1], s0,
                                     ALU.mult, ALU.add), reads=[("exl", e), "coef", "s0"], writes=["s0"])
                hi = av(SIN_HI + h * 256, 256)
                lo = av(SIN_LO + h * 256, 256)
                fw.op("dve", lambda E, hi=hi, s0=s0: E.tensor_copy(out=hi, in_=s0), reads=["s0"], writes=["sinhi"])
                fw.op("dve", tt(s0, s0, hi, ALU.subtract), reads=["s0", "sinhi"], writes=["s0"])
                fw.op("dve", lambda E, lo=lo, s0=s0: E.tensor_copy(out=lo, in_=s0), reads=["s0"], writes=["sinlo"])
            SB_O = EXL + 4 * G * 256
            SC_O = SB_O + 1024
            TF_O = SC_O + 512
            U_O = TF_O + 2048
            assert U_O + 1024 <= RTMP + RTMP_SZ, "RTMP overflow"

            def head_gen(h, pi):
                s = pi
                qT, kT, v, sg = head_views(s)
                ld("sp", qT, ZQ[h], writes=[("hq", s)])
                ld("sp", kT, ZK[h], writes=[("hk", s)])
                ld("sp", v.rearrange("p (n e) -> p n e", n=NT), ZVv[:, :, h * 256:(h + 1) * 256], writes=[("hv", s)])
                ld("sp", sg.rearrange("p (n e) -> p n e", n=NT), ZG[:, :, h * 256:(h + 1) * 256], writes=[("hg", s)])
                if pi == 0:
                    acc, acck = psS[:, 0:256], ("psS", 0)
                else:
                    acc, acck = psG[3][:, 0:256], ("psG", 3)
                fw.pe_group([mm(acc, ident_b[:], av(SIN_HI + h * 256, 256), True, False),
                             mm(acc, ident_b[:], av(SIN_LO + h * 256, 256), False, False)],
                            reads=["sinhi", "sinlo", "identb"], writes=[acck])
                o_all = avf(O_OFF + pi * 2 * NT * 256, 2 * NT * 256)
                mv = small[:, pi * 32: pi * 32 + 2 * NT]
                yield
                for n in range(NT):
                    sbi = pi * 2 + n % 2
                    Sb = av(SB_O + sbi * 256, 256)
                    fw.op("act", actf(Sb, acc, AF.Copy), reads=[acck], writes=[("Sb", sbi)])
                    kt, ktk = kt_make(h, n, kT, s, pi)
                    fw.pe_group([mm(psX[:, 0:128], kT[:, n * 128:(n + 1) * 128], qT[:, n * 128:(n + 1) * 128], True, True)],
                                reads=[("hk", s), ("hq", s)], writes=["psX"])
                    sc = av(SC_O + sbi * 128, 128)
                    fw.op("dve", stt(sc, psX[:, 0:128], kdec_t[:, n * H + h:n * H + h + 1],
                                     mask_t[:, h * 128:(h + 1) * 128], ALU.mult, ALU.mult),
                          reads=["psX", "kdec", "mask"], writes=[("sc", sbi)])
                    yield
                    gi = nxt("g3", 3)
                    po = psG[gi][:, 0:256]
                    fw.pe_group([mm(po, sc, v[:, n * 256:(n + 1) * 256], True, False),
                                 mm(po, qT[:, n * 128:(n + 1) * 128], Sb, False, True)],
                                reads=[("sc", sbi), ("Sb", sbi), ("hv", s), ("hq", s)], writes=[("psG", gi)])
                    fw.pe_group([mm(acc, kt, v[:, n * 256:(n + 1) * 256], False, n == NT - 1)],
                                reads=[ktk, ("hv", s)], writes=[acck])
                    fw.op("act", actf(o_all[:, n * 256:(n + 1) * 256], po, AF.Copy, scale=odec_t[:, n * H + h:n * H + h + 1]),
                          reads=[("psG", gi), "odec"], writes=[("o", pi, n)])
                    st6 = small[:, 64 + sbi * 6: 64 + sbi * 6 + 6]
                    fw.op("dve", lambda E, n=n, st6=st6, o_all=o_all: E.bn_stats(out=st6, in_=o_all[:, n * 256:(n + 1) * 256]),
                          reads=[("o", pi, n)], writes=[("st6", sbi)])
                    fw.op("dve", lambda E, n=n, st6=st6, mv=mv: E.bn_aggr(out=mv[:, 2 * n:2 * n + 2], in_=st6),
                          reads=[("st6", sbi)], writes=[("rmv", pi)])
                    yield
                rs = small[:, 128 + pi * 32:128 + pi * 32 + NT]
                fw.op("act", actf(rs, mv.rearrange("p (t two) -> p t two", two=2)[:, :, 1], AF.Sqrt, bias=EPS),
                      reads=[("rmv", pi)], writes=[("rrs", pi)])
                fw.op("dve", lambda E, rs=rs: E.reciprocal(out=rs, in_=rs), reads=[("rrs", pi)], writes=[("rrs", pi)])
                gth = av(GTH + s * 2 * T, 2 * T)
                yield
                for n in range(NT):
                    ti2 = pi * 2 + n % 2
                    tf = avf(TF_O + ti2 * 512, 512)
                    fw.op("dve", tsc(tf, o_all[:, n * 256:(n + 1) * 256], mv[:, 2 * n:2 * n + 1], rs[:, n:n + 1],
                                     ALU.subtract, ALU.mult), reads=[("o", pi, n), ("rmv", pi), ("rrs", pi)], writes=[("tf", ti2)])
                    u = av(U_O + ti2 * 256, 256)
                    fw.op("dve", tt(u, tf, sg[:, n * 256:(n + 1) * 256], ALU.mult),
                          reads=[("tf", ti2), ("hg", s)], writes=[("u", ti2)])
                    ti = nxt("t", 2)
                    fw.pe_group([trp(psT[ti][:, 0:128], u[:, 0:128], ident_b[:]),
                                 trp(psT[ti][:, 128:256], u[:, 128:256], ident_b[:])],
                                reads=[("u", ti2), "identb"], writes=[("psT", ti)])
                    yield
                    for e2 in range(2):
                        fw.op("act", actf(gth[:, e2 * T + n * 128: e2 * T + (n + 1) * 128], psT[ti][:, e2 * 128:(e2 + 1) * 128],
                                          AF.Copy, scale=gnw_t[:, l * RC + h * 2 + e2: l * RC + h * 2 + e2 + 1]),
                              reads=[("psT", ti), "gnw"], writes=[("gth", s)])
                    yield
                ld("pool", GT[h * 2:h * 2 + 2].rearrange("k p t -> p k t"), gth.rearrange("p (k t) -> p k t", k=2),
                   reads=[("gth", s)], writes=["GT"], group="g_GT")

            for h0 in range(0, H, 2):
                gens = [head_gen(h0 + pi, pi) for pi in range(min(2, H - h0))]
                alive = list(gens)
                while alive:
                    for g_ in list(alive):
                        try:
                            next(g_)
                        except StopIteration:
                            alive.remove(g_)

        def load_bufA(src, nch):
            grp = newgroup()
            for k0 in range(0, nch, 4):
                k1 = min(nch, k0 + 4)
                ld("sp", arena[:, BUFA + k0 * T: BUFA + k1 * T].rearrange("p (k t) -> p k t", t=T),
                   src[k0:k1].rearrange("k p t -> p k t"), writes=["bufA"], group=grp)
            assert nch * T <= ARENA

        def phase3(l):
            load_bufA(GT, RC)

            def mk_units(gate, first):
                def units(blk, slot):
                    us = []
                    for m in range(WBC // 128):
                        ch = blk * (WBC // 128) + m
                        for tb in range(NB):
                            def epi(ps, gi, ch=ch, tb=tb):
                                li = nxt("lb", 2)
                                ld("sp", ld_b[li][:], gate[ch, :, tb * 512:(tb + 1) * 512], writes=[("ldb", li)])
                                if first:
                                    fw.op("dve", tt(mT(ch, tb * 512, 512), ps, ld_b[li][:], ALU.mult),
                                          reads=[("psG", gi), ("ldb", li)], writes=[("mT", ch, tb)])
                                else:
                                    fi = nxt("sf", 3)
                                    fw.op("dve", tt(stg_f[fi][:], ps, ld_b[li][:], ALU.mult),
                                          reads=[("psG", gi), ("ldb", li)], writes=[("stgf", fi)])
                                    fw.op("dve", tt(mT(ch, tb * 512, 512), mT(ch, tb * 512, 512), stg_f[fi][:], ALU.add),
                                          reads=[("stgf", fi), ("mT", ch, tb)], writes=[("mT", ch, tb)])
                                return None
                            us.append(dict(lhs=lambda k, slot=slot, m=m: wslice(slot, k, m * 128, 128),
                                           rhs=lambda k, tb=tb: hT(k, tb * 512, 512), n=512, epi=epi))
                    return us
                return units

            gemm(hT, RC, D // WBC, wload_cols(ret_proj[l], RC, lambda blk: blk * WBC), mk_units(GA, True), tag=("p3a", l))
            prefetch_w(wload_cols(sgu_proj[l], KC, lambda blk: blk * WBC), ("p3b", l))
            fw.barrier()
            load_bufA(ST, KC)
            gemm(hT, KC, D // WBC, wload_cols(sgu_proj[l], KC, lambda blk: blk * WBC), mk_units(GB, False), tag=("p3b", l))
            prefetch_w(wload_cols(w_out[l], KC, lambda blk: blk * WBC), ("p3c", l))
            fw.barrier()
            resid_gemm(mT, KC, w_out[l], tag=("p3c", l))
            prefetch_w(wl4(l), ("p4", l))

        def resid_gemm(Xfn, kc, Wl, tag=None):
            def units(blk, slot):
                us = []
                for m in range(WBC // 128):
                    ch = blk * (WBC // 128) + m
                    for tb in range(NB):
                        def epi(ps, gi, ch=ch, tb=tb):
                            li = nxt("lf", 2)
                            ld("sp", ld_f[li][:], XT[ch, :, tb * 512:(tb + 1) * 512], reads=[("XT", ch, tb)], writes=[("ldf", li)])
                            fi = nxt("sf", 3)
                            fw.op("dve", tt(stg_f[fi][:], ps, ld_f[li][:], ALU.add),
                                  reads=[("psG", gi), ("ldf", li)], writes=[("stgf", fi)])
                            ld("pool", XT[ch, :, tb * 512:(tb + 1) * 512], stg_f[fi][:], reads=[("stgf", fi)],
                               writes=[("XT", ch, tb)])
                            return None
                        us.append(dict(lhs=lambda k, slot=slot, m=m: wslice(slot, k, m * 128, 128),
                                       rhs=lambda k, tb=tb: Xfn(k, tb * 512, 512), n=512, epi=epi))
                return us
            gemm(Xfn, kc, D // WBC, wload_cols(Wl, kc, lambda blk: blk * WBC), units, tag=tag)

        def wl4(l):
            Wl = w_ffn_in[l]
            HB = WBC // 2

            return wload_cols(Wl, KC, lambda b: b * WBC)

        def phase4(l):
            HB = WBC // 2
            nblk = DFF // HB
            wl = wl4(l)

            def units(blk, slot):
                us = []
                for m in range(HB // 128):
                    ch = blk * (HB // 128) + m
                    for tb in range(NB):
                        hold = {}

                        def epi_a(ps, gi, hold=hold):
                            fi = nxt("sf", 3)
                            fw.op("act", actf(stg_f[fi][:], ps, AF.Silu), reads=[("psG", gi)], writes=[("stgf", fi)])
                            hold["fi"] = fi
                            return None

                        def epi_c(ps, gi, hold=hold, ch=ch, tb=tb):
                            fi = hold["fi"]
                            si = nxt("sb", NSB)
                            fw.op("dve", tt(stg_b[si][:], ps, stg_f[fi][:], ALU.mult),
                                  reads=[("psG", gi), ("stgf", fi)], writes=[("stgb", si)])
                            ld("pool", AT[ch, :, tb * 512:(tb + 1) * 512], stg_b[si][:], reads=[("stgb", si)], writes=["AT"], group="g_AT")
                            return None
                        us.append(dict(lhs=lambda k, slot=slot, m=m: wslice(slot, k, m * 128, 128),
                                       rhs=lambda k, tb=tb: hT(k, tb * 512, 512), n=512, epi=epi_a))
                        us.append(dict(lhs=lambda k, slot=slot, m=m: wslice(slot, k, HB + m * 128, 128),
                                       rhs=lambda k, tb=tb: hT(k, tb * 512, 512), n=512, epi=epi_c))
                return us
            gemm(hT, KC, nblk, wl, units, tag=("p4", l))
            prefetch_w(wload_cols(w_ffn_out[l][0:KH * 128, :], KH, lambda blk: blk * WBC), ("p5", l, 0))

        def phase5(l):
            Wl = w_ffn_out[l]
            for half in range(2):
                load_bufA(AT[half * KH:(half + 1) * KH], KH)
                resid_gemm(hT, KH, Wl[half * KH * 128:(half + 1) * KH * 128, :], tag=("p5", l, half))
                if half == 0:
                    prefetch_w(wload_cols(Wl[KH * 128:2 * KH * 128, :], KH, lambda blk: blk * WBC), ("p5", l, 1))

        import os as _os
        STOP = int(_os.environ.get("K_STOP", "99"))

        def program():
            ingest()
            fw.barrier()
            if STOP <= 1:
                return
            for l in range(NL):
                load_layer_small(l)
                norm_pass(lambda k, l=l: nw1_t[:, l * KC + k: l * KC + k + 1])
                fw.barrier()
                if STOP <= 2:
                    return
                phase1(l)
                fw.barrier()
                if STOP <= 3:
                    return
                exchange()
                if STOP <= 4:
                    return
                sgu(l)
                fw.barrier()
                if STOP <= 5:
                    return
                prefetch_w(wload_cols(ret_proj[l], RC, lambda blk: blk * WBC), ("p3a", l))
                pass_b(l)
                fw.barrier()
                if STOP <= 6:
                    return
                phase3(l)
                fw.barrier()
                if STOP <= 7:
                    return
                norm_pass(lambda k, l=l: nw2_t[:, l * KC + k: l * KC + k + 1])
                fw.barrier()
                if STOP <= 8:
                    return
                phase4(l)
                fw.barrier()
                if STOP <= 9:
                    return
                phase5(l)
                fw.barrier()
                if STOP <= 10:
                    return
            norm_pass(lambda k: nwf_t[:, k:k + 1], final=True)

        program()
        fw.finish()
        fw.run(block)
    return nc


def _run(cfg, x, norm_mix_w, w_in, ret_gn_w, ret_proj, sgu_ln_w, sgu_ln_b, sgu_w_s, sgu_b_s, sgu_proj, w_out,
         norm_ffn_w, w_ffn_in, w_ffn_out, final_norm_w, trace=False):
    c = cfg
    f = lambda a: np.ascontiguousarray(np.asarray(a, dtype=np.float32))
    x = f(x).reshape(c.BATCH * c.SEQ, c.D)
    shared = {
        "w_in": f(w_in), "ret_proj": f(ret_proj), "sgu_proj": f(sgu_proj), "w_out": f(w_out),
        "w_ffn_in": f(f(w_ffn_in).reshape(c.NL, c.D, 2, c.FC, 128).transpose(0, 1, 3, 2, 4).reshape(c.NL, c.D, 2 * c.DFF)),
        "w_ffn_out": f(w_ffn_out),
        "norm_mix_w": f(f(norm_mix_w).reshape(c.NL, c.KC, 128).transpose(0, 2, 1)),
        "norm_ffn_w": f(f(norm_ffn_w).reshape(c.NL, c.KC, 128).transpose(0, 2, 1)),
        "final_norm_w": f(f(final_norm_w).reshape(c.KC, 128).transpose(1, 0)),
        "ret_gn_w": f(f(ret_gn_w).reshape(c.NL, c.RC, 128).transpose(0, 2, 1)),
        "sgu_ln_w": f(sgu_ln_w), "sgu_ln_b": f(sgu_ln_b), "sgu_w_s": f(sgu_w_s),
        "sgu_b_s": f(f(sgu_b_s).reshape(c.NL, c.SG * 128)),
    }
    nc = _build(c)
    in_maps = []
    for core in range(c.NCORES):
        m = dict(shared)
        m["x"] = x[core * c.T:(core + 1) * c.T]
        m.update(_tables(c, core))
        in_maps.append(m)
    res = run_bass_kernel_spmd(nc, in_maps, core_ids=list(range(c.NCORES)), **({"trace": True} if trace else {}))
    y = np.concatenate([res.results[i]["y"] for i in range(c.NCORES)], 0)
    return y.reshape(c.BATCH, c.SEQ, c.D).astype(np.float32), res


def kernel(**inputs):
    cfg = Cfg()
    y, _ = _run(cfg, **inputs)
    return y
```
